# Optimizing a Trainium2 kernel written in Bass

```python
import math
import jax
import jax.numpy as jnp
from jax import lax
import numpy as np

D_MODEL = 1024
BATCH = 16
SEQ = 256
DEPTH = 2
DEC_BATCH = 2
DEC_SEQ = 1024
PAST_LEN = 256

GRID_W = 64
N_EVEN = (DEPTH + 1) // 2
N_ODD = DEPTH // 2

ATTN_HEADS = 8
ATTN_KV_HEADS = 2
HEAD_DIM = 64
ROPE_THETA = 10000.0
Q_BLOCK = 128
ATTN_Q_DIM = ATTN_HEADS * HEAD_DIM
ATTN_KV_DIM = ATTN_KV_HEADS * HEAD_DIM

SSD_HEADS = 8
SSD_HEAD_DIM = 64
SSD_D_INNER = SSD_HEADS * SSD_HEAD_DIM
SSD_GROUPS = 2
SSD_STATE = 64
SSD_CONV_K = 5
SSD_CHUNK = 128
SSD_CONV_DIM = SSD_D_INNER + 2 * SSD_GROUPS * SSD_STATE

AB_IN_DIM = ATTN_Q_DIM + 2 * ATTN_KV_DIM + SSD_D_INNER + SSD_CONV_DIM + 2 * SSD_HEADS
AB_SPLITS = (ATTN_Q_DIM, ATTN_Q_DIM + ATTN_KV_DIM, ATTN_Q_DIM + 2 * ATTN_KV_DIM,
             ATTN_Q_DIM + 2 * ATTN_KV_DIM + SSD_D_INNER,
             ATTN_Q_DIM + 2 * ATTN_KV_DIM + SSD_D_INNER + SSD_CONV_DIM)
AB_MIX_DIM = ATTN_Q_DIM + SSD_D_INNER

RWKV_HEAD_DIM = 64
RWKV_HEADS = D_MODEL // RWKV_HEAD_DIM
R_DECAY = 64
R_AAA = 64
R_GATE = 128

FFN_DIM = (((8 * D_MODEL + 2) // 3 + 255) // 256) * 256

RMS_EPS = 1e-6
GN_EPS = 64e-5
L2_EPS = 1e-12
F32 = jnp.float32

kernel_name = 'hybrid_diffusion_gqa_ssd_rwkv7_step'


def rmsnorm(x, g, eps=RMS_EPS):
    xf = x.astype(F32)
    y = xf * lax.rsqrt(jnp.mean(xf * xf, axis=-1, keepdims=True) + eps)
    return (y * g.astype(F32)).astype(x.dtype)


def ada_params(cvec, w, b):
    m = jnp.einsum('...d,de->...e', jax.nn.silu(cvec), w) + b
    return jnp.split(m[..., None, :], 6, axis=-1)


def modulate(h, shift, scale):
    return h * (1 + scale) + shift


def swiglu(h, w_gate, w_up, w_down):
    hid = jax.nn.silu(jnp.einsum('btd,df->btf', h, w_gate)) * jnp.einsum('btd,df->btf', h, w_up)
    return jnp.einsum('btf,fd->btd', hid, w_down)


def grid_rope(n_tokens):
    rows = n_tokens // GRID_W
    row = jnp.repeat(jnp.arange(rows, dtype=F32), GRID_W)
    col = jnp.tile(jnp.arange(GRID_W, dtype=F32), rows)
    n_freq = HEAD_DIM // 4
    inv_freq = ROPE_THETA ** (-jnp.arange(n_freq, dtype=F32) / n_freq)
    ang = jnp.concatenate([row[:, None] * inv_freq, col[:, None] * inv_freq], axis=-1)
    return jnp.cos(ang), jnp.sin(ang)


def apply_rope(x, cos, sin):
    xf = x.astype(F32)
    x1, x2 = jnp.split(xf, 2, axis=-1)
    c = cos[None, :, None, :]
    s = sin[None, :, None, :]
    return jnp.concatenate([x1 * c - x2 * s, x1 * s + x2 * c], axis=-1).astype(x.dtype)


def block_attention(q, k, v):
    b, n, nh, d = q.shape
    kvh = k.shape[2]
    grp = nh // kvh
    nb = n // Q_BLOCK
    qb = jnp.moveaxis(q.reshape(b, nb, Q_BLOCK, kvh, grp, d), 1, 0)
    scale = d ** -0.5

    def one_block(qblk):
        s = jnp.einsum('bqkgd,bskd->bkgqs', qblk, k).astype(F32) * scale
        p = jax.nn.softmax(s, axis=-1).astype(v.dtype)
        return jnp.einsum('bkgqs,bskd->bqkgd', p, v)

    o = lax.map(one_block, qb)
    return jnp.moveaxis(o, 0, 1).reshape(b, n, nh * d)


def centred_dwconv(x, w, b):
    pad = SSD_CONV_K // 2
    n = x.shape[1]
    xp = jnp.pad(x, ((0, 0), (pad, pad), (0, 0)))
    y = b
    for i in range(SSD_CONV_K):
        y = y + xp[:, i:i + n] * w[i]
    return y


def ssd_chunked(x, dt, A, bm, cm, h0):
    b, n, nh, hp = x.shape
    ng, ns = bm.shape[2], bm.shape[3]
    L = SSD_CHUNK
    nc = n // L
    bh = jnp.repeat(bm.astype(F32), nh // ng, axis=2).reshape(b, nc, L, nh, ns)
    ch = jnp.repeat(cm.astype(F32), nh // ng, axis=2).reshape(b, nc, L, nh, ns)
    xdt = (x.astype(F32) * dt[..., None]).reshape(b, nc, L, nh, hp)
    a_cs = jnp.cumsum((dt * A).reshape(b, nc, L, nh), axis=2)
    lower = jnp.tril(jnp.ones((L, L), dtype=bool))[None, None, :, :, None]
    seg = jnp.exp(jnp.where(lower, a_cs[:, :, :, None, :] - a_cs[:, :, None, :, :], -jnp.inf))
    scores = jnp.einsum('bclhn,bcshn->bclsh', ch, bh) * seg
    y_diag = jnp.einsum('bclsh,bcshp->bclhp', scores, xdt)
    decay_end = jnp.exp(a_cs[:, :, -1:, :] - a_cs)
    chunk_states = jnp.einsum('bclhn,bclhp->bchpn', bh * decay_end[..., None], xdt)
    chunk_decay = jnp.exp(a_cs[:, :, -1, :])

    def carry_step(h, inp):
        st, dec = inp
        return h * dec[:, :, None, None] + st, h

    h_final, h_in = lax.scan(carry_step, h0.astype(F32),
                             (jnp.moveaxis(chunk_states, 1, 0), jnp.moveaxis(chunk_decay, 1, 0)))
    h_in = jnp.moveaxis(h_in, 0, 1)
    y_off = jnp.einsum('bclhn,bchpn->bclhp', ch * jnp.exp(a_cs)[..., None], h_in)
    return (y_diag + y_off).reshape(b, n, nh, hp), h_final


def mixer_ab(h, w_in, w_out, q_g, k_g, conv_w, conv_b, dt_bias, a_log, d_skip, ssd_g,
             rope, ctx_k, ctx_v, h0_f, h0_b):
    b, n, _ = h.shape
    proj = jnp.einsum('btd,de->bte', h, w_in)
    q, k, v, z, xbc, dt = jnp.split(proj, AB_SPLITS, axis=-1)
    q = rmsnorm(q.reshape(b, n, ATTN_HEADS, HEAD_DIM), q_g)
    k = rmsnorm(k.reshape(b, n, ATTN_KV_HEADS, HEAD_DIM), k_g)
    v = v.reshape(b, n, ATTN_KV_HEADS, HEAD_DIM)
    if rope is not None:
        q = apply_rope(q, rope[0], rope[1])
        k = apply_rope(k, rope[0], rope[1])
    if ctx_k is None:
        keys, vals = k, v
    else:
        keys = jnp.concatenate([ctx_k.astype(k.dtype), k], axis=1)
        vals = jnp.concatenate([ctx_v.astype(v.dtype), v], axis=1)
    attn = block_attention(q, keys, vals)
    xbc = jax.nn.silu(centred_dwconv(xbc, conv_w, conv_b))
    xs, bm, cm = jnp.split(xbc, [SSD_D_INNER, SSD_D_INNER + SSD_GROUPS * SSD_STATE], axis=-1)
    xs = xs.reshape(b, n, SSD_HEADS, SSD_HEAD_DIM)
    bm = bm.reshape(b, n, SSD_GROUPS, SSD_STATE)
    cm = cm.reshape(b, n, SSD_GROUPS, SSD_STATE)
    dt = jax.nn.softplus(dt.astype(F32).reshape(b, n, 2, SSD_HEADS) + dt_bias.astype(F32))
    A = -jnp.exp(a_log.astype(F32))
    y_f, hf = ssd_chunked(xs, dt[:, :, 0], A[0], bm, cm, h0_f)
    y_b, hb = ssd_chunked(xs[:, ::-1], dt[:, ::-1, 1], A[1], bm[:, ::-1], cm[:, ::-1], h0_b)
    y = y_f + y_b[:, ::-1] + d_skip.astype(F32)[:, None] * xs.astype(F32)
    y = rmsnorm(y.reshape(b, n, SSD_D_INNER) * jax.nn.silu(z.astype(F32)), ssd_g).astype(h.dtype)
    out = jnp.einsum('bte,ed->btd', jnp.concatenate([attn.astype(h.dtype), y], axis=-1), w_out)
    return out, k, v, hf, hb


def centred_shift(x):
    prev = jnp.pad(x, ((0, 0), (1, 0), (0, 0)))[:, :-1]
    nxt = jnp.pad(x, ((0, 0), (0, 1), (0, 0)))[:, 1:]
    return prev - x, nxt - x


def split_heads(t):
    return t.reshape(t.shape[:-1] + (RWKV_HEADS, RWKV_HEAD_DIM))


def rwkv7_scan(r, w, k, v, kk, a, s0):
    def step(S, inp):
        r_t, w_t, k_t, v_t, kk_t, a_t = inp
        sa = jnp.einsum('bhvk,bhk->bhv', S, kk_t)
        S = (S * w_t[:, :, None, :] - sa[..., None] * (kk_t * a_t)[:, :, None, :]
             + v_t[..., None] * k_t[:, :, None, :])
        return S, jnp.einsum('bhvk,bhk->bhv', S, r_t)

    xs = tuple(jnp.moveaxis(t, 1, 0) for t in (r, w, k, v, kk, a))
    s_final, y = lax.scan(step, s0.astype(F32), xs)
    return jnp.moveaxis(y, 0, 1), s_final


def mixer_rwkv(h, mu, w_r, w_k, w_v, w0, w1, w2, a0, a1, a2, g1, g2, k_k, k_a, r_k, ln_g, ln_b, w_o,
               s0_f, s0_b):
    b, n, _ = h.shape
    dp, dn = centred_shift(h)
    xmix = h[:, :, None, :] + dp[:, :, None, :] * mu[0] + dn[:, :, None, :] * mu[1]
    xr, xw, xk, xv, xa, xg = (xmix[:, :, i] for i in range(6))
    r = jnp.einsum('btd,de->bte', xr, w_r).astype(F32)
    k = jnp.einsum('btd,de->bte', xk, w_k).astype(F32)
    v = jnp.einsum('btd,de->bte', xv, w_v).astype(F32)
    wl = w0[:, None, None, :] + jnp.einsum('jbtr,jrd->jbtd', jnp.tanh(jnp.einsum('btd,jdr->jbtr', xw, w1)), w2)
    decay = jnp.exp(-jnp.exp(-jax.nn.softplus(-wl.astype(F32)) - 0.5))
    a = jax.nn.sigmoid((a0[:, None, None, :]
                        + jnp.einsum('jbtr,jrd->jbtd', jnp.einsum('btd,jdr->jbtr', xa, a1), a2)).astype(F32))
    g = jnp.einsum('btr,rd->btd', jax.nn.sigmoid(jnp.einsum('btd,dr->btr', xg, g1)), g2).astype(F32)
    kk = split_heads(k * k_k.astype(F32))
    kk = kk * lax.rsqrt(jnp.sum(kk * kk, axis=-1, keepdims=True) + L2_EPS)
    k_dir = split_heads(k[None] * (1 + (a - 1) * k_a.astype(F32)))
    rh, vh, a_h, w_h = split_heads(r), split_heads(v), split_heads(a), split_heads(decay)
    y_f, s_f = rwkv7_scan(rh, w_h[0], k_dir[0], vh, kk, a_h[0], s0_f)
    y_b, s_b = rwkv7_scan(rh[:, ::-1], w_h[1][:, ::-1], k_dir[1][:, ::-1], vh[:, ::-1], kk[:, ::-1],
                          a_h[1][:, ::-1], s0_b)
    y = y_f + y_b[:, ::-1]
    mean = jnp.mean(y, axis=-1, keepdims=True)
    var = jnp.mean(jnp.square(y - mean), axis=-1, keepdims=True)
    y = ((y - mean) * lax.rsqrt(var + GN_EPS)).reshape(b, n, D_MODEL) * ln_g.astype(F32) + ln_b.astype(F32)
    bonus = jnp.sum(rh[None] * k_dir * r_k.astype(F32), axis=(0, -1))[..., None] * vh
    out = ((y + bonus.reshape(b, n, D_MODEL)) * g).astype(h.dtype)
    return jnp.einsum('btd,de->bte', out, w_o), s_f, s_b


def trunk(x, cvec, rope, ctx, P):
    collected = {'k': [], 'v': [], 'ssd_f': [], 'ssd_b': [], 'rwkv_f': [], 'rwkv_b': []}
    b = x.shape[0]
    for layer in range(DEPTH):
        j = layer // 2
        sh1, sc1, gt1, sh2, sc2, gt2 = ada_params(cvec, P['mod_w'][layer], P['mod_b'][layer])
        h = modulate(rmsnorm(x, P['norm_mix_g'][layer]), sh1, sc1)
        if layer % 2 == 0:
            if ctx is None:
                ck = cv = None
                h0f = h0b = jnp.zeros((b, SSD_HEADS, SSD_HEAD_DIM, SSD_STATE), F32)
            else:
                ck, cv = ctx['k'][:, j], ctx['v'][:, j]
                h0f, h0b = ctx['ssd_f'][:, j], ctx['ssd_b'][:, j]
            out, k, v, hf, hb = mixer_ab(h, P['ab_w_in'][j], P['ab_w_out'][j], P['attn_q_g'][j], P['attn_k_g'][j],
                                         P['ssd_conv_w'][j], P['ssd_conv_b'][j], P['ssd_dt_bias'][j],
                                         P['ssd_a_log'][j], P['ssd_d'][j], P['ssd_norm_g'][j],
                                         rope, ck, cv, h0f, h0b)
            produced = {'k': k, 'v': v, 'ssd_f': hf, 'ssd_b': hb}
        else:
            if ctx is None:
                s0f = s0b = jnp.zeros((b, RWKV_HEADS, RWKV_HEAD_DIM, RWKV_HEAD_DIM), F32)
            else:
                s0f, s0b = ctx['rwkv_f'][:, j], ctx['rwkv_b'][:, j]
            out, sf, sb = mixer_rwkv(h, P['rwkv_mu'][j], P['rwkv_w_r'][j], P['rwkv_w_k'][j], P['rwkv_w_v'][j],
                                     P['rwkv_w0'][j], P['rwkv_w1'][j], P['rwkv_w2'][j],
                                     P['rwkv_a0'][j], P['rwkv_a1'][j], P['rwkv_a2'][j],
                                     P['rwkv_g1'][j], P['rwkv_g2'][j], P['rwkv_k_k'][j], P['rwkv_k_a'][j],
                                     P['rwkv_r_k'][j], P['rwkv_ln_g'][j], P['rwkv_ln_b'][j], P['rwkv_w_o'][j],
                                     s0f, s0b)
            produced = {'rwkv_f': sf, 'rwkv_b': sb}
        if ctx is None:
            for name, t in produced.items():
                collected[name].append(t.astype(x.dtype))
        x = x + gt1 * out
        h = modulate(rmsnorm(x, P['norm_ffn_g'][layer]), sh2, sc2)
        x = x + gt2 * swiglu(h, P['ffn_w_gate'][layer], P['ffn_w_up'][layer], P['ffn_w_down'][layer])
    return rmsnorm(x, P['final_norm_g']), collected


def setup_inputs(seed: int = 0) -> dict:
    key = jax.random.key(seed)
    keys = iter(jax.random.split(key, 64))

    def nrm(shape, scale):
        return jax.random.normal(next(keys), shape, F32) * scale

    def unif(shape, lo, hi):
        return jax.random.uniform(next(keys), shape, F32, lo, hi)

    D = D_MODEL
    dt0 = jnp.exp(unif((N_EVEN, 2, SSD_HEADS), math.log(1e-3), math.log(1e-1)))
    return {
        'x_prompt': nrm((BATCH, SEQ, D), 1.0),
        'x_sample': nrm((DEC_BATCH, DEC_SEQ, D), 1.0),
        'cache_attn_k': nrm((DEC_BATCH, N_EVEN, PAST_LEN, ATTN_KV_HEADS, HEAD_DIM), 1.0),
        'cache_attn_v': nrm((DEC_BATCH, N_EVEN, PAST_LEN, ATTN_KV_HEADS, HEAD_DIM), 1.0),
        'state_ssd_fwd': nrm((DEC_BATCH, N_EVEN, SSD_HEADS, SSD_HEAD_DIM, SSD_STATE), 0.1),
        'state_ssd_bwd': nrm((DEC_BATCH, N_EVEN, SSD_HEADS, SSD_HEAD_DIM, SSD_STATE), 0.1),
        'state_rwkv_fwd': nrm((DEC_BATCH, N_ODD, RWKV_HEADS, RWKV_HEAD_DIM, RWKV_HEAD_DIM), 0.3),
        'state_rwkv_bwd': nrm((DEC_BATCH, N_ODD, RWKV_HEADS, RWKV_HEAD_DIM, RWKV_HEAD_DIM), 0.3),
        'c': nrm((DEC_BATCH, D), 1.0),
        'c_ctx': nrm((D,), 1.0),
        'mod_w': nrm((DEPTH, D, 6 * D), 0.3 * D ** -0.5),
        'mod_b': nrm((DEPTH, 6 * D), 0.02),
        'norm_mix_g': 1.0 + nrm((DEPTH, D), 0.05),
        'norm_ffn_g': 1.0 + nrm((DEPTH, D), 0.05),
        'ffn_w_gate': nrm((DEPTH, D, FFN_DIM), D ** -0.5),
        'ffn_w_up': nrm((DEPTH, D, FFN_DIM), D ** -0.5),
        'ffn_w_down': nrm((DEPTH, FFN_DIM, D), FFN_DIM ** -0.5),
        'ab_w_in': nrm((N_EVEN, D, AB_IN_DIM), D ** -0.5),
        'ab_w_out': nrm((N_EVEN, AB_MIX_DIM, D), AB_MIX_DIM ** -0.5),
        'attn_q_g': 1.0 + nrm((N_EVEN, HEAD_DIM), 0.05),
        'attn_k_g': 1.0 + nrm((N_EVEN, HEAD_DIM), 0.05),
        'ssd_conv_w': nrm((N_EVEN, SSD_CONV_K, SSD_CONV_DIM), SSD_CONV_K ** -0.5),
        'ssd_conv_b': nrm((N_EVEN, SSD_CONV_DIM), 0.02),
        'ssd_dt_bias': dt0 + jnp.log(-jnp.expm1(-dt0)),
        'ssd_a_log': jnp.log(unif((N_EVEN, 2, SSD_HEADS), 1.0, 16.0)),
        'ssd_d': 1.0 + nrm((N_EVEN, SSD_HEADS), 0.1),
        'ssd_norm_g': 1.0 + nrm((N_EVEN, SSD_D_INNER), 0.05),
        'rwkv_mu': unif((N_ODD, 2, 6, D), 0.0, 0.5),
        'rwkv_w_r': nrm((N_ODD, D, D), D ** -0.5),
        'rwkv_w_k': nrm((N_ODD, D, D), D ** -0.5),
        'rwkv_w_v': nrm((N_ODD, D, D), D ** -0.5),
        'rwkv_w0': unif((N_ODD, 2, D), -6.0, 1.0),
        'rwkv_w1': nrm((N_ODD, 2, D, R_DECAY), D ** -0.5),
        'rwkv_w2': nrm((N_ODD, 2, R_DECAY, D), 0.3 * R_DECAY ** -0.5),
        'rwkv_a0': nrm((N_ODD, 2, D), 0.3),
        'rwkv_a1': nrm((N_ODD, 2, D, R_AAA), D ** -0.5),
        'rwkv_a2': nrm((N_ODD, 2, R_AAA, D), 0.3 * R_AAA ** -0.5),
        'rwkv_g1': nrm((N_ODD, D, R_GATE), D ** -0.5),
        'rwkv_g2': nrm((N_ODD, R_GATE, D), R_GATE ** -0.5),
        'rwkv_k_k': 0.85 + nrm((N_ODD, D), 0.05),
        'rwkv_k_a': 1.0 + nrm((N_ODD, D), 0.05),
        'rwkv_r_k': nrm((N_ODD, RWKV_HEADS, RWKV_HEAD_DIM), 0.1),
        'rwkv_ln_g': 1.0 + nrm((N_ODD, D), 0.05),
        'rwkv_ln_b': nrm((N_ODD, D), 0.02),
        'rwkv_w_o': nrm((N_ODD, D, D), D ** -0.5),
        'final_norm_g': 1.0 + nrm((D,), 0.05),
    }


def reference(x_prompt, x_sample, cache_attn_k, cache_attn_v, state_ssd_fwd, state_ssd_bwd,
              state_rwkv_fwd, state_rwkv_bwd, c, c_ctx, mod_w, mod_b, norm_mix_g, norm_ffn_g,
              ffn_w_gate, ffn_w_up, ffn_w_down, ab_w_in, ab_w_out, attn_q_g, attn_k_g,
              ssd_conv_w, ssd_conv_b, ssd_dt_bias, ssd_a_log, ssd_d, ssd_norm_g,
              rwkv_mu, rwkv_w_r, rwkv_w_k, rwkv_w_v, rwkv_w0, rwkv_w1, rwkv_w2,
              rwkv_a0, rwkv_a1, rwkv_a2, rwkv_g1, rwkv_g2, rwkv_k_k, rwkv_k_a, rwkv_r_k,
              rwkv_ln_g, rwkv_ln_b, rwkv_w_o, final_norm_g):
    P = dict(mod_w=mod_w, mod_b=mod_b, norm_mix_g=norm_mix_g, norm_ffn_g=norm_ffn_g,
             ffn_w_gate=ffn_w_gate, ffn_w_up=ffn_w_up, ffn_w_down=ffn_w_down,
             ab_w_in=ab_w_in, ab_w_out=ab_w_out, attn_q_g=attn_q_g, attn_k_g=attn_k_g,
             ssd_conv_w=ssd_conv_w, ssd_conv_b=ssd_conv_b, ssd_dt_bias=ssd_dt_bias,
             ssd_a_log=ssd_a_log, ssd_d=ssd_d, ssd_norm_g=ssd_norm_g,
             rwkv_mu=rwkv_mu, rwkv_w_r=rwkv_w_r, rwkv_w_k=rwkv_w_k, rwkv_w_v=rwkv_w_v,
             rwkv_w0=rwkv_w0, rwkv_w1=rwkv_w1, rwkv_w2=rwkv_w2,
             rwkv_a0=rwkv_a0, rwkv_a1=rwkv_a1, rwkv_a2=rwkv_a2, rwkv_g1=rwkv_g1, rwkv_g2=rwkv_g2,
             rwkv_k_k=rwkv_k_k, rwkv_k_a=rwkv_k_a, rwkv_r_k=rwkv_r_k,
             rwkv_ln_g=rwkv_ln_g, rwkv_ln_b=rwkv_ln_b, rwkv_w_o=rwkv_w_o, final_norm_g=final_norm_g)
    y_prompt, st = trunk(x_prompt, c_ctx, None, None, P)
    ctx = dict(k=cache_attn_k, v=cache_attn_v, ssd_f=state_ssd_fwd, ssd_b=state_ssd_bwd,
               rwkv_f=state_rwkv_fwd, rwkv_b=state_rwkv_bwd)
    y_sample, _ = trunk(x_sample, c, grid_rope(x_sample.shape[1]), ctx, P)
    new_attn_k = jnp.stack(st['k'], axis=1)
    new_attn_v = jnp.stack(st['v'], axis=1)
    new_ssd_fwd = jnp.stack(st['ssd_f'], axis=1)
    new_ssd_bwd = jnp.stack(st['ssd_b'], axis=1)
    new_rwkv_fwd = jnp.stack(st['rwkv_f'], axis=1)
    new_rwkv_bwd = jnp.stack(st['rwkv_b'], axis=1)
    return (y_prompt, y_sample, new_attn_k, new_attn_v, new_ssd_fwd, new_ssd_bwd, new_rwkv_fwd, new_rwkv_bwd)
```

```python
import numpy as np
from contextlib import ExitStack
import concourse.bass as bass
import concourse.mybir as mybir
from concourse.bass_utils import run_bass_kernel_spmd
import numpy as np
from contextlib import contextmanager, ExitStack
import concourse.bass as bass
import concourse.mybir as mybir

F32 = mybir.dt.float32
BF16 = mybir.dt.bfloat16
I32 = mybir.dt.int32
AF = mybir.ActivationFunctionType
ALU = mybir.AluOpType
AX = mybir.AxisListType

ENGS = ["pe", "act", "dve", "pool", "sp"]


class Prog:
    def __init__(self, nc, es, n_dsem=14):
        self.nc = nc
        self.es = es
        self.ops = {e: [] for e in ENGS}
        self.cnt = {e: 0 for e in ENGS}
        self.sem = {e: es.enter_context(nc.semaphore("c_" + e)) for e in ENGS if e != "sp"}
        self.dsem = [es.enter_context(nc.semaphore("d%d" % i)) for i in range(n_dsem)]
        self.dval = [0] * n_dsem
        self.dnext = 0
        self.seen = {e: {} for e in ENGS}
        self.lastw = {}
        self.readers = {}
        self.psum_names = set()
        self.n_inst = 0
        self.pending = {e: [] for e in ENGS}
        self.final_done = False

    def sb(self, name, shape, dtype=F32):
        self.n_sb = getattr(self, "n_sb", 0) + 1
        name = "%s_%d" % (name, self.n_sb)
        return self.es.enter_context(self.nc.sbuf_tensor(name, list(shape), dtype))

    def ps(self, name, shape=(128, 512), dtype=F32):
        t = self.es.enter_context(self.nc.psum_tensor(name, list(shape), dtype))
        self.psum_names.add(name)
        return t

    @staticmethod
    def _key(x):
        if x is None:
            return None
        if isinstance(x, tuple):
            return x[1]
        if isinstance(x, str):
            return x
        t = getattr(x, "tensor", x)
        return getattr(t, "name", None)

    @staticmethod
    def _ap(x):
        return x[0] if isinstance(x, tuple) else x

    def _deps(self, eng, reads, writes):
        deps = {}

        def add(tok):
            if tok is None:
                return
            src, val = tok
            if src == "pe" and eng == "pe":
                return
            if deps.get(src, 0) < val:
                deps[src] = val

        rk = [k for k in (self._key(r) for r in reads) if k is not None]
        wk = [k for k in (self._key(w) for w in writes) if k is not None]
        wk = [(k[0] if isinstance(k, tuple) and k[0] in self.psum_names else k) for k in wk]
        rk2 = []
        for k in rk:
            base = k[0] if isinstance(k, tuple) else k
            if base in self.psum_names:
                wk.append(base)
            else:
                rk2.append(k)
        rk = rk2
        for k in rk:
            add(self.lastw.get(k))
        for k in wk:
            add(self.lastw.get(k))
            for tok in self.readers.get(k, {}).values():
                add(tok)
        waits = list(self.pending[eng])
        self.pending[eng] = []
        for src, val in deps.items():
            if self.seen[eng].get(src, 0) >= val:
                continue
            self.seen[eng][src] = val
            waits.append((src, val))
        return waits, rk, wk

    def _commit(self, eng, tok, rk, wk):
        for k in wk:
            self.lastw[k] = tok
            self.readers[k] = {}
        for k in rk:
            self.readers.setdefault(k, {})[eng] = tok

    def op(self, eng, fn, reads=(), writes=()):
        waits, rk, wk = self._deps(eng, reads, writes)
        self.cnt[eng] += 1
        tok = (eng, self.cnt[eng])
        self.ops[eng].append((waits, fn, ("e", eng)))
        self._commit(eng, tok, rk, wk)
        self.n_inst += 1

    def dma(self, out, in_, reads=None, writes=None, q="sp"):
        reads = [in_] if reads is None else reads
        writes = [out] if writes is None else writes
        waits, rk, wk = self._deps(q, reads, writes)
        nsw = 4
        nhw = len(self.dsem) - nsw
        if q == "pool":
            self.dnext_sw = (getattr(self, "dnext_sw", -1) + 1) % nsw
            j = nhw + self.dnext_sw
        else:
            j = self.dnext
            self.dnext = (self.dnext + 1) % nhw
        src = ("d", j)
        if self.dval[j] > 0 and self.seen[q].get(src, 0) < self.dval[j]:
            self.seen[q][src] = self.dval[j]
            waits.append((src, self.dval[j]))
        self.dval[j] += 16
        tok = (src, self.dval[j])
        o, i = self._ap(out), self._ap(in_)
        self.ops[q].append((waits, lambda e: e.dma_start(out=o, in_=i), ("d", j)))
        self._commit(q, tok, rk, wk)
        self.n_inst += 1

    def mm(self, out, lhsT, rhs, start=True, stop=True, **kw):
        o, l, r = self._ap(out), self._ap(lhsT), self._ap(rhs)
        self.op("pe", lambda e: e.matmul(o, l, r, start=start, stop=stop, **kw),
                reads=[lhsT, rhs], writes=[out])

    def tr(self, out, in_, ident):
        o, i, d = self._ap(out), self._ap(in_), self._ap(ident)
        self.op("pe", lambda e: e.transpose(o, i, d), reads=[in_, ident], writes=[out])

    def act(self, out, in_, func, bias=None, scale=None, accum_out=None, extra_r=()):
        o, i = self._ap(out), self._ap(in_)
        kw = {}
        rd = [in_] + list(extra_r)
        if bias is not None:
            kw["bias"] = self._ap(bias)
            if not isinstance(bias, (int, float)):
                rd.append(bias)
        if scale is not None:
            kw["scale"] = self._ap(scale)
            if not isinstance(scale, (int, float)):
                rd.append(scale)
        wr = [out]
        if accum_out is not None:
            kw["accum_out"] = self._ap(accum_out)
            wr.append(accum_out)
        self.op("act", lambda e: e.activation(o, i, func, **kw), reads=rd, writes=wr)

    def gen(self, eng, name, outs, ins, *args, **kw):
        oa = [self._ap(x) for x in outs]
        ia = [self._ap(x) if not isinstance(x, (int, float)) else x for x in ins]
        rd = [x for x in ins if not isinstance(x, (int, float))]
        self.op(eng, lambda e: getattr(e, name)(*oa, *ia, *args, **kw), reads=rd, writes=list(outs))

    def tt(self, out, a, b, op, eng="dve"):
        self.gen(eng, "tensor_tensor", [out], [a, b], op)

    def ts(self, out, a, s1, s2, op0, op1=None, eng="dve"):
        o, aa = self._ap(out), self._ap(a)
        rd = [a]
        s1a, s2a = s1, s2
        if not isinstance(s1, (int, float)) and s1 is not None:
            rd.append(s1)
            s1a = self._ap(s1)
        if not isinstance(s2, (int, float)) and s2 is not None:
            rd.append(s2)
            s2a = self._ap(s2)
        if op1 is None:
            self.op(eng, lambda e: e.tensor_scalar(o, aa, s1a, None, op0), reads=rd, writes=[out])
        else:
            self.op(eng, lambda e: e.tensor_scalar(o, aa, s1a, s2a, op0, op1), reads=rd, writes=[out])

    def stt(self, out, a, s, b, op0, op1):
        o, aa, bb = self._ap(out), self._ap(a), self._ap(b)
        rd = [a, b]
        sa = s
        if not isinstance(s, (int, float)):
            rd.append(s)
            sa = self._ap(s)
        self.op("dve", lambda e: e.scalar_tensor_tensor(o, aa, sa, bb, op0, op1), reads=rd, writes=[out])

    def copy(self, out, in_, eng="dve"):
        if eng == "act":
            self.act(out, in_, AF.Copy)
        else:
            self.gen(eng, "tensor_copy", [out], [in_])

    def memset(self, out, val, eng="pool"):
        o = self._ap(out)
        self.op(eng, lambda e: e.memset(o, val), reads=[], writes=[out])

    @contextmanager
    def scope(self):
        old = self.es
        with ExitStack() as s_:
            self.es = s_
            yield
            self.barrier()
            self.emit(final=False)
        self.es = old

    def barrier(self):
        for e in ENGS:
            for f in ENGS:
                if f == "sp" or f == e:
                    continue
                v = self.cnt[f]
                if v > 0 and self.seen[e].get(f, 0) < v:
                    self.seen[e][f] = v
                    self.pending[e].append((f, v))
            for j, v in enumerate(self.dval):
                src = ("d", j)
                if v > 0 and self.seen[e].get(src, 0) < v:
                    self.seen[e][src] = v
                    self.pending[e].append((src, v))

    def emit(self, final=True):
        nc = self.nc
        fin = []
        if final:
            for j, v in enumerate(self.dval):
                if v > 0:
                    fin.append((("d", j), v))
        sems = self.sem
        dsem = self.dsem

        def run(eng_name, e):
            for waits, fn, kind in self.ops[eng_name]:
                for src, val in waits:
                    if isinstance(src, tuple):
                        e.wait_ge(dsem[src[1]], val)
                    else:
                        e.wait_ge(sems[src], val)
                ins = fn(e)
                if kind[0] == "d":
                    ins.then_inc(dsem[kind[1]], 16)
                else:
                    ins.then_inc(sems[kind[1]], 1)
            if eng_name == "sp":
                for src, val in fin:
                    e.wait_ge(dsem[src[1]], val)

        with nc.Block() as block:
            @block.sync
            def _(e):
                run("sp", e)

            @block.tensor
            def _(e):
                run("pe", e)

            @block.scalar
            def _(e):
                run("act", e)

            @block.vector
            def _(e):
                run("dve", e)

            @block.gpsimd
            def _(e):
                run("pool", e)
        self.ops = {e: [] for e in ENGS}


NCORES = 8
TT = 1024
NEG = -30000.0
FFN = 2816
NF = FFN // 128
STAGE = 6
NHEAT = 10
DBG = False
SSD_LOOP = True
DEBUG = {}


def build(stage=STAGE):
    nc = bass.Bass("TRN2", target_bir_lowering=False)
    IN = {}
    OUT = {}

    def I(name, shape, dt=F32):
        IN[name] = nc.dram_tensor(name, list(shape), dt, kind="ExternalInput").ap()
        return IN[name]

    def O(name, shape):
        OUT[name] = nc.dram_tensor(name, list(shape), F32, kind="ExternalOutput").ap()
        return OUT[name]

    x_tok = I("x_tok", [TT, 1024])
    cvec = I("cvec", [128, 8])
    mcol_d = I("mcol", [128, 1])
    mask01_d = I("mask01", [128, 40])
    ropeC_d = I("ropeC", [128, TT])
    ropeS_d = I("ropeS", [128, TT])
    ctxkA_d = I("ctxkA", [128, 256])
    ctxkB_d = I("ctxkB", [128, 256])
    ctxv_d = I("ctxv", [256, 128])
    ssd_h0_d = I("ssd_h0", [2, 64, 8, 64])
    modw_d = I("modw", [2, 12, 128, 8, 512])
    modb_d = I("modb", [2, 128, 48])
    ng_d = I("normg", [128, 5, 8])
    win_d = I("win", [6, 128, 8, 512])
    wout_d = I("wout", [2, 128, 8, 512])
    wgu_d = I("wgu", [2, 11, 128, 8, 512])
    wdn_d = I("wdn", [2, 4, 128, 22, 256])
    qg_d = I("qg", [128, 4])
    convw_d = I("convw", [128, 6, 6])
    dtb_d = I("dtb", [128, 16])
    alog_d = I("alog", [128, 16])
    dsk_d = I("dsk", [128, 4])
    ssdg_d = I("ssdg", [128, 4])
    cst_d = I("cst", [128, 6, 128])
    sel_d = I("sel", [16, 16, 128])

    y_o = O("y", [TT, 1024])
    nk_o = O("nk", [TT, 128])
    nv_o = O("nv", [TT, 128])
    nssd_o = O("nssd", [2, 4, 64, 8, 64])
    nrw_o = O("nrw", [128, 4096]) if stage < 6 else None

    with ExitStack() as es:
        P = Prog(nc, es)
        xT = P.sb("xT", [128, 8, TT], F32)
        wbuf = [P.sb("wbuf%d" % i, [128, 6144], BF16) for i in range(2)]
        identf = P.sb("identf", [128, 128], F32)
        identb = P.sb("identb", [128, 128], BF16)
        onesb = P.sb("onesb", [128, 128], BF16)
        modv = P.sb("modv", [128, 2, 48], F32)
        modb = P.sb("modb_s", [128, 2, 48], F32)
        ng = P.sb("ng", [128, 5, 8], F32)
        mcol = P.sb("mcol_s", [128, 1], F32)
        epsc = P.sb("epsc", [128, 1], F32)
        cv = P.sb("cv", [128, 8], F32)
        cvb = P.sb("cvb", [128, 8], BF16)
        PS = [P.ps("ps%d" % i) for i in range(7)]
        PSB = P.ps("psb", [128, 1024], BF16)
        wstate = {"i": 0}

        P.memset(identf[:], 1.0)
        P.op("pool", lambda e: e.affine_select(identf[:], identf[:], [[-1, 128]], ALU.is_equal, 0.0,
                                                 base=0, channel_multiplier=1), reads=[identf], writes=[identf])
        P.copy(identb[:], identf[:])
        P.memset(onesb[:], 1.0)
        P.memset(epsc[:], 1e-6)
        P.dma(modb[:], modb_d.rearrange("l p e -> p l e"))
        P.dma(ng[:], ng_d)
        P.dma(mcol[:], mcol_d)
        P.dma(cv[:], cvec)

        def wload(dram_tile, nelem):
            s = wstate["i"] % 2
            wstate["i"] += 1
            wb = wbuf[s]
            P.dma(wb[:, 0:nelem], dram_tile, q="pool")
            return wb

        with P.scope():
            xs = [P.sb("xstage%d" % i, [128, 1024], F32) for i in range(2)]
            for t in range(8):
                st = xs[t % 2]
                P.dma(st[:], x_tok[t * 128:(t + 1) * 128, :])
                for half in range(2):
                    pb = PS[(2 * t + half) % 4]
                    for j in range(4):
                        c = half * 4 + j
                        P.tr(pb[:, j * 128:(j + 1) * 128], st[:, c * 128:(c + 1) * 128], identf[:])
                    P.copy(xT[:, half * 4:half * 4 + 4, t * 128:(t + 1) * 128],
                           pb[:].rearrange("p (j t) -> p j t", j=4), eng="act" if half else "dve")
        P.act(cvb[:], cv[:], AF.Silu)
        def ada_load(l, tl):
            return wload(modw_d[l, tl].rearrange("p k c -> p (k c)"), 4096)

        def ada_mm(l, tl, wb, pm):
            wv = wb[:, 0:4096].rearrange("p (k c) -> p k c", k=8)
            for j in range(4):
                e = tl * 4 + j
                for k in range(8):
                    P.mm(pm[:, e:e + 1], wv[:, k, j * 128:(j + 1) * 128], cvb[:, k:k + 1],
                         start=(k == 0), stop=(k == 7))

        for tl in range(12):
            ada_mm(0, tl, ada_load(0, tl), PS[4])
        P.tt(modv[:, 0, :], PS[4][:, 0:48], modb[:, 0, :], ALU.add)

        def modcol(l, i, c):
            return modv[:, l, i * 8 + c:i * 8 + c + 1]

        def rms_to_hT(hT, l, which, outf=None):
            with P.scope():
                sq = P.sb("sq", [128, 8, 512], BF16)
                rstd = P.sb("rstd", [128, TT], F32)
                G = P.sb("Gcol", [128, 8], F32)
                tmps = [P.sb("ntmp%d" % i, [128, TT], F32) for i in range(2)]
                if l < 2:
                    ni = 2 * l + which
                    sci = 1 + 3 * which
                    shi = 3 * which
                    P.stt(G[:], modv[:, l, sci * 8:sci * 8 + 8], 1.0, ng[:, ni, :], ALU.add, ALU.mult)
                else:
                    P.copy(G[:], ng[:, 4, :])
                for half in range(2):
                    hs = slice(half * 512, (half + 1) * 512)
                    for c in range(8):
                        if half == 0:
                            P.act(sq[:, c, :], xT[:, c, hs], AF.Square)
                        else:
                            P.tt(sq[:, c, :], xT[:, c, hs], xT[:, c, hs], ALU.mult)
                    for c in range(8):
                        P.mm(PS[half][:], onesb[:], sq[:, c, :], start=(c == 0), stop=(c == 7))
                for half in range(2):
                    hs = slice(half * 512, (half + 1) * 512)
                    P.act(rstd[:, hs], PS[half][:], AF.Sqrt, bias=epsc[:], scale=1.0 / 1024)
                    P.gen("dve", "reciprocal", [rstd[:, hs]], [rstd[:, hs]])
                for c in range(8):
                    tmp = tmps[c % 2]
                    if l < 2:
                        P.stt(tmp[:], xT[:, c, :], G[:, c:c + 1], rstd[:], ALU.mult, ALU.mult)
                        if outf is None:
                            P.act(hT[:, c, :], tmp[:], AF.Identity, bias=modcol(l, shi, c))
                        else:
                            P.act(outf(c), tmp[:].rearrange("p (s t) -> p s t", s=4), AF.Identity, bias=modcol(l, shi, c))
                    else:
                        P.stt(hT[:, c, :], xT[:, c, :], G[:, c:c + 1], rstd[:], ALU.mult, ALU.mult)

        def ffn(l, hT):
            with P.scope():
                hid = P.sb("hid", [128, NF, TT], BF16)
                gtmps = [P.sb("gtmp%d" % i, [128, 512], F32) for i in range(2)]
                for tl in range(11):
                    wb = wload(wgu_d[l, tl].rearrange("p k c -> p (k c)"), 4096)
                    wv = wb[:, 0:4096].rearrange("p (k c) -> p k c", k=8)
                    for j in range(2):
                        f = tl * 2 + j
                        for half in range(2):
                            pg = PS[(2 * half) % 4]
                            pu = PS[(2 * half + 1) % 4]
                            for k in range(8):
                                P.mm(pg[:], wv[:, k, j * 128:(j + 1) * 128], hT[:, k, half * 512:(half + 1) * 512],
                                     start=(k == 0), stop=(k == 7))
                            for k in range(8):
                                P.mm(pu[:], wv[:, k, 256 + j * 128:256 + (j + 1) * 128],
                                     hT[:, k, half * 512:(half + 1) * 512], start=(k == 0), stop=(k == 7))
                            gtmp = gtmps[half]
                            P.act(gtmp[:], pg[:], AF.Silu)
                            P.tt(hid[:, f, half * 512:(half + 1) * 512], gtmp[:], pu[:], ALU.mult)
                for tl in range(4):
                    wb = wload(wdn_d[l, tl].rearrange("p k c -> p (k c)"), 22 * 256)
                    wv = wb[:, 0:22 * 256].rearrange("p (k c) -> p k c", k=22)
                    for j in range(2):
                        c = tl * 2 + j
                        for half in range(2):
                            pb = PS[(2 * j + half) % 4]
                            for k in range(NF):
                                P.mm(pb[:], wv[:, k, j * 128:(j + 1) * 128], hid[:, k, half * 512:(half + 1) * 512],
                                     start=(k == 0), stop=(k == NF - 1))
                            xs_ = xT[:, c, half * 512:(half + 1) * 512]
                            P.stt(xs_, pb[:], modcol(l, 5, c), xs_, ALU.mult, ALU.add)

        def dbg(name, ap_sb, shape):
            DEBUG[name] = shape
            P.dma(O(name, shape), ap_sb)

        with P.scope():
            esL0 = P.es
            qf = P.sb("qf", [128, 4, TT], BF16)
            kall = [P.sb("kall%d" % i, [128, 1280], BF16) for i in range(2)]
            vall = P.sb("vall", [128, 10, 128], BF16)
            vsw = P.sb("vsw", [128, 10, 128], BF16)
            zs = P.sb("zs", [128, 4, TT], BF16)
            xpad = P.sb("xpad", [128, 6, 4, 260], BF16)
            dtraw = P.sb("dtraw", [128, 8, 16], F32)
            attnT = P.sb("attnT", [128, 4, TT], BF16)
            ynT = P.sb("ynT", [128, 4, TT], BF16)
            esP1 = ExitStack()
            P.es = esP1
            hT = P.sb("hT", [128, 8, TT], BF16)
            kf32 = P.sb("kf32", [128, TT], F32)
            vf32 = P.sb("vf32", [128, 8, 128], F32)
            ropeC = P.sb("ropeC_s", [128, TT], F32)
            ropeS = P.sb("ropeS_s", [128, TT], F32)
            qg = P.sb("qg_s", [128, 4], F32)
            tabs = [P.sb("ropetab%d" % i, [128, TT], F32) for i in range(4)]
            P.es = esL0
            rms_to_hT(hT, 0, 0)
            P.es = esP1
            P.dma(ropeC[:], ropeC_d)
            P.dma(ropeS[:], ropeS_d)
            P.dma(qg[:], qg_d)
            for i in range(4):
                P.ts(tabs[i][:], (ropeC if i % 2 == 0 else ropeS)[:], qg[:, i:i + 1], None, ALU.mult)
            kst = P.sb("kst", [128, 256], F32)
            P.dma(kst[:], ctxkA_d)
            P.copy(kall[0][:, 0:256], kst[:])
            kst2 = P.sb("kst2", [128, 256], F32)
            P.dma(kst2[:], ctxkB_d)
            P.copy(kall[1][:, 0:256], kst2[:])
            P.dma(vall[:, 0:2, :], ctxv_d.rearrange("(t p) c -> p t c", p=128), q="pool")
            cvv = ctxv_d.rearrange("(t s p) c -> s p t c", s=2, p=64)
            P.dma(vsw[0:64, 0:2, :], cvv[1], q="pool")
            P.dma(vsw[64:128, 0:2, :], cvv[0], q="pool")
            sqq = P.sb("sqq", [128, 512], BF16)
            rq = P.sb("rq", [128, 512], F32)
            t1 = P.sb("qt1", [128, 512], F32)
            t2 = P.sb("qt2", [128, 512], F32)
            bd = P.sb("bdones", [128, 128], BF16)
            P.memset(bd[:], 0.0)
            P.memset(bd[0:64, 0:64], 1.0)
            P.memset(bd[64:128, 64:128], 1.0)

            def qk_post(pq, pr, tabC, tabS, half, outs):
                hs = slice(half * 512, (half + 1) * 512)
                P.act(sqq[:], pq[:], AF.Square)
                P.tt(t1[:], pq[:], tabC[:, hs], ALU.mult)
                P.tt(t2[:], pr[:], tabS[:, hs], ALU.mult)
                P.mm(pr[:], bd[:], sqq[:])
                P.act(rq[:], pr[:], AF.Sqrt, bias=epsc[:], scale=1.0 / 64)
                P.gen("dve", "reciprocal", [rq[:]], [rq[:]])
                P.tt(t1[:], t1[:], t2[:], ALU.add)
                for o in outs:
                    P.tt(o, t1[:], rq[:], ALU.mult)

            def lin4(wv, j, half, pb):
                for k in range(8):
                    P.mm(pb[:], wv[:, k, j * 128:(j + 1) * 128], hT[:, k, half * 512:(half + 1) * 512],
                         start=(k == 0), stop=(k == 7))

            wvs = {}

            def get_wv(tl):
                if tl not in wvs:
                    wb = wload(win_d[tl].rearrange("p k c -> p (k c)"), 4096)
                    wvs[tl] = wb[:, 0:4096].rearrange("p (k c) -> p k c", k=8)
                return wvs[tl]

            items = [(tl, half, jj) for tl in range(3) for half in range(2) for jj in range(2)]

            def qk_lin(i):
                tl, half, jj = items[i]
                wv = get_wv(tl)
                lin4(wv, 2 * jj, half, PS[2 * jj])
                lin4(wv, 2 * jj + 1, half, PS[2 * jj + 1])

            def qk_fin(i):
                tl, half, jj = items[i]
                hs = slice(half * 512, (half + 1) * 512)
                pq, pr = PS[2 * jj], PS[2 * jj + 1]
                if tl < 2:
                    qk_post(pq, pr, tabs[0], tabs[1], half, [qf[:, tl * 2 + jj, hs]])
                else:
                    outs = [kall[jj][:, 256 + half * 512:256 + (half + 1) * 512]]
                    if jj == 0:
                        outs.append(kf32[:, hs])
                    qk_post(pq, pr, tabs[2], tabs[3], half, outs)

            qk_lin(0)
            for i in range(len(items)):
                if i + 1 < len(items):
                    qk_lin(i + 1)
                qk_fin(i)

            for tl in range(3, 6):
                wv = get_wv(tl)
                for half in range(2):
                    hs = slice(half * 512, (half + 1) * 512)
                    if tl < 3:
                        pass
                    elif tl == 3:
                        for j in range(4):
                            pb = PS[j]
                            lin4(wv, j, half, pb)
                            P.act(zs[:, j, hs], pb[:], AF.Silu)
                    elif tl == 4:
                        for j in range(4):
                            pb = PS[j]
                            lin4(wv, j, half, pb)
                            P.copy(xpad[:, j, 2 * half:2 * half + 2, 2:258], pb[:].rearrange("p (s t) -> p s t", s=2),
                                   eng="act" if j % 2 else "dve")
                    else:
                        for j in range(2):
                            pb = PS[j]
                            lin4(wv, j, half, pb)
                            P.copy(xpad[:, 4 + j, 2 * half:2 * half + 2, 2:258],
                                   pb[:].rearrange("p (s t) -> p s t", s=2), eng="act" if j % 2 else "dve")
                if tl == 5:
                    for t in range(8):
                        pb = PS[4 + t % 2]
                        for k in range(8):
                            P.mm(pb[:, 0:144], hT[:, k, t * 128:(t + 1) * 128], wv[:, k, 256:400],
                                 start=(k == 0), stop=(k == 7))
                        P.copy(vf32[:, t, :], pb[:, 0:128], eng="act")
                        P.copy(vall[:, 2 + t, :], pb[:, 0:128])
                        pb2 = PS[6]
                        for sh in range(2):
                            tsl = slice(t * 128 + (1 - sh) * 64, t * 128 + (1 - sh) * 64 + 64)
                            for k in range(8):
                                P.mm(pb2[sh * 64:sh * 64 + 64, 0:128], hT[:, k, tsl], wv[:, k, 256:384],
                                     start=(k == 0), stop=(k == 7))
                        P.copy(vsw[:, 2 + t, :], pb2[:, 0:128], eng="act")
                        P.copy(dtraw[:, t, :], pb[:, 128:144])
            ktok = P.sb("ktok", [128, 8, 128], F32)
            for t in range(8):
                pb = PS[t % 4]
                P.tr(pb[:, 0:128], kf32[:, t * 128:(t + 1) * 128], identf[:])
                P.copy(ktok[:, t, :], pb[:, 0:128], eng="act" if t % 2 else "dve")
            P.dma(nk_o.rearrange("(t p) c -> p t c", p=128), ktok[:])
            P.dma(nv_o.rearrange("(t p) c -> p t c", p=128), vf32[:])
            P.barrier()
            P.emit(final=False)
            esP1.close()
            P.es = esL0
            if stage <= 1:
                P.emit(final=True)
                return nc, IN, OUT
            with P.scope():
                m01 = P.sb("m01", [128, 40], F32)
                P.dma(m01[:], mask01_d)
                pT = [P.sb("pT%d" % i, [128, 512], BF16) for i in range(4)]
                rden = P.sb("rden", [128, 512], F32)
                bdo = P.sb("bdo", [128, 128], BF16)
                P.memset(bdo[:], 0.0)
                P.memset(bdo[0:64, 0:64], 1.0)
                P.memset(bdo[64:128, 64:128], 1.0)
                kbd = [P.sb("kbd%d" % g, [128, 10, 2, 128], BF16) for g in range(2)]
                vbd = [P.sb("vbd%d" % g, [128, 10, 2, 128], BF16) for g in range(2)]
                for g in range(2):
                    top, bot = (kall[0], kall[1]) if g == 0 else (kall[1], kall[0])
                    topv = top[:].rearrange("p (t s c) -> p t s c", s=2, c=64)
                    botv = bot[:].rearrange("p (t s c) -> p t s c", s=2, c=64)
                    P.memset(kbd[g][:].rearrange("p a b c -> p (a b c)"), 0.0, eng="pool")
                    P.memset(vbd[g][:].rearrange("p a b c -> p (a b c)"), 0.0, eng="pool")
                    P.copy(kbd[g][0:64, :, :, 0:64], topv[0:64])
                    P.copy(kbd[g][64:128, :, 0, 64:128], botv[64:128, :, 1, :], eng="act")
                    P.copy(kbd[g][64:128, :, 1, 64:128], botv[64:128, :, 0, :], eng="act")
                    gc = slice(g * 64, (g + 1) * 64)
                    P.copy(vbd[g][0:64, :, 0, 0:64], vall[0:64, :, gc])
                    P.copy(vbd[g][64:128, :, 0, 64:128], vall[64:128, :, gc], eng="act")
                    P.copy(vbd[g][0:64, :, 1, 0:64], vsw[0:64, :, gc])
                    P.copy(vbd[g][64:128, :, 1, 64:128], vsw[64:128, :, gc], eng="act")
                ada_next = [0]
                for j in range(4):
                    g = j // 2
                    for half in range(2):
                        hs = slice(half * 512, (half + 1) * 512)
                        po, pd = PS[4], PS[5]
                        it = j * 2 + half
                        n_ada = 2 if it < 4 else 1
                        wb_ada = []
                        for _ in range(n_ada):
                            if ada_next[0] < 12:
                                wb_ada.append((ada_next[0], ada_load(1, ada_next[0])))
                                ada_next[0] += 1

                        def pv(st):
                            kt, s = st // 2, st % 2
                            P.mm(po[:], vbd[g][:, kt, s, :], pT[st % 4][:], start=(st == 0), stop=(st == 19))
                            P.mm(pd[:], bdo[:], pT[st % 4][:], start=(st == 0), stop=(st == 19))
                        for st in range(20):
                            kt, s = st // 2, st % 2
                            pS = PS[st % 3]
                            P.mm(pS[:], kbd[g][:, kt, s, :], qf[:, j, hs])
                            P.act(pT[st % 4][:], pS[:], AF.Exp, scale=0.125)
                            for qb in range(2):
                                mi = kt * 4 + half * 2 + qb
                                cs_ = slice(qb * 256, (qb + 1) * 256)
                                P.ts(pT[st % 4][:, cs_], pT[st % 4][:, cs_], m01[:, mi:mi + 1], None, ALU.mult)
                            if st > 1:
                                pv(st - 2)
                        pv(18)
                        pv(19)
                        for (tl_, wb_) in wb_ada:
                            ada_mm(1, tl_, wb_, PS[6])
                        P.gen("dve", "reciprocal", [rden[:]], [pd[:]])
                        P.tt(attnT[:, j, hs], po[:], rden[:], ALU.mult)
                P.tt(modv[:, 1, :], PS[6][:, 0:48], modb[:, 1, :], ALU.add)
            if stage <= 2:
                P.emit(final=True)
                return nc, IN, OUT
            with P.scope():
                cst = P.sb("cst_s", [128, 6, 128], F32)
                cstb = P.sb("cstb", [128, 6, 128], BF16)
                sel = P.sb("sel_s", [16, 16, 128], F32)
                convw = P.sb("convw_s", [128, 6, 6], F32)
                dtb = P.sb("dtb_s", [128, 16], F32)
                Aneg = P.sb("Aneg", [128, 16], F32)
                dsk = P.sb("dsk_s", [128, 4], F32)
                ssdg = P.sb("ssdg_s", [128, 4], F32)
                P.dma(cst[:], cst_d)
                P.dma(sel[:], sel_d)
                P.dma(convw[:], convw_d)
                P.dma(dtb[:], dtb_d)
                P.dma(Aneg[:], alog_d)
                P.dma(dsk[:], dsk_d)
                P.dma(ssdg[:], ssdg_d)
                P.copy(cstb[:], cst[:])
                P.act(Aneg[:], Aneg[:], AF.Exp)
                P.ts(Aneg[:], Aneg[:], -1.0, None, ALU.mult)
                P.memset(xpad[:, :, 0, 0:2], 0.0, eng="dve")
                P.memset(xpad[:, :, 3, 258:260], 0.0, eng="dve")
                P.ts(xpad[:, :, 1:4, 0:2], xpad[:, :, 0:3, 256:258], mcol[:], None, ALU.mult)
                P.ts(xpad[:, :, 0:3, 258:260], xpad[:, :, 1:4, 2:4], mcol[:], None, ALU.mult)
                xc = P.sb("xc", [128, 6, TT], BF16)
                acc = P.sb("cacc", [128, 4, 256], F32)
                for c in range(6):
                    P.ts(acc[:], xpad[:, c, :, 0:256], convw[:, c, 0:1], convw[:, c, 5:6], ALU.mult, ALU.add)
                    for i in range(1, 5):
                        P.stt(acc[:], xpad[:, c, :, i:i + 256], convw[:, c, i:i + 1], acc[:], ALU.mult, ALU.add)
                    P.act(xc[:, c, :], acc[:].rearrange("p s t -> p (s t)"), AF.Silu)
                xtok = P.sb("xtok", [128, 8, 640], BF16)
                for t in range(8):
                    for jx in range(5):
                        P.tr(PSB[:, jx * 128:(jx + 1) * 128], xc[:, jx, t * 128:(t + 1) * 128], identb[:])
                    P.copy(xtok[:, t, :], PSB[:, 0:640], eng="act" if t % 2 else "dve")
                dtt = P.sb("dtt", [128, 8, 16], F32)
                atok = P.sb("atok", [128, 8, 16], F32)
                P.tt(dtt[:], dtraw[:], dtb[:].rearrange("p (o e) -> p o e", o=1).to_broadcast([128, 8, 16]), ALU.add)
                P.act(dtt[:], dtt[:], AF.Exp)
                P.act(dtt[:], dtt[:], AF.Ln, bias=1.0)
                P.tt(atok[:], dtt[:], Aneg[:].rearrange("p (o e) -> p o e", o=1).to_broadcast([128, 8, 16]), ALU.mult)
                if stage <= 2.5:
                    P.emit(final=True)
                    return nc, IN, OUT
                onesf = P.sb("onesf", [128, 128], F32)
                P.memset(onesf[:], 1.0)
                ysum = P.sb("ysum", [128, 4, TT], F32)
                hin = P.sb("hin", [128, 4, 64], F32)
                hinb = P.sb("hinb", [128, 4, 64], BF16)
                h0s = P.sb("h0s", [64, 8, 64], F32)
                hout = P.sb("hout", [64, 8, 64], F32)
                negcs2 = [P.sb("negcs%d" % i, [128, 8], F32) for i in range(2)]
                csT2 = [None, None]
                csH = [P.sb("csH%d" % i, [16, 128], BF16) for i in range(2)]
                csL = [P.sb("csL%d" % i, [16, 128], BF16) for i in range(2)]
                selb = P.sb("selb", [16, 16, 128], BF16)
                P.copy(selb[:], sel[:])
                etot2 = [P.sb("etot%d" % i, [128, 16], F32) for i in range(2)]
                de2 = [P.sb("de%d" % i, [128, 8], F32) for i in range(2)]
                GT2 = [P.sb("GT%d" % i, [128, 2, 128], F32) for i in range(2)]
                xdt2 = [P.sb("xdt%d" % i, [128, 8, 64], BF16) for i in range(2)]
                Bd2 = [P.sb("Bd%d" % i, [128, 8, 64], BF16) for i in range(2)]
                segs = [P.sb("seg%d" % i, [128, 8, 128], F32) for i in range(2)]
                scs = [P.sb("sc%d" % i, [128, 8, 128], BF16) for i in range(2)]
                eas = [P.sb("ea%d" % i, [128, 4, 128], F32) for i in range(2)]
                Css = [P.sb("Cs%d" % i, [128, 4, 128], BF16) for i in range(2)]
                htmp = P.sb("htmp", [128, 4, 64], F32)
                pSm, pG, pD, pE, pY, pSt, pO = PS[0], PS[1], PS[2], PS[3], PS[4], PS[5], PS[6]
                if not SSD_LOOP:
                    for j in range(4):
                        P.memset(ysum[:, j, :], 0.0, eng="dve")
                    P.memset(hout[:], 0.0, eng="dve")
                    for d in range(2):
                        for sidx in range(4):
                            P.dma(nssd_o[d, sidx], hout[:])
                for d in (range(2) if SSD_LOOP else []):
                    P.dma(h0s[:], ssd_h0_d[d])
                    for h in range(8):
                        bg = (h // 4) * 64
                        P.mm(pSt[bg:bg + 64, (h % 4) * 64:(h % 4 + 1) * 64], h0s[:, h, :], identf[0:64, 0:64])
                    P.copy(hin[:], pSt[:, 0:256].rearrange("p (s q) -> p s q", s=4))
                    P.copy(hinb[:], hin[:], eng="act")
                    tri = cst[:, d, :]
                    msk = cstb[:, 2 + d, :]
                    def chunk_pro(ci):
                        c = ci if d == 0 else 7 - ci
                        cl = slice(c * 128, (c + 1) * 128)
                        pz = ci % 2
                        negcs, csT, etot, de, GT, xdt, Bd = negcs2[pz], csT2[pz], etot2[pz], de2[pz], GT2[pz], xdt2[pz], Bd2[pz]
                        a_c = atok[:, c, d * 8:(d + 1) * 8]
                        P.mm(pSm[:, 0:8], tri, a_c)
                        P.mm(pSm[0:16, 128:256], atok[:, c, :], tri)
                        P.mm(pSm[:, 256:272], onesf[:], atok[:, c, :])
                        P.ts(negcs[:], pSm[:, 0:8], -1.0, None, ALU.mult)
                        P.copy(csH[pz][:], pSm[0:16, 128:256], eng="act")
                        P.tt(csL[pz][:], pSm[0:16, 128:256], csH[pz][:], ALU.subtract)
                        P.act(etot[:], pSm[:, 256:272], AF.Exp)
                        P.tt(de[:], pSm[:, 256 + d * 8:256 + d * 8 + 8], negcs[:], ALU.add)
                        P.act(de[:], de[:], AF.Exp)
                        P.mm(pG[:, 0:128], xc[0:64, 4, cl], xc[0:64, 5, cl])
                        P.copy(GT[:, 0, :], pG[:, 0:128], eng="act")
                        P.mm(pG[:, 128:256], xc[64:128, 4, cl], xc[64:128, 5, cl])
                        P.copy(GT[:, 1, :], pG[:, 128:256], eng="act")
                        P.tt(xdt[:], xtok[:, c, 0:512].rearrange("p (h q) -> p h q", h=8),
                             dtt[:, c, d * 8:(d + 1) * 8].rearrange("p (h o) -> p h o", o=1).to_broadcast([128, 8, 64]),
                             ALU.mult)
                        P.tt(Bd[:].rearrange("p (g f) n -> p g f n", g=2),
                             xtok[:, c, 512:640].rearrange("p (g o n) -> p g o n", g=2, o=1).to_broadcast([128, 2, 4, 64]),
                             de[:].rearrange("p (g f o) -> p g f o", g=2, o=1).to_broadcast([128, 2, 4, 64]), ALU.mult)

                    def stages_a(ci):
                        c = ci if d == 0 else 7 - ci
                        cl = slice(c * 128, (c + 1) * 128)
                        pz = ci % 2
                        negcs, csT, etot, de, GT, xdt, Bd = negcs2[pz], csT2[pz], etot2[pz], de2[pz], GT2[pz], xdt2[pz], Bd2[pz]
                        segA, scA, eaA, CsA = segs[ci % 2], scs[ci % 2], eas[ci % 2], Css[ci % 2]
                        for h in range(8):
                            e = d * 8 + h
                            pDh = (pD if h < 4 else pO)[:, (h % 4) * 128:(h % 4 + 1) * 128]
                            P.mm(pDh, selb[:, e, :], csH[pz][:], start=True, stop=False)
                            P.mm(pDh, selb[:, e, :], csL[pz][:], start=False, stop=False)
                            P.mm(pDh, cstb[:, 4, :], msk, start=False, stop=True)
                        for h in range(8):
                            pDh = (pD if h < 4 else pO)[:, (h % 4) * 128:(h % 4 + 1) * 128]
                            P.act(segA[:, h, :], pDh, AF.Exp, bias=negcs[:, h:h + 1])
                        for g in range(2):
                            P.tt(scA[:, 4 * g:4 * g + 4, :], segA[:, 4 * g:4 * g + 4, :],
                                 GT[:, g:g + 1, :].to_broadcast([128, 4, 128]), ALU.mult)
                        for h in range(8):
                            bg, e = (h // 4) * 64, d * 8 + h
                            pEh = pE[bg:bg + 64, (h % 4) * 128:(h % 4 + 1) * 128]
                            P.mm(pEh, selb[:, e, 0:64], csH[pz][:], start=True, stop=False)
                            P.mm(pEh, selb[:, e, 0:64], csL[pz][:], start=False, stop=True)
                        for g in range(2):
                            bgs = slice(g * 64, g * 64 + 64)
                            P.act(eaA[bgs].rearrange("p a b -> p (a b)"), pE[bgs, 0:512], AF.Exp)
                            P.tt(CsA[bgs], eaA[bgs],
                                 xc[bgs, 5, cl].rearrange("p (o l) -> p o l", o=1).to_broadcast([64, 4, 128]), ALU.mult)

                    def stage_b(ci):
                        c = ci if d == 0 else 7 - ci
                        cl = slice(c * 128, (c + 1) * 128)
                        pz = ci % 2
                        negcs, csT, etot, de, GT, xdt, Bd = negcs2[pz], csT2[pz], etot2[pz], de2[pz], GT2[pz], xdt2[pz], Bd2[pz]
                        segA, scA, eaA, CsA = segs[ci % 2], scs[ci % 2], eas[ci % 2], Css[ci % 2]
                        for h in range(8):
                            bg = (h // 4) * 64
                            b0 = (h % 2) * 64
                            pYh = pY[b0:b0 + 64, (h // 2) * 128:(h // 2 + 1) * 128]
                            P.mm(pYh, xdt[:, h, :], scA[:, h, :], start=True, stop=False)
                            P.mm(pYh, hinb[bg:bg + 64, h % 4, :], CsA[bg:bg + 64, h % 4, :], start=False, stop=True)
                            P.mm(pSt[bg:bg + 64, (h % 4) * 64:(h % 4 + 1) * 64], Bd[:, h, :], xdt[:, h, :])
                        pYv = pY[:].rearrange("p (j l) -> p j l", j=4)
                        if d == 0:
                            P.copy(ysum[:, :, cl], pYv)
                        else:
                            P.tt(ysum[:, :, cl], pYv, ysum[:, :, cl], ALU.add)
                        for hf in range(2):
                            ps_ = slice(hf * 64, (hf + 1) * 64)
                            P.tt(htmp[ps_], hin[ps_],
                                 etot[ps_, d * 8 + hf * 4:d * 8 + hf * 4 + 4].rearrange("p (s o) -> p s o", o=1).to_broadcast([64, 4, 64]),
                                 ALU.mult)
                        P.tt(hin[:], htmp[:], pSt[:, 0:256].rearrange("p (s q) -> p s q", s=4), ALU.add)
                        seg_end = (c % 2 == 1) if d == 0 else (c % 2 == 0)
                        if seg_end:
                            sidx = c // 2
                            for h in range(8):
                                bg = (h // 4) * 64
                                P.tr((pO if h < 4 else pG)[0:64, (h % 4) * 64:(h % 4 + 1) * 64], hin[bg:bg + 64, h % 4, :], identf[bg:bg + 64, bg:bg + 64])
                            P.copy(hout[:, 0:4, :], pO[0:64, 0:256].rearrange("p (h n) -> p h n", h=4), eng="act")
                            P.copy(hout[:, 4:8, :], pG[0:64, 0:256].rearrange("p (h n) -> p h n", h=4), eng="act")
                            P.dma(nssd_o[d, sidx], hout[:])
                            P.ts(hin[:], hin[:], mcol[:], None, ALU.mult)
                        P.copy(hinb[:], hin[:], eng="act")

                    chunk_pro(0)
                    stages_a(0)
                    for ci in range(8):
                        if ci + 1 < 8:
                            chunk_pro(ci + 1)
                            stages_a(ci + 1)
                        stage_b(ci)
                sq4 = xtok[:, 0:4, 0:512]
                rs4 = segs[0][:].rearrange("p a b -> p (a b)")
                for j in range(4):
                    P.stt(ysum[:, j, :], xc[:, j, :], dsk[:, j:j + 1], ysum[:, j, :], ALU.mult, ALU.add)
                    P.tt(ysum[:, j, :], ysum[:, j, :], zs[:, j, :], ALU.mult)
                for half in range(2):
                    hs = slice(half * 512, (half + 1) * 512)
                    for j in range(4):
                        P.act(sq4[:, j, :], ysum[:, j, hs], AF.Square)
                    for j in range(4):
                        P.mm(PS[half][:], onesb[:], sq4[:, j, :], start=(j == 0), stop=(j == 3))
                    P.act(rs4[:, hs], PS[half][:], AF.Sqrt, bias=epsc[:], scale=1.0 / 512)
                P.gen("dve", "reciprocal", [rs4[:]], [rs4[:]])
                for j in range(4):
                    P.stt(ynT[:, j, :], ysum[:, j, :], ssdg[:, j:j + 1], rs4[:], ALU.mult, ALU.mult)
            if stage <= 3:
                P.emit(final=True)
                return nc, IN, OUT
            for tl in range(2):
                wb = wload(wout_d[tl].rearrange("p k c -> p (k c)"), 4096)
                wv = wb[:, 0:4096].rearrange("p (k c) -> p k c", k=8)
                for j in range(4):
                    c = tl * 4 + j
                    for half in range(2):
                        hs = slice(half * 512, (half + 1) * 512)
                        pb = PS[(2 * j + half) % 4]
                        for kk in range(8):
                            rhs = attnT[:, kk, hs] if kk < 4 else ynT[:, kk - 4, hs]
                            P.mm(pb[:], wv[:, kk, j * 128:(j + 1) * 128], rhs, start=(kk == 0), stop=(kk == 7))
                        P.stt(xT[:, c, hs], pb[:], modcol(0, 2, c), xT[:, c, hs], ALU.mult, ALU.add)
        if DBG:
            P.dma(O("dbg_a", [128, 8, TT]), xT[:])
        with P.scope():
            hT2 = P.sb("hT2", [128, 8, TT], BF16)
            rms_to_hT(hT2, 0, 1)
            ffn(0, hT2)
        if DBG:
            P.dma(O("dbg_b", [128, 8, TT]), xT[:])
        if stage >= 6:
            rwkv_layer(P, nc, IN, OUT, xT, PS, PSB, modcol, rms_to_hT, wload, identf, identb, onesb, epsc, mcol, I, O, wbuf)
        if DBG:
            P.dma(O("dbg_c", [128, 8, TT]), xT[:])
        with P.scope():
            hT3 = P.sb("hT3", [128, 8, TT], BF16)
            rms_to_hT(hT3, 1, 1)
            ffn(1, hT3)
        with P.scope():
            if stage < 6:
                zt = P.sb("zt", [128, 4096], F32)
                P.memset(zt[:], 0.0)
                P.dma(nrw_o, zt[:])
            yT = P.sb("yT", [128, 8, TT], F32)
            rms_to_hT(yT, 2, 0)
            ost = [P.sb("ostage%d" % i, [128, 1024], F32) for i in range(2)]
            for t in range(8):
                st = ost[t % 2]
                for half in range(2):
                    pb = PS[(2 * t + half) % 4]
                    for j in range(4):
                        P.tr(pb[:, j * 128:(j + 1) * 128], yT[:, half * 4 + j, t * 128:(t + 1) * 128], identf[:])
                    P.copy(st[:, half * 512:(half + 1) * 512], pb[:], eng="act" if half else "dve")
                P.dma(y_o[t * 128:(t + 1) * 128, :], st[:])
        P.emit(final=True)
    return nc, IN, OUT


def rwkv_layer(P, nc, IN, OUT, xT, PS, PSB, modcol, rms_to_hT, wload, identf, identb, onesb, epsc, mcol, I, O, wbuf):
    mu_d = I("rw_mu", [128, 2, 6, 8])
    wr_d = I("rw_wr", [2, 128, 8, 512])
    wk_d = I("rw_wk", [2, 128, 8, 512])
    wv_d = I("rw_wv", [2, 128, 8, 512])
    wo_d = I("rw_wo", [2, 128, 8, 512])
    l1_d = I("rw_l1", [128, 8, 384])
    l2_d = I("rw_l2", [128, 3, 1024])
    cols_d = I("rw_cols", [128, 8, 8])
    lnbc_d = I("rw_ln", [128, 2, 1024])
    s0_d = I("rw_s0", [2, 8, 128, 64])
    rmask_d = I("rw_mask", [128, 6, 128])
    nrw_o = O("nrw", [2, 4, 8, 128, 64])

    with P.scope():
        mu = P.sb("mu", [128, 2, 6, 8], F32)
        c0 = P.sb("c0", [128, 6, 8], F32)
        cols = P.sb("rwcols", [128, 8, 8], F32)
        l2 = P.sb("l2", [128, 3, 1024], BF16)
        mk = P.sb("rmk", [128, 8, 128], BF16)
        blkf = P.sb("blkf", [128, 128], F32)
        bd = P.sb("bd2", [128, 128], BF16)
        E2 = P.sb("E2", [128, 2], BF16)
        rst = P.sb("rst", [128, 1024], BF16)
        eps12 = P.sb("eps12", [128, 1], F32)
        gneps = P.sb("gneps", [128, 1], F32)
        P.dma(mu[:], mu_d)
        P.dma(cols[:], cols_d)
        P.dma(l2[:], l2_d, q="pool")
        P.memset(rst[:], 1.0)
        P.memset(rst[:].rearrange("p (c t) -> p c t", t=64)[:, :, 0:1], 0.0)
        P.memset(eps12[:], 1e-12)
        P.memset(gneps[:], 64e-5)
        P.tt(c0[:], mu[:, 0, :, :], mu[:, 1, :, :], ALU.add)
        P.ts(c0[:], c0[:], -1.0, 1.0, ALU.mult, ALU.add)
        P.ts(cols[:, :, 6:7], cols[:, :, 5:6], -1.0, 1.0, ALU.mult, ALU.add)

        r_all = P.sb("r_all", [128, 8, TT], BF16)
        k_all = P.sb("k_all", [128, 8, TT], BF16)
        v_tok = P.sb("v_tok", [128, 8, 1024], BF16)
        tw = P.sb("tw", [128, TT], BF16)
        ta = P.sb("ta", [128, TT], BF16)
        sg = P.sb("sg", [128, TT], BF16)
        oT = r_all
        with P.scope():
            hpad = P.sb("hpad", [128, 8, 4, 258], BF16)
            l1 = P.sb("l1", [128, 8, 384], BF16)
            P.dma(l1[:], l1_d, q="pool")
            rmf = P.sb("rmf", [128, 6, 128], F32)
            P.dma(rmf[:], rmask_d)
            P.ts(mk[:, 0, :], rmf[:, 0, :], -1.0, None, ALU.mult)
            P.ts(mk[:, 1, :], rmf[:, 1, :], -1.0, None, ALU.mult)
            P.copy(mk[:, 2, :], rmf[:, 0, :])
            P.copy(mk[:, 3, :], rmf[:, 1, :])
            P.copy(mk[:, 4, :], rmf[:, 2, :])
            P.copy(mk[:, 5, :], rmf[:, 3, :])
            P.ts(mk[:, 6, :], rmf[:, 4, :], -1.0, None, ALU.mult)
            P.copy(mk[:, 7, :], rmf[:, 4, :])
            P.copy(blkf[:], rmf[:, 4, :])
            P.copy(bd[:], rmf[:, 4, :])
            P.copy(E2[:, 0:1], rmf[:, 4, 0:1])
            P.copy(E2[:, 1:2], rmf[:, 4, 64:65])
            xms = [P.sb("xm%d" % i, [128, 8, TT], BF16) for i in range(2)]
            rms_to_hT(None, 1, 0, outf=lambda c: hpad[:, c, :, 1:257])
            P.memset(hpad[:, :, 0, 0:1], 0.0, eng="dve")
            P.memset(hpad[:, :, 3, 257:258], 0.0, eng="dve")
            P.ts(hpad[:, :, 1:4, 0:1], hpad[:, :, 0:3, 256:257], mcol[:], None, ALU.mult)
            P.ts(hpad[:, :, 0:3, 257:258], hpad[:, :, 1:4, 1:2], mcol[:], None, ALU.mult)

            def mix(i):
                xm = xms[i % 2]
                for c in range(8):
                    xv_ = xm[:, c, :].rearrange("p (s t) -> p s t", s=4)
                    P.act(xv_, hpad[:, c, :, 1:257], AF.Copy, scale=c0[:, i, c:c + 1])
                    P.stt(xv_, hpad[:, c, :, 0:256], mu[:, 0, i, c:c + 1], xv_, ALU.mult, ALU.add)
                    P.stt(xv_, hpad[:, c, :, 2:258], mu[:, 1, i, c:c + 1], xv_, ALU.mult, ALU.add)
                return xm

            def proj_fm(xm, w_d, dst):
                for tl in range(2):
                    wb = wload(w_d[tl].rearrange("p k c -> p (k c)"), 4096)
                    wv = wb[:, 0:4096].rearrange("p (k c) -> p k c", k=8)
                    for jj in range(4):
                        c = tl * 4 + jj
                        for half in range(2):
                            hs = slice(half * 512, (half + 1) * 512)
                            pb = PS[(2 * jj + half) % 4]
                            for kk in range(8):
                                P.mm(pb[:], wv[:, kk, jj * 128:(jj + 1) * 128], xm[:, kk, hs], start=(kk == 0), stop=(kk == 7))
                            P.copy(dst[:, c, hs], pb[:], eng="act" if half else "dve")

            def lora1(xm, cs_, func, dst):
                for half in range(2):
                    hs = slice(half * 512, (half + 1) * 512)
                    pb = PS[4 + half]
                    for kk in range(8):
                        P.mm(pb[:], l1[:, kk, cs_], xm[:, kk, hs], start=(kk == 0), stop=(kk == 7))
                    P.act(dst[:, hs], pb[:], func)

            def vproj(xm):
                for tl in range(2):
                    wb = wload(wv_d[tl].rearrange("p k c -> p (k c)"), 4096)
                    wv = wb[:, 0:4096].rearrange("p (k c) -> p k c", k=8)
                    for t in range(8):
                        pb = PS[t % 4]
                        for kk in range(8):
                            P.mm(pb[:], xm[:, kk, t * 128:(t + 1) * 128], wv[:, kk, :], start=(kk == 0), stop=(kk == 7))
                        P.copy(v_tok[:, t, tl * 512:(tl + 1) * 512], pb[:], eng="act" if t % 2 else "dve")

            consumers = [lambda xm: proj_fm(xm, wr_d, r_all),
                         lambda xm: lora1(xm, slice(0, 128), AF.Tanh, tw),
                         lambda xm: proj_fm(xm, wk_d, k_all),
                         vproj,
                         lambda xm: lora1(xm, slice(128, 256), AF.Copy, ta),
                         lambda xm: lora1(xm, slice(256, 384), AF.Sigmoid, sg)]
            xmv = {0: mix(0)}
            for i in range(6):
                if i + 1 < 6:
                    xmv[i + 1] = mix(i + 1)
                consumers[i](xmv[i])

        S_f = [[None] * 8, [None] * 8]
        with P.scope():
            Sfd = [P.sb("Sf%d" % i, [128, 128], F32) for i in range(2)]
            Sbd = [P.sb("Sb%d" % i, [128, 128], BF16) for i in range(2)]
            f32t = lambda n: P.sb(n, [128, TT], F32)
            b16t = lambda n: P.sb(n, [128, TT], BF16)
            lwv, csv, ex, tmpf = [f32t(n) for n in ("lwv", "csv", "ex", "tmpf")]
            kkr, av, kdir, bvec = [b16t(n) for n in ("kkr", "av", "kdir", "bvec")]
            bsum2 = [b16t("bsum%d" % i) for i in range(2)]
            rinv, kap = ex, kkr
            lnbc2 = [P.sb("lnbc%d" % i, [128, 2, 128], F32) for i in range(2)]
            sqb = P.sb("sqb", [128, 512], BF16)
            gj2 = [P.sb("gj%d" % i, [128, TT], BF16) for i in range(2)]
            rkb = P.sb("rkb", [128, TT], BF16)
            bon = P.sb("bon", [128, 8, 2], F32)
            Rt = P.sb("Rt", [128, TT], BF16)
            Kt = P.sb("Kt", [128, TT], BF16)
            Kh = P.sb("Kh", [128, TT], BF16)
            Bh = P.sb("Bh", [128, TT], BF16)
            WLt = P.sb("WLt", [128, 16], F32)
            tokm = P.sb("tokm", [128, 8, 384], BF16)
            ytok2 = [P.sb("ytok%d" % i, [128, 8, 128], F32) for i in range(2)]
            Gm = [P.sb("Gm%d" % i, [128, 4, 128], BF16) for i in range(8)]
            Rb = [P.sb("Rb%d" % i, [128, 128], BF16) for i in range(8)]
            X4s = [P.sb("X4_%d" % i, [128, 4, 128], BF16) for i in range(2)]
            SQ = [[P.sb("SQ%d_%d" % (h_, i), [128, 2, 2, 128], BF16) for i in range(2)] for h_ in range(2)]
            QT = [P.sb("QT%d" % i, [128, 128], BF16) for i in range(4)]
            ATb = [P.sb("ATb%d" % i, [128, 128], BF16) for i in range(2)]
            negI = P.sb("negI", [128, 128], BF16)
            P.ts(negI[:], identf[:], -1.0, None, ALU.mult)
            Cpp = [P.sb("Cpp%d" % i, [128, 128], F32) for i in range(2)]
            mcat = [P.sb("mcat%d" % i, [128, 4, 128], BF16) for i in range(2)]
            for dd, idx in enumerate(((0, 1, 1, 5), (1, 0, 0, 4))):
                for a_, mi in enumerate(idx):
                    P.copy(mcat[dd][:, a_, :], mk[:, mi, :], eng="pool")
            s0t = P.sb("s0t", [128, 128], F32)
            sout = P.sb("sout", [128, 64], F32)
            gnm = P.sb("gnm", [128, 16], F32)
            gnv = P.sb("gnv", [128, 16], F32)
            cen = csv[:].rearrange("p (t f) -> p t f", t=8)
            sq2 = ex[:].rearrange("p (t f) -> p t f", t=8)
            o1 = kdir[:].rearrange("p (t f) -> p t f", t=8)
            P.memset(s0t[:], 0.0)

            class KV:
                def __init__(self, ap, key):
                    self.ap, self.key = ap, key

                def __getitem__(self, idx):
                    return (self.ap[idx], self.key)

            WLt1 = P.sb("WLt1", [128, 16], F32)
            DS = [dict(Rt=Rt, Kt=Kt, Kh=Kh, Bh=Bh, tokm=tokm, WLt=WLt),
                  dict(Rt=KV(wbuf[0][:, 0:1024], "Rt1"), Kt=KV(wbuf[0][:, 1024:2048], "Kt1"),
                       Kh=KV(wbuf[0][:, 2048:3072], "Kh1"), Bh=KV(wbuf[0][:, 3072:4096], "Bh1"),
                       tokm=KV(wbuf[1][:, 0:3072].rearrange("p (g c) -> p g c", g=8), "tokm1"), WLt=WLt1)]
            LWS = 0.6065306597126334
            PSq, PS6 = PS[5], PS[6]
            nbatch = [0]

            def prologue_gen(j):
                pp = j % 2
                kj = k_all[:, j, :]
                col = lambda i: cols[:, j, i:i + 1]
                for half in range(2):
                    hs = slice(half * 512, (half + 1) * 512)
                    P.mm(PS[4][:], l2[:, 2, j * 128:(j + 1) * 128], sg[:, hs])
                    P.copy(gj2[pp][:, hs], PS[4][:], eng="act")
                    yield
                P.dma(lnbc2[pp][:], lnbc_d[:, :, j * 128:(j + 1) * 128])
                P.ts(tmpf[:], kj, col(4), None, ALU.mult)
                yield
                for half in range(2):
                    hs = slice(half * 512, (half + 1) * 512)
                    P.act(sqb[:], tmpf[:, hs], AF.Square)
                    P.mm(PS[4][:], bd[:], sqb[:])
                    P.act(rinv[:, hs], PS[4][:], AF.Sqrt, bias=eps12[:])
                    yield
                P.gen("dve", "reciprocal", [rinv[:]], [rinv[:]])
                yield
                P.tt(kap[:], tmpf[:], rinv[:], ALU.mult)
                P.memset(ytok2[pp][:], 0.0, eng="pool")
                yield

            def derive_gen(j, d, p):
                S_ = DS[p]
                Rt_, Kt_, Kh_, Bh_, tokm_, WLt_ = S_["Rt"], S_["Kt"], S_["Kh"], S_["Bh"], S_["tokm"], S_["WLt"]
                pp = j % 2
                rj = r_all[:, j, :]
                kj = k_all[:, j, :]
                col = lambda i: cols[:, j, i:i + 1]
                dsl = slice(d * 64, (d + 1) * 64)
                for half in range(2):
                    hs = slice(half * 512, (half + 1) * 512)
                    P.mm(PS[4][:], l2[dsl, 0, j * 128:(j + 1) * 128], tw[dsl, hs])
                    P.act(lwv[:, hs], PS[4][:], AF.Sigmoid, bias=col(0 + d))
                    yield
                    P.mm(PS[4][:], l2[dsl, 1, j * 128:(j + 1) * 128], ta[dsl, hs])
                    P.act(av[:, hs], PS[4][:], AF.Sigmoid, bias=col(2 + d))
                    yield
                H2 = (slice(0, 512), slice(512, 1024))
                for hs in H2:
                    P.act(tmpf[:, hs], av[:, hs], AF.Identity, bias=col(6), scale=col(5))
                    yield
                for hs in H2:
                    P.tt(kdir[:, hs], tmpf[:, hs], kj[:, hs], ALU.mult)
                    yield
                for hs in H2:
                    P.tt(bvec[:, hs], kap[:, hs], av[:, hs], ALU.mult)
                    yield
                for hs in H2:
                    if d == 0:
                        P.copy(bsum2[pp][:, hs], kdir[:, hs], eng="act")
                    else:
                        P.tt(bsum2[pp][:, hs], bsum2[pp][:, hs], kdir[:, hs], ALU.add)
                    yield
                for hs in H2:
                    P.gen("dve", "tensor_tensor_scan", [csv[:, hs]], [rst[:, hs], lwv[:, hs]], 0.0, ALU.mult, ALU.add)
                    yield
                if d == 1:
                    for hs in H2:
                        c3 = csv[:, hs].rearrange("p (c t) -> p c t", t=64)
                        t3 = tmpf[:, hs].rearrange("p (c t) -> p c t", t=64)
                        P.tt(t3, c3[:, :, 63:64].to_broadcast([128, 8, 64]), c3, ALU.subtract)
                        yield
                        P.tt(csv[:, hs], tmpf[:, hs], lwv[:, hs], ALU.add)
                        yield
                for hs in H2:
                    P.act(ex[:, hs], csv[:, hs], AF.Exp, scale=-LWS)
                    yield
                for hs in H2:
                    P.tt(Rt_[:, hs], ex[:, hs], rj[:, hs], ALU.mult)
                    yield
                e3 = ex[:].rearrange("p (c t) -> p c t", t=64)
                P.copy(WLt_[:], e3[:, :, 63] if d == 0 else e3[:, :, 0])
                for hs in H2:
                    P.tt(tmpf[:, hs], csv[:, hs], lwv[:, hs], ALU.subtract)
                    yield
                for hs in H2:
                    P.act(ex[:, hs], tmpf[:, hs], AF.Exp, scale=-LWS)
                    yield
                for hs in H2:
                    P.tt(Kt_[:, hs], ex[:, hs], kap[:, hs], ALU.mult)
                    yield
                for hs in H2:
                    P.act(ex[:, hs], csv[:, hs], AF.Exp, scale=LWS)
                    yield
                for hs in H2:
                    P.tt(Kh_[:, hs], ex[:, hs], kdir[:, hs], ALU.mult)
                    yield
                for hs in H2:
                    P.tt(Bh_[:, hs], ex[:, hs], bvec[:, hs], ALU.mult)
                    yield
                for gI in range(8):
                    gs = slice(gI * 128, (gI + 1) * 128)
                    for ii_, src in enumerate((Kt_, Kh_, Bh_)):
                        P.mm(PS[4][:, ii_ * 128:(ii_ + 1) * 128], src[:, gs], identb[:])
                    P.copy(tokm_[:, gI, :], PS[4][:, 0:384], eng="act" if gI % 2 else "dve")
                    yield
                P.dma(s0t[0:64, 0:64], s0_d[d, j, 0:64, :])
                P.dma(s0t[64:128, 64:128], s0_d[d, j, 64:128, :])
                P.tr(PS[4][:, 0:128], s0t[:], identf[:])
                P.copy(Sbd[d][:], PS[4][:, 0:128], eng="act")
                yield

            def emit_batch(j, d, p, groups, tickers):
                S_ = DS[p]
                Rt_, Kt_, Kh_, Bh_ = S_["Rt"], S_["Kt"], S_["Kh"], S_["Bh"]
                mR_d = mk[:, 5, :] if d == 0 else mk[:, 4, :]
                mcat_d = mcat[d]
                bi = nbatch[0]
                nbatch[0] += 1

                def tick():
                    for g_ in list(tickers):
                        try:
                            next(g_)
                        except StopIteration:
                            tickers.remove(g_)
                X4 = X4s[bi % 2]
                PXh = [PS[0], PS[2]]
                SCh = [PS[1], PS[3]]
                info = {}
                for hf in range(2):
                    for cl in range(2):
                        c = hf * 2 + cl
                        gI, hh, s = groups[hf], cl, (bi % 2) * 4 + c
                        gs = slice(gI * 128, (gI + 1) * 128)
                        bs = slice(hh * 64, hh * 64 + 64)
                        info[c] = (Rt_[bs, gs], Kt_[bs, gs], Kh_[bs, gs], Bh_[bs, gs], bs,
                                   v_tok[:, gI, j * 128 + hh * 64:j * 128 + hh * 64 + 64], s)
                for cl in range(2):
                    for hf in range(2):
                        c = hf * 2 + cl
                        rt_, kt_, kh_, bh_, bs, vh, s = info[c]
                        pc = SCh[hf]
                        P.mm(pc[:, 0:128], kt_, bh_)
                        P.mm(pc[:, 128:256], bh_, kt_)
                        P.mm(pc[:, 256:384], kh_, kt_)
                        P.mm(pc[:, 384:512], kh_, rt_)
                        P.tt(Gm[s][:].rearrange("p a b -> p (a b)"), pc[:, 0:512],
                             mcat_d[:].rearrange("p a b -> p (a b)"), ALU.mult)
                        tick()
                for cl in range(2):
                    for hf in range(2):
                        c = hf * 2 + cl
                        rt_, kt_, kh_, bh_, bs, vh, s = info[c]
                        pc, PX = SCh[hf], PXh[hf]
                        P.mm(pc[:, 0:128], bh_, rt_)
                        P.mm(PX[:, cl * 128:cl * 128 + 64], Gm[s][:, 2, :], vh, start=(cl == 0), stop=True)
                        P.mm(PX[:, cl * 128 + 64:(cl + 1) * 128], kt_, identb[bs, bs], start=False, stop=True)
                        P.tt(Rb[s][:], pc[:, 0:128], mR_d, ALU.mult)
                        tick()
                for hf in range(2):
                    P.copy(X4[:, 2 * hf:2 * hf + 2, :].rearrange("p a b -> p (a b)"), PXh[hf][:, 0:256], eng="act")
                cur = {}
                for c in range(4):
                    s = info[c][6]
                    cur[c] = (Gm[s][:, 0, :], Gm[s][:, 1, :])
                for lev in range(6):
                    for hf in range(2):
                        pc, PX = SCh[hf], PXh[hf]
                        for cl in range(2):
                            c = hf * 2 + cl
                            P.mm(PX[:, cl * 128:(cl + 1) * 128], cur[c][1], X4[:, c, :], start=False, stop=True)
                        if lev < 5:
                            for cl in range(2):
                                c = hf * 2 + cl
                                P.mm(pc[:, cl * 128:(cl + 1) * 128], cur[c][1], cur[c][0])
                                P.mm(pc[:, 256 + cl * 128:256 + (cl + 1) * 128], cur[c][0], cur[c][1])
                    for _ in range(NHEAT):
                        P.tr(PSB[:, 0:128], identb[:], identb[:])
                    for hf in range(2):
                        P.copy(X4[:, 2 * hf:2 * hf + 2, :].rearrange("p a b -> p (a b)"), PXh[hf][:, 0:256], eng="act")
                        if lev < 5:
                            sq = SQ[hf][lev % 2]
                            P.copy(sq[:].rearrange("p a b c -> p (a b c)"), SCh[hf][:, 0:512],
                                   eng="dve" if hf == 0 else "act")
                            for cl in range(2):
                                cur[hf * 2 + cl] = (sq[:, 0, cl, :], sq[:, 1, cl, :])
                    tick()
                for cl in range(2):
                    for hf in range(2):
                        c = hf * 2 + cl
                        rt_, kt_, kh_, bh_, bs, vh, s = info[c]
                        pq = SCh[hf][bs, cl * 128:(cl + 1) * 128]
                        P.mm(pq, X4[:, c, 64:128], Rb[s][:])
                        P.stt(QT[(bi % 2) * 2 + hf][bs, :], pq, -1.0, rt_, ALU.mult, ALU.add)
                        tick()
                return bi

            def seq(j, d, p, bi, groups):
                S_ = DS[p]
                tokm_, WLt_ = S_["tokm"], S_["WLt"]
                ytok_ = ytok2[j % 2]
                Sb_, Sf_ = Sbd[d], Sfd[d]
                X4 = X4s[bi % 2]
                for gl, gI in enumerate(groups):
                    sl = [(bi % 2) * 4 + gl * 2 + hh for hh in range(2)]
                    cc = [gl * 2 + hh for hh in range(2)]
                    qt = QT[(bi % 2) * 2 + gl]
                    for hh in range(2):
                        vh = v_tok[:, gI, j * 128 + hh * 64:j * 128 + hh * 64 + 64]
                        P.mm(PSq[:, hh * 64:hh * 64 + 64], Gm[sl[hh]][:, 3, :], vh, start=True, stop=False)
                        P.mm(PSq[:, hh * 64:hh * 64 + 64], Rb[sl[hh]][:], X4[:, cc[hh], 0:64], start=False, stop=True)
                    yield
                    P.tt(ytok_[:, gI, :], PSq[:, 0:128], ytok_[:, gI, :], ALU.add)
                    for qi in range(2):
                        q = qi if d == 0 else 1 - qi
                        qs = slice(q * 64, (q + 1) * 64)
                        ch = gI * 2 + q
                        pb_ = ch % 2
                        for hh in range(2):
                            P.mm(PSq[hh * 64:hh * 64 + 64, 256:384], X4[qs, cc[hh], 64:128], tokm_[qs, gI, 256:384],
                                 start=True, stop=False)
                        P.mm(PSq[:, 256:384], negI[:], identb[:], start=False, stop=True)
                        for hh in range(2):
                            oc = slice(384 + hh * 64, 448 + hh * 64)
                            P.mm(PSq[:, oc], tokm_[qs, gI, 128:256],
                                 v_tok[qs, gI, j * 128 + hh * 64:j * 128 + hh * 64 + 64], start=True, stop=False)
                            P.mm(PSq[:, oc], tokm_[qs, gI, 256:384], X4[qs, cc[hh], 0:64], start=False, stop=True)
                        yield
                        P.tt(ATb[pb_][:], PSq[:, 256:384], mk[:, 6, :], ALU.mult)
                        P.stt(Cpp[pb_][:], PSq[:, 384:512], WLt_[:, ch:ch + 1], blkf[:], ALU.mult, ALU.mult)
                        yield
                        P.mm(PS6[:, 0:128], ATb[pb_][:], Sb_[:])
                        P.mm(PSq[qs, 128:256], qt[:, qs], Sb_[:])
                        yield
                        seg_end = (ch % 4 == 3) if d == 0 else (ch % 4 == 0)
                        if seg_end:
                            P.stt(Sf_[:], PS6[:, 0:128], WLt_[:, ch:ch + 1], Cpp[pb_][:], ALU.mult, ALU.add)
                        P.stt(Sb_[:], PS6[:, 0:128], WLt_[:, ch:ch + 1], Cpp[pb_][:], ALU.mult, ALU.add)
                        P.tt(ytok_[qs, gI, :], PSq[qs, 128:256], ytok_[qs, gI, :], ALU.add)
                        if seg_end:
                            P.tr(PS6[:, 128:256], Sf_[:], identf[:])
                            P.copy(sout[0:64, :], PS6[0:64, 128:192])
                            P.copy(sout[64:128, :], PS6[64:128, 192:256])
                            P.dma(nrw_o[d, ch // 4, j], sout[:])
                            P.ts(Sb_[:], Sb_[:], mcol[:], None, ALU.mult)
                        yield

            def epilogue_gen(j):
                pp = j % 2
                rj = r_all[:, j, :]
                col = lambda i: cols[:, j, i:i + 1]
                ytok, gj, lnbc, bsum = ytok2[pp], gj2[pp], lnbc2[pp], bsum2[pp]
                P.stt(tmpf[:], bsum[:], col(7), rj, ALU.mult, ALU.mult)
                yield
                P.copy(rkb[:], tmpf[:], eng="act")
                yield
                for t in range(8):
                    P.mm(PS[4][:, 2 * t:2 * t + 2], rkb[:, t * 128:(t + 1) * 128], E2[:])
                P.copy(bon[:], PS[4][:, 0:16].rearrange("p (t h) -> p t h", h=2))
                yield
                y3 = ytok[:].rearrange("p t (h v) -> p (t h) v", h=2)
                c3 = cen[:].rearrange("p t (h v) -> p (t h) v", h=2)
                s3 = sq2[:].rearrange("p t (h v) -> p (t h) v", h=2)
                P.gen("dve", "tensor_reduce", [gnm[:]], [y3], AX.X, ALU.add)
                yield
                P.ts(gnm[:], gnm[:], 1.0 / 64, None, ALU.mult)
                yield
                P.tt(c3, y3, gnm[:].rearrange("p (a o) -> p a o", o=1).to_broadcast([128, 16, 64]), ALU.subtract)
                yield
                P.tt(s3, c3, c3, ALU.mult)
                yield
                P.gen("dve", "tensor_reduce", [gnv[:]], [s3], AX.X, ALU.add)
                yield
                P.act(gnv[:], gnv[:], AF.Sqrt, bias=gneps[:], scale=1.0 / 64)
                yield
                P.gen("dve", "reciprocal", [gnv[:]], [gnv[:]])
                yield
                P.tt(c3, c3, gnv[:].rearrange("p (a o) -> p a o", o=1).to_broadcast([128, 16, 64]), ALU.mult)
                yield
                js = slice(j * 128, (j + 1) * 128)
                P.tt(cen[:], cen[:], lnbc[:, 0, :].rearrange("p (o f) -> p o f", o=1).to_broadcast([128, 8, 128]), ALU.mult)
                yield
                P.tt(cen[:], cen[:], lnbc[:, 1, :].rearrange("p (o f) -> p o f", o=1).to_broadcast([128, 8, 128]), ALU.add)
                yield
                P.tt(sq2[:].rearrange("p t (h v) -> p t h v", h=2), v_tok[:, :, js].rearrange("p t (h v) -> p t h v", h=2),
                     bon[:].rearrange("p t (h o) -> p t h o", o=1).to_broadcast([128, 8, 2, 64]), ALU.mult)
                yield
                P.tt(o1[:], cen[:], sq2[:], ALU.add)
                yield
                for hb in range(2):
                    for t4 in range(4):
                        P.mm(PS[4][:, t4 * 128:(t4 + 1) * 128], o1[:, hb * 4 + t4, :], identb[:])
                    P.tt(oT[:, j, hb * 512:(hb + 1) * 512], PS[4][:, 0:512], gj[:, hb * 512:(hb + 1) * 512], ALU.mult)
                    yield

            def chain_gens(*gs):
                for g_ in gs:
                    if g_ is not None:
                        yield from g_

            def drain(gs):
                for g_ in list(gs):
                    for _ in g_:
                        pass
                    gs.remove(g_)

            drain([chain_gens(prologue_gen(0), derive_gen(0, 0, 0))])
            tickers = []
            pending_epi = [None]
            for t in range(16):
                j, d, p = t // 2, t % 2, t % 2
                if t < 15:
                    j2, d2 = (t + 1) // 2, (t + 1) % 2
                    epi_, pending_epi[0] = pending_epi[0], None
                    step_chain = chain_gens(epi_, prologue_gen(j2) if d2 == 0 else None,
                                            derive_gen(j2, d2, (t + 1) % 2))
                    tickers.append(step_chain)
                else:
                    step_chain = None
                order = list(range(8)) if d == 0 else list(range(7, -1, -1))
                for b_ in range(4):
                    groups = order[2 * b_:2 * b_ + 2]
                    bi = emit_batch(j, d, p, groups, tickers)
                    tickers.append(seq(j, d, p, bi, groups))
                if step_chain is not None and step_chain in tickers:
                    for _ in step_chain:
                        pass
                    tickers.remove(step_chain)
                if d == 1:
                    drain(tickers)
                    pending_epi[0] = epilogue_gen(j)
            drain(tickers)
            drain([pending_epi[0]])
        for tl in range(2):
            wb = wload(wo_d[tl].rearrange("p k c -> p (k c)"), 4096)
            wv = wb[:, 0:4096].rearrange("p (k c) -> p k c", k=8)
            for jj in range(4):
                c = tl * 4 + jj
                for half in range(2):
                    hs = slice(half * 512, (half + 1) * 512)
                    pb = PS[(2 * jj + half) % 4]
                    for kk in range(8):
                        P.mm(pb[:], wv[:, kk, jj * 128:(jj + 1) * 128], oT[:, kk, hs], start=(kk == 0), stop=(kk == 7))
                    P.stt(xT[:, c, hs], pb[:], modcol(1, 2, c), xT[:, c, hs], ALU.mult, ALU.add)


def rope_tables():
    rows = TT // 64
    row = np.repeat(np.arange(rows, dtype=np.float32), 64)
    col = np.tile(np.arange(64, dtype=np.float32), rows)
    n_freq = 16
    inv_freq = (10000.0 ** (-np.arange(n_freq, dtype=np.float32) / n_freq)).astype(np.float32)
    ang = np.concatenate([row[:, None] * inv_freq, col[:, None] * inv_freq], axis=-1)
    return np.cos(ang).astype(np.float32), np.sin(ang).astype(np.float32)


def host_prepare(inp):
    f = lambda a: np.ascontiguousarray(a, dtype=np.float32)
    shared = {}
    mw = inp["mod_w"].reshape(2, 8, 128, 12, 512)
    shared["modw"] = f(mw.transpose(0, 3, 2, 1, 4))
    shared["modb"] = f(inp["mod_b"].reshape(2, 48, 128).transpose(0, 2, 1))
    ngs = np.stack([inp["norm_mix_g"][0], inp["norm_ffn_g"][0], inp["norm_mix_g"][1], inp["norm_ffn_g"][1],
                    inp["final_norm_g"]], 0)
    shared["normg"] = f(ngs.reshape(5, 8, 128).transpose(2, 0, 1))
    W = inp["ab_w_in"][0]
    perm64 = np.concatenate([np.arange(32, 64), np.arange(0, 32)])
    qcols = [np.arange(j * 128, (j + 1) * 128) for j in range(4)]
    qrcols = [np.concatenate([h * 64 + perm64 for h in (2 * j, 2 * j + 1)]) for j in range(4)]
    kA = 512 + np.arange(128)
    kB = 512 + np.concatenate([np.arange(64, 128), np.arange(0, 64)])
    kAr = 512 + np.concatenate([perm64, 64 + perm64])
    kBr = 512 + np.concatenate([64 + perm64, perm64])
    zc = 768 + np.arange(512)
    xbc = 1280 + np.arange(768)
    vc = 640 + np.arange(128)
    dtc = 2048 + np.arange(16)
    pad = np.zeros(112, dtype=np.int64)
    cols = np.concatenate([qcols[0], qrcols[0], qcols[1], qrcols[1], qcols[2], qrcols[2], qcols[3], qrcols[3],
                           kA, kAr, kB, kBr, zc, xbc, vc, dtc, pad])
    assert cols.shape[0] == 6 * 512
    Wc = W[:, cols]
    shared["win"] = f(Wc.reshape(8, 128, 6, 512).transpose(2, 1, 0, 3))
    shared["wout"] = f(inp["ab_w_out"][0].reshape(8, 128, 2, 512).transpose(2, 1, 0, 3))
    wg = inp["ffn_w_gate"].reshape(2, 8, 128, 11, 256)
    wu = inp["ffn_w_up"].reshape(2, 8, 128, 11, 256)
    wgu = np.concatenate([wg, wu], axis=-1)
    shared["wgu"] = f(wgu.transpose(0, 3, 2, 1, 4))
    wd = inp["ffn_w_down"].reshape(2, 22, 128, 4, 256)
    shared["wdn"] = f(wd.transpose(0, 3, 2, 1, 4))
    qgv, kgv = inp["attn_q_g"][0], inp["attn_k_g"][0]
    qgc = np.tile(qgv, 2)
    kgc = np.tile(kgv, 2)
    shared["qg"] = f(np.stack([qgc, np.tile(qgv[perm64], 2), kgc, np.tile(kgv[perm64], 2)], 1))
    cw = inp["ssd_conv_w"][0]
    cb = inp["ssd_conv_b"][0]
    cwb = np.concatenate([cw, cb[None]], 0)
    shared["convw"] = f(cwb.reshape(6, 6, 128).transpose(2, 1, 0))
    shared["dtb"] = f(np.broadcast_to(inp["ssd_dt_bias"][0].reshape(1, 16), (128, 16)))
    shared["alog"] = f(np.broadcast_to(inp["ssd_a_log"][0].reshape(1, 16), (128, 16)))
    dsk = inp["ssd_d"][0]
    shared["dsk"] = f(np.stack([np.concatenate([np.full(64, dsk[2 * j]), np.full(64, dsk[2 * j + 1])])
                                for j in range(4)], 1))
    shared["ssdg"] = f(inp["ssd_norm_g"][0].reshape(4, 128).T)
    ii = np.arange(128)
    tri_f = (ii[:, None] <= ii[None, :]).astype(np.float32)
    tri_b = (ii[:, None] >= ii[None, :]).astype(np.float32)
    mask_f = (ii[:, None] > ii[None, :]).astype(np.float32)
    mask_b = (ii[:, None] < ii[None, :]).astype(np.float32)
    shared["cst"] = f(np.stack([tri_f, tri_b, mask_f, mask_b, np.eye(128) * NEG * 3, np.ones((128, 128))], 1))
    sel = np.zeros((16, 16, 128), np.float32)
    for h in range(16):
        sel[h, h, :] = 1.0
    shared["sel"] = sel
    mu = inp["rwkv_mu"][0]
    shared["rw_mu"] = f(mu.reshape(2, 6, 8, 128).transpose(3, 0, 1, 2))
    lay = lambda w: f(w.reshape(8, 128, 2, 512).transpose(2, 1, 0, 3))
    shared["rw_wr"] = lay(inp["rwkv_w_r"][0])
    shared["rw_wk"] = lay(inp["rwkv_w_k"][0])
    shared["rw_wv"] = lay(inp["rwkv_w_v"][0])
    shared["rw_wo"] = lay(inp["rwkv_w_o"][0])
    w1, a1, g1 = inp["rwkv_w1"][0], inp["rwkv_a1"][0], inp["rwkv_g1"][0]
    l1 = np.concatenate([w1[0], w1[1], a1[0], a1[1], g1], axis=1)
    shared["rw_l1"] = f(l1.reshape(8, 128, 384).transpose(1, 0, 2))
    w2, a2, g2 = inp["rwkv_w2"][0], inp["rwkv_a2"][0], inp["rwkv_g2"][0]
    shared["rw_l2"] = f(np.stack([w2.reshape(128, 1024), a2.reshape(128, 1024), g2], 1))
    cl = np.stack([inp["rwkv_w0"][0, 0], inp["rwkv_w0"][0, 1], inp["rwkv_a0"][0, 0], inp["rwkv_a0"][0, 1],
                   inp["rwkv_k_k"][0], inp["rwkv_k_a"][0], np.zeros(1024, np.float32),
                   inp["rwkv_r_k"][0].reshape(1024)], 0)
    shared["rw_cols"] = f(cl.reshape(8, 8, 128).transpose(2, 1, 0))
    ln = np.stack([inp["rwkv_ln_g"][0], inp["rwkv_ln_b"][0]], 0)
    shared["rw_ln"] = f(np.broadcast_to(ln[None], (128, 2, 1024)))
    blk = (ii[:, None] // 64) == (ii[None, :] // 64)
    M1f = ((ii[:, None] > ii[None, :]) & blk).astype(np.float32)
    M2f = ((ii[:, None] >= ii[None, :]) & blk).astype(np.float32)
    shared["rw_mask"] = f(np.stack([M1f, M1f.T, M2f, M2f.T, blk.astype(np.float32), np.zeros((128, 128))], 1))
    cosT, sinT = rope_tables()
    per_core = []
    for core in range(NCORES):
        d = {}
        if core < 2:
            b = core
            d["x_tok"] = f(inp["x_sample"][b])
            d["cvec"] = f(inp["c"][b].reshape(8, 128).T)
            d["mcol"] = np.ones((128, 1), np.float32)
            d["mask01"] = np.ones((128, 40), np.float32)
            C = np.zeros((128, TT), np.float32)
            S = np.zeros((128, TT), np.float32)
            for p in range(128):
                dd = p % 64
                fq = dd % 32
                C[p] = cosT[:, fq]
                S[p] = -sinT[:, fq] if dd < 32 else sinT[:, fq]
            d["ropeC"], d["ropeS"] = C, S
            ck = inp["cache_attn_k"][b, 0].reshape(256, 128)
            d["ctxkA"] = f(ck.T)
            d["ctxkB"] = f(np.concatenate([ck[:, 64:], ck[:, :64]], 1).T)
            d["ctxv"] = f(inp["cache_attn_v"][b, 0].reshape(256, 128))
            h0 = np.stack([inp["state_ssd_fwd"][b, 0], inp["state_ssd_bwd"][b, 0]], 0)
            d["ssd_h0"] = f(h0.transpose(0, 2, 1, 3))
            s0 = np.stack([inp["state_rwkv_fwd"][b, 0], inp["state_rwkv_bwd"][b, 0]], 0)
            d["rw_s0"] = f(s0.reshape(2, 8, 128, 64))
        else:
            pc = min(core - 2, 3)
            seqs = inp["x_prompt"][pc * 4:(pc + 1) * 4]
            d["x_tok"] = f(seqs.reshape(TT, 1024))
            d["cvec"] = f(inp["c_ctx"].reshape(8, 128).T)
            d["mcol"] = np.zeros((128, 1), np.float32)
            m01 = np.zeros((128, 10, 4), np.float32)
            for kt in range(2, 10):
                m01[:, kt, (kt - 2) // 2] = 1.0
            d["mask01"] = m01.reshape(128, 40)
            d["ropeC"] = np.ones((128, TT), np.float32)
            d["ropeS"] = np.zeros((128, TT), np.float32)
            d["ctxkA"] = np.zeros((128, 256), np.float32)
            d["ctxkB"] = np.zeros((128, 256), np.float32)
            d["ctxv"] = np.zeros((256, 128), np.float32)
            d["ssd_h0"] = np.zeros((2, 64, 8, 64), np.float32)
            d["rw_s0"] = np.zeros((2, 8, 128, 64), np.float32)
        d.update(shared)
        per_core.append(d)
    return per_core


_CACHE = {}


def run(inputs, stage=STAGE):
    inp = {k: np.asarray(v) for k, v in inputs.items()}
    per_core = host_prepare(inp)
    if stage not in _CACHE:
        _CACHE[stage] = build(stage)
    nc, IN, OUT = _CACHE[stage]
    in_maps = [{k: d[k] for k in IN} for d in per_core]
    res = run_bass_kernel_spmd(nc, in_maps, core_ids=list(range(NCORES)))
    return res.results


def kernel(**inputs):
    r = run(inputs)
    return assemble(r)


def assemble(r):
    y_sample = np.stack([r[0]["y"], r[1]["y"]], 0)
    y_prompt = np.concatenate([r[2 + i]["y"].reshape(4, 256, 1024) for i in range(4)], 0)
    nk = np.concatenate([r[2 + i]["nk"].reshape(4, 256, 2, 64) for i in range(4)], 0)[:, None]
    nv = np.concatenate([r[2 + i]["nv"].reshape(4, 256, 2, 64) for i in range(4)], 0)[:, None]
    ssd = []
    for d in range(2):
        a = np.concatenate([r[2 + i]["nssd"][d] for i in range(4)], 0)
        ssd.append(np.ascontiguousarray(a.transpose(0, 2, 1, 3))[:, None])
    rw = []
    for d in range(2):
        a = np.concatenate([r[2 + i]["nrw"].reshape(2, 4, 16, 64, 64)[d] for i in range(4)], 0)
        rw.append(np.ascontiguousarray(a)[:, None])
    f = lambda a: np.ascontiguousarray(a, dtype=np.float32)
    return (f(y_prompt), f(y_sample), f(nk), f(nv), f(ssd[0]), f(ssd[1]), f(rw[0]), f(rw[1]))
```

```python
import numpy as np
from contextlib import ExitStack
import concourse.bass as bass
import concourse.mybir as mybir
from concourse.bass_utils import run_bass_kernel_spmd
import numpy as np
from contextlib import contextmanager, ExitStack
import concourse.bass as bass
import concourse.mybir as mybir

F32 = mybir.dt.float32
BF16 = mybir.dt.bfloat16
I32 = mybir.dt.int32
AF = mybir.ActivationFunctionType
ALU = mybir.AluOpType
AX = mybir.AxisListType

ENGS = ["pe", "act", "dve", "pool", "sp"]


class Prog:
    def __init__(self, nc, es, n_dsem=14):
        self.nc = nc
        self.es = es
        self.ops = {e: [] for e in ENGS}
        self.cnt = {e: 0 for e in ENGS}
        self.sem = {e: es.enter_context(nc.semaphore("c_" + e)) for e in ENGS if e != "sp"}
        self.dsem = [es.enter_context(nc.semaphore("d%d" % i)) for i in range(n_dsem)]
        self.dval = [0] * n_dsem
        self.dnext = 0
        self.seen = {e: {} for e in ENGS}
        self.lastw = {}
        self.readers = {}
        self.psum_names = set()
        self.n_inst = 0
        self.pending = {e: [] for e in ENGS}
        self.final_done = False

    def sb(self, name, shape, dtype=F32):
        self.n_sb = getattr(self, "n_sb", 0) + 1
        name = "%s_%d" % (name, self.n_sb)
        return self.es.enter_context(self.nc.sbuf_tensor(name, list(shape), dtype))

    def ps(self, name, shape=(128, 512), dtype=F32):
        t = self.es.enter_context(self.nc.psum_tensor(name, list(shape), dtype))
        self.psum_names.add(name)
        return t

    @staticmethod
    def _key(x):
        if x is None:
            return None
        if isinstance(x, tuple):
            return x[1]
        if isinstance(x, str):
            return x
        t = getattr(x, "tensor", x)
        return getattr(t, "name", None)

    @staticmethod
    def _ap(x):
        return x[0] if isinstance(x, tuple) else x

    def _deps(self, eng, reads, writes):
        deps = {}

        def add(tok):
            if tok is None:
                return
            src, val = tok
            if src == "pe" and eng == "pe":
                return
            if deps.get(src, 0) < val:
                deps[src] = val

        rk = [k for k in (self._key(r) for r in reads) if k is not None]
        wk = [k for k in (self._key(w) for w in writes) if k is not None]
        wk = [(k[0] if isinstance(k, tuple) and k[0] in self.psum_names else k) for k in wk]
        rk2 = []
        for k in rk:
            base = k[0] if isinstance(k, tuple) else k
            if base in self.psum_names:
                wk.append(base)
            else:
                rk2.append(k)
        rk = rk2
        for k in rk:
            add(self.lastw.get(k))
        for k in wk:
            add(self.lastw.get(k))
            for tok in self.readers.get(k, {}).values():
                add(tok)
        waits = list(self.pending[eng])
        self.pending[eng] = []
        for src, val in deps.items():
            if self.seen[eng].get(src, 0) >= val:
                continue
            self.seen[eng][src] = val
            waits.append((src, val))
        return waits, rk, wk

    def _commit(self, eng, tok, rk, wk):
        for k in wk:
            self.lastw[k] = tok
            self.readers[k] = {}
        for k in rk:
            self.readers.setdefault(k, {})[eng] = tok

    def op(self, eng, fn, reads=(), writes=()):
        waits, rk, wk = self._deps(eng, reads, writes)
        self.cnt[eng] += 1
        tok = (eng, self.cnt[eng])
        self.ops[eng].append((waits, fn, ("e", eng)))
        self._commit(eng, tok, rk, wk)
        self.n_inst += 1

    def dma(self, out, in_, reads=None, writes=None, q="sp"):
        reads = [in_] if reads is None else reads
        writes = [out] if writes is None else writes
        waits, rk, wk = self._deps(q, reads, writes)
        nsw = 4
        nhw = len(self.dsem) - nsw
        if q == "pool":
            self.dnext_sw = (getattr(self, "dnext_sw", -1) + 1) % nsw
            j = nhw + self.dnext_sw
        else:
            j = self.dnext
            self.dnext = (self.dnext + 1) % nhw
        src = ("d", j)
        if self.dval[j] > 0 and self.seen[q].get(src, 0) < self.dval[j]:
            self.seen[q][src] = self.dval[j]
            waits.append((src, self.dval[j]))
        self.dval[j] += 16
        tok = (src, self.dval[j])
        o, i = self._ap(out), self._ap(in_)
        self.ops[q].append((waits, lambda e: e.dma_start(out=o, in_=i), ("d", j)))
        self._commit(q, tok, rk, wk)
        self.n_inst += 1

    def mm(self, out, lhsT, rhs, start=True, stop=True, **kw):
        o, l, r = self._ap(out), self._ap(lhsT), self._ap(rhs)
        self.op("pe", lambda e: e.matmul(o, l, r, start=start, stop=stop, **kw),
                reads=[lhsT, rhs], writes=[out])

    def tr(self, out, in_, ident):
        o, i, d = self._ap(out), self._ap(in_), self._ap(ident)
        self.op("pe", lambda e: e.transpose(o, i, d), reads=[in_, ident], writes=[out])

    def act(self, out, in_, func, bias=None, scale=None, accum_out=None, extra_r=()):
        o, i = self._ap(out), self._ap(in_)
        kw = {}
        rd = [in_] + list(extra_r)
        if bias is not None:
            kw["bias"] = self._ap(bias)
            if not isinstance(bias, (int, float)):
                rd.append(bias)
        if scale is not None:
            kw["scale"] = self._ap(scale)
            if not isinstance(scale, (int, float)):
                rd.append(scale)
        wr = [out]
        if accum_out is not None:
            kw["accum_out"] = self._ap(accum_out)
            wr.append(accum_out)
        self.op("act", lambda e: e.activation(o, i, func, **kw), reads=rd, writes=wr)

    def gen(self, eng, name, outs, ins, *args, **kw):
        oa = [self._ap(x) for x in outs]
        ia = [self._ap(x) if not isinstance(x, (int, float)) else x for x in ins]
        rd = [x for x in ins if not isinstance(x, (int, float))]
        self.op(eng, lambda e: getattr(e, name)(*oa, *ia, *args, **kw), reads=rd, writes=list(outs))

    def tt(self, out, a, b, op, eng="dve"):
        self.gen(eng, "tensor_tensor", [out], [a, b], op)

    def ts(self, out, a, s1, s2, op0, op1=None, eng="dve"):
        o, aa = self._ap(out), self._ap(a)
        rd = [a]
        s1a, s2a = s1, s2
        if not isinstance(s1, (int, float)) and s1 is not None:
            rd.append(s1)
            s1a = self._ap(s1)
        if not isinstance(s2, (int, float)) and s2 is not None:
            rd.append(s2)
            s2a = self._ap(s2)
        if op1 is None:
            self.op(eng, lambda e: e.tensor_scalar(o, aa, s1a, None, op0), reads=rd, writes=[out])
        else:
            self.op(eng, lambda e: e.tensor_scalar(o, aa, s1a, s2a, op0, op1), reads=rd, writes=[out])

    def stt(self, out, a, s, b, op0, op1):
        o, aa, bb = self._ap(out), self._ap(a), self._ap(b)
        rd = [a, b]
        sa = s
        if not isinstance(s, (int, float)):
            rd.append(s)
            sa = self._ap(s)
        self.op("dve", lambda e: e.scalar_tensor_tensor(o, aa, sa, bb, op0, op1), reads=rd, writes=[out])

    def copy(self, out, in_, eng="dve"):
        if eng == "act":
            self.act(out, in_, AF.Copy)
        else:
            self.gen(eng, "tensor_copy", [out], [in_])

    def memset(self, out, val, eng="pool"):
        o = self._ap(out)
        self.op(eng, lambda e: e.memset(o, val), reads=[], writes=[out])

    @contextmanager
    def scope(self):
        old = self.es
        with ExitStack() as s_:
            self.es = s_
            yield
            self.barrier()
            self.emit(final=False)
        self.es = old

    def barrier(self):
        for e in ENGS:
            for f in ENGS:
                if f == "sp" or f == e:
                    continue
                v = self.cnt[f]
                if v > 0 and self.seen[e].get(f, 0) < v:
                    self.seen[e][f] = v
                    self.pending[e].append((f, v))
            for j, v in enumerate(self.dval):
                src = ("d", j)
                if v > 0 and self.seen[e].get(src, 0) < v:
                    self.seen[e][src] = v
                    self.pending[e].append((src, v))

    def emit(self, final=True):
        nc = self.nc
        fin = []
        if final:
            for j, v in enumerate(self.dval):
                if v > 0:
                    fin.append((("d", j), v))
        sems = self.sem
        dsem = self.dsem

        def run(eng_name, e):
            for waits, fn, kind in self.ops[eng_name]:
                for src, val in waits:
                    if isinstance(src, tuple):
                        e.wait_ge(dsem[src[1]], val)
                    else:
                        e.wait_ge(sems[src], val)
                ins = fn(e)
                if kind[0] == "d":
                    ins.then_inc(dsem[kind[1]], 16)
                else:
                    ins.then_inc(sems[kind[1]], 1)
            if eng_name == "sp":
                for src, val in fin:
                    e.wait_ge(dsem[src[1]], val)

        with nc.Block() as block:
            @block.sync
            def _(e):
                run("sp", e)

            @block.tensor
            def _(e):
                run("pe", e)

            @block.scalar
            def _(e):
                run("act", e)

            @block.vector
            def _(e):
                run("dve", e)

            @block.gpsimd
            def _(e):
                run("pool", e)
        self.ops = {e: [] for e in ENGS}


NCORES = 8
TT = 1024
NEG = -30000.0
FFN = 2816
NF = FFN // 128
STAGE = 6
NHEAT = 10
DBG = False
SSD_LOOP = True
DEBUG = {}


def build(stage=STAGE):
    nc = bass.Bass("TRN2", target_bir_lowering=False)
    IN = {}
    OUT = {}

    def I(name, shape, dt=F32):
        IN[name] = nc.dram_tensor(name, list(shape), dt, kind="ExternalInput").ap()
        return IN[name]

    def O(name, shape):
        OUT[name] = nc.dram_tensor(name, list(shape), F32, kind="ExternalOutput").ap()
        return OUT[name]

    x_tok = I("x_tok", [TT, 1024])
    cvec = I("cvec", [128, 8])
    mcol_d = I("mcol", [128, 1])
    mask01_d = I("mask01", [128, 40])
    ropeC_d = I("ropeC", [128, TT])
    ropeS_d = I("ropeS", [128, TT])
    ctxkA_d = I("ctxkA", [128, 256])
    ctxkB_d = I("ctxkB", [128, 256])
    ctxv_d = I("ctxv", [256, 128])
    ssd_h0_d = I("ssd_h0", [2, 64, 8, 64])
    modw_d = I("modw", [2, 12, 128, 8, 512])
    modb_d = I("modb", [2, 128, 48])
    ng_d = I("normg", [128, 5, 8])
    win_d = I("win", [6, 128, 8, 512])
    wout_d = I("wout", [2, 128, 8, 512])
    wgu_d = I("wgu", [2, 11, 128, 8, 512])
    wdn_d = I("wdn", [2, 4, 128, 22, 256])
    qg_d = I("qg", [128, 4])
    convw_d = I("convw", [128, 6, 6])
    dtb_d = I("dtb", [128, 16])
    alog_d = I("alog", [128, 16])
    dsk_d = I("dsk", [128, 4])
    ssdg_d = I("ssdg", [128, 4])
    cst_d = I("cst", [128, 6, 128])
    sel_d = I("sel", [16, 16, 128])

    y_o = O("y", [TT, 1024])
    nk_o = O("nk", [TT, 128])
    nv_o = O("nv", [TT, 128])
    nssd_o = O("nssd", [2, 4, 64, 8, 64])
    nrw_o = O("nrw", [128, 4096]) if stage < 6 else None

    with ExitStack() as es:
        P = Prog(nc, es)
        xT = P.sb("xT", [128, 8, TT], F32)
        wbuf = [P.sb("wbuf%d" % i, [128, 6144], BF16) for i in range(2)]
        identf = P.sb("identf", [128, 128], F32)
        identb = P.sb("identb", [128, 128], BF16)
        onesb = P.sb("onesb", [128, 128], BF16)
        modv = P.sb("modv", [128, 2, 48], F32)
        modb = P.sb("modb_s", [128, 2, 48], F32)
        ng = P.sb("ng", [128, 5, 8], F32)
        mcol = P.sb("mcol_s", [128, 1], F32)
        epsc = P.sb("epsc", [128, 1], F32)
        cv = P.sb("cv", [128, 8], F32)
        cvb = P.sb("cvb", [128, 8], BF16)
        PS = [P.ps("ps%d" % i) for i in range(7)]
        PSB = P.ps("psb", [128, 1024], BF16)
        wstate = {"i": 0}

        P.memset(identf[:], 1.0)
        P.op("pool", lambda e: e.affine_select(identf[:], identf[:], [[-1, 128]], ALU.is_equal, 0.0,
                                                 base=0, channel_multiplier=1), reads=[identf], writes=[identf])
        P.copy(identb[:], identf[:])
        P.memset(onesb[:], 1.0)
        P.memset(epsc[:], 1e-6)
        P.dma(modb[:], modb_d.rearrange("l p e -> p l e"))
        P.dma(ng[:], ng_d)
        P.dma(mcol[:], mcol_d)
        P.dma(cv[:], cvec)

        def wload(dram_tile, nelem):
            s = wstate["i"] % 2
            wstate["i"] += 1
            wb = wbuf[s]
            P.dma(wb[:, 0:nelem], dram_tile, q="pool")
            return wb

        P.act(cvb[:], cv[:], AF.Silu)

        def ada_load(l, tl):
            return wload(modw_d[l, tl].rearrange("p k c -> p (k c)"), 4096)

        def ada_mm(l, tl, wb, pm):
            wv = wb[:, 0:4096].rearrange("p (k c) -> p k c", k=8)
            for j in range(4):
                e = tl * 4 + j
                for k in range(8):
                    P.mm(pm[:, e:e + 1], wv[:, k, j * 128:(j + 1) * 128], cvb[:, k:k + 1],
                         start=(k == 0), stop=(k == 7))

        with P.scope():
            xs = [P.sb("xstage%d" % i, [128, 1024], F32) for i in range(2)]
            pend = ada_load(0, 0)
            for t in range(8):
                st = xs[t % 2]
                P.dma(st[:], x_tok[t * 128:(t + 1) * 128, :])
                for half in range(2):
                    pb = PS[(2 * t + half) % 4]
                    for j in range(4):
                        c = half * 4 + j
                        P.tr(pb[:, j * 128:(j + 1) * 128], st[:, c * 128:(c + 1) * 128], identf[:])
                    P.copy(xT[:, half * 4:half * 4 + 4, t * 128:(t + 1) * 128],
                           pb[:].rearrange("p (j t) -> p j t", j=4), eng="act" if half else "dve")
                nxt = ada_load(0, t + 1)
                ada_mm(0, t, pend, PS[4])
                pend = nxt
            for tl in range(8, 12):
                nxt = ada_load(0, tl + 1) if tl + 1 < 12 else None
                ada_mm(0, tl, pend, PS[4])
                pend = nxt
        P.tt(modv[:, 0, :], PS[4][:, 0:48], modb[:, 0, :], ALU.add)

        def modcol(l, i, c):
            return modv[:, l, i * 8 + c:i * 8 + c + 1]

        def rms_to_hT(hT, l, which, outf=None):
            with P.scope():
                sq = P.sb("sq", [128, 8, 512], BF16)
                rstd = P.sb("rstd", [128, TT], F32)
                G = P.sb("Gcol", [128, 8], F32)
                tmps = [P.sb("ntmp%d" % i, [128, TT], F32) for i in range(2)]
                if l < 2:
                    ni = 2 * l + which
                    sci = 1 + 3 * which
                    shi = 3 * which
                    P.stt(G[:], modv[:, l, sci * 8:sci * 8 + 8], 1.0, ng[:, ni, :], ALU.add, ALU.mult)
                else:
                    P.copy(G[:], ng[:, 4, :])
                for half in range(2):
                    hs = slice(half * 512, (half + 1) * 512)
                    for c in range(8):
                        if half == 0:
                            P.act(sq[:, c, :], xT[:, c, hs], AF.Square)
                        else:
                            P.tt(sq[:, c, :], xT[:, c, hs], xT[:, c, hs], ALU.mult)
                    for c in range(8):
                        P.mm(PS[half][:], onesb[:], sq[:, c, :], start=(c == 0), stop=(c == 7))
                for half in range(2):
                    hs = slice(half * 512, (half + 1) * 512)
                    P.act(rstd[:, hs], PS[half][:], AF.Sqrt, bias=epsc[:], scale=1.0 / 1024)
                    P.gen("dve", "reciprocal", [rstd[:, hs]], [rstd[:, hs]])
                for c in range(8):
                    tmp = tmps[c % 2]
                    if l < 2:
                        P.stt(tmp[:], xT[:, c, :], G[:, c:c + 1], rstd[:], ALU.mult, ALU.mult)
                        if outf is None:
                            P.act(hT[:, c, :], tmp[:], AF.Identity, bias=modcol(l, shi, c))
                        else:
                            P.act(outf(c), tmp[:].rearrange("p (s t) -> p s t", s=4), AF.Identity, bias=modcol(l, shi, c))
                    else:
                        P.stt(hT[:, c, :], xT[:, c, :], G[:, c:c + 1], rstd[:], ALU.mult, ALU.mult)

        def ffn(l, hT):
            with P.scope():
                hid = P.sb("hid", [128, NF, TT], BF16)
                gtmps = [P.sb("gtmp%d" % i, [128, 512], F32) for i in range(2)]
                for tl in range(11):
                    wb = wload(wgu_d[l, tl].rearrange("p k c -> p (k c)"), 4096)
                    wv = wb[:, 0:4096].rearrange("p (k c) -> p k c", k=8)
                    for j in range(2):
                        f = tl * 2 + j
                        for half in range(2):
                            pg = PS[(2 * half) % 4]
                            pu = PS[(2 * half + 1) % 4]
                            for k in range(8):
                                P.mm(pg[:], wv[:, k, j * 128:(j + 1) * 128], hT[:, k, half * 512:(half + 1) * 512],
                                     start=(k == 0), stop=(k == 7))
                            for k in range(8):
                                P.mm(pu[:], wv[:, k, 256 + j * 128:256 + (j + 1) * 128],
                                     hT[:, k, half * 512:(half + 1) * 512], start=(k == 0), stop=(k == 7))
                            gtmp = gtmps[half]
                            P.act(gtmp[:], pg[:], AF.Silu)
                            P.tt(hid[:, f, half * 512:(half + 1) * 512], gtmp[:], pu[:], ALU.mult)
                for tl in range(4):
                    wb = wload(wdn_d[l, tl].rearrange("p k c -> p (k c)"), 22 * 256)
                    wv = wb[:, 0:22 * 256].rearrange("p (k c) -> p k c", k=22)
                    for j in range(2):
                        c = tl * 2 + j
                        for half in range(2):
                            pb = PS[(2 * j + half) % 4]
                            for k in range(NF):
                                P.mm(pb[:], wv[:, k, j * 128:(j + 1) * 128], hid[:, k, half * 512:(half + 1) * 512],
                                     start=(k == 0), stop=(k == NF - 1))
                            xs_ = xT[:, c, half * 512:(half + 1) * 512]
                            P.stt(xs_, pb[:], modcol(l, 5, c), xs_, ALU.mult, ALU.add)

        def dbg(name, ap_sb, shape):
            DEBUG[name] = shape
            P.dma(O(name, shape), ap_sb)

        with P.scope():
            esL0 = P.es
            qf = P.sb("qf", [128, 4, TT], BF16)
            kall = [P.sb("kall%d" % i, [128, 1280], BF16) for i in range(2)]
            vall = P.sb("vall", [128, 10, 128], BF16)
            vsw = P.sb("vsw", [128, 10, 128], BF16)
            zs = P.sb("zs", [128, 4, TT], BF16)
            xpad = P.sb("xpad", [128, 6, 4, 260], BF16)
            dtraw = P.sb("dtraw", [128, 8, 16], F32)
            attnT = P.sb("attnT", [128, 4, TT], BF16)
            ynT = P.sb("ynT", [128, 4, TT], BF16)
            esP1 = ExitStack()
            P.es = esP1
            hT = P.sb("hT", [128, 8, TT], BF16)
            kf32 = P.sb("kf32", [128, TT], F32)
            vf32 = P.sb("vf32", [128, 8, 128], F32)
            ropeC = P.sb("ropeC_s", [128, TT], F32)
            ropeS = P.sb("ropeS_s", [128, TT], F32)
            qg = P.sb("qg_s", [128, 4], F32)
            tabs = [P.sb("ropetab%d" % i, [128, TT], F32) for i in range(4)]
            P.es = esL0
            rms_to_hT(hT, 0, 0)
            P.es = esP1
            P.dma(ropeC[:], ropeC_d)
            P.dma(ropeS[:], ropeS_d)
            P.dma(qg[:], qg_d)
            for i in range(4):
                P.ts(tabs[i][:], (ropeC if i % 2 == 0 else ropeS)[:], qg[:, i:i + 1], None, ALU.mult)
            kst = P.sb("kst", [128, 256], F32)
            P.dma(kst[:], ctxkA_d)
            P.copy(kall[0][:, 0:256], kst[:])
            kst2 = P.sb("kst2", [128, 256], F32)
            P.dma(kst2[:], ctxkB_d)
            P.copy(kall[1][:, 0:256], kst2[:])
            P.dma(vall[:, 0:2, :], ctxv_d.rearrange("(t p) c -> p t c", p=128), q="pool")
            cvv = ctxv_d.rearrange("(t s p) c -> s p t c", s=2, p=64)
            P.dma(vsw[0:64, 0:2, :], cvv[1], q="pool")
            P.dma(vsw[64:128, 0:2, :], cvv[0], q="pool")
            sqq = P.sb("sqq", [128, 512], BF16)
            rq = P.sb("rq", [128, 512], F32)
            t1 = P.sb("qt1", [128, 512], F32)
            t2 = P.sb("qt2", [128, 512], F32)
            bd = P.sb("bdones", [128, 128], BF16)
            P.memset(bd[:], 0.0)
            P.memset(bd[0:64, 0:64], 1.0)
            P.memset(bd[64:128, 64:128], 1.0)

            def qk_post(pq, pr, tabC, tabS, half, outs):
                hs = slice(half * 512, (half + 1) * 512)
                P.act(sqq[:], pq[:], AF.Square)
                P.tt(t1[:], pq[:], tabC[:, hs], ALU.mult)
                P.tt(t2[:], pr[:], tabS[:, hs], ALU.mult)
                P.mm(pr[:], bd[:], sqq[:])
                P.act(rq[:], pr[:], AF.Sqrt, bias=epsc[:], scale=1.0 / 64)
                P.gen("dve", "reciprocal", [rq[:]], [rq[:]])
                P.tt(t1[:], t1[:], t2[:], ALU.add)
                for o in outs:
                    P.tt(o, t1[:], rq[:], ALU.mult)

            def lin4(wv, j, half, pb):
                for k in range(8):
                    P.mm(pb[:], wv[:, k, j * 128:(j + 1) * 128], hT[:, k, half * 512:(half + 1) * 512],
                         start=(k == 0), stop=(k == 7))

            wvs = {}

            def get_wv(tl):
                if tl not in wvs:
                    wb = wload(win_d[tl].rearrange("p k c -> p (k c)"), 4096)
                    wvs[tl] = wb[:, 0:4096].rearrange("p (k c) -> p k c", k=8)
                return wvs[tl]

            items = [(tl, half, jj) for tl in range(3) for half in range(2) for jj in range(2)]

            def qk_lin(i):
                tl, half, jj = items[i]
                wv = get_wv(tl)
                lin4(wv, 2 * jj, half, PS[2 * jj])
                lin4(wv, 2 * jj + 1, half, PS[2 * jj + 1])

            def qk_fin(i):
                tl, half, jj = items[i]
                hs = slice(half * 512, (half + 1) * 512)
                pq, pr = PS[2 * jj], PS[2 * jj + 1]
                if tl < 2:
                    qk_post(pq, pr, tabs[0], tabs[1], half, [qf[:, tl * 2 + jj, hs]])
                else:
                    outs = [kall[jj][:, 256 + half * 512:256 + (half + 1) * 512]]
                    if jj == 0:
                        outs.append(kf32[:, hs])
                    qk_post(pq, pr, tabs[2], tabs[3], half, outs)

            qk_lin(0)
            for i in range(len(items)):
                if i + 1 < len(items):
                    qk_lin(i + 1)
                qk_fin(i)

            for tl in range(3, 6):
                wv = get_wv(tl)
                for half in range(2):
                    hs = slice(half * 512, (half + 1) * 512)
                    if tl < 3:
                        pass
                    elif tl == 3:
                        for j in range(4):
                            pb = PS[j]
                            lin4(wv, j, half, pb)
                            P.act(zs[:, j, hs], pb[:], AF.Silu)
                    elif tl == 4:
                        for j in range(4):
                            pb = PS[j]
                            lin4(wv, j, half, pb)
                            P.copy(xpad[:, j, 2 * half:2 * half + 2, 2:258], pb[:].rearrange("p (s t) -> p s t", s=2),
                                   eng="act" if j % 2 else "dve")
                    else:
                        for j in range(2):
                            pb = PS[j]
                            lin4(wv, j, half, pb)
                            P.copy(xpad[:, 4 + j, 2 * half:2 * half + 2, 2:258],
                                   pb[:].rearrange("p (s t) -> p s t", s=2), eng="act" if j % 2 else "dve")
                if tl == 5:
                    for t in range(8):
                        pb = PS[4 + t % 2]
                        for k in range(8):
                            P.mm(pb[:, 0:144], hT[:, k, t * 128:(t + 1) * 128], wv[:, k, 256:400],
                                 start=(k == 0), stop=(k == 7))
                        P.copy(vf32[:, t, :], pb[:, 0:128], eng="act")
                        P.copy(vall[:, 2 + t, :], pb[:, 0:128])
                        pb2 = PS[6]
                        for sh in range(2):
                            tsl = slice(t * 128 + (1 - sh) * 64, t * 128 + (1 - sh) * 64 + 64)
                            for k in range(8):
                                P.mm(pb2[sh * 64:sh * 64 + 64, 0:128], hT[:, k, tsl], wv[:, k, 256:384],
                                     start=(k == 0), stop=(k == 7))
                        P.copy(vsw[:, 2 + t, :], pb2[:, 0:128], eng="act")
                        P.copy(dtraw[:, t, :], pb[:, 128:144])
            ktok = P.sb("ktok", [128, 8, 128], F32)
            for t in range(8):
                pb = PS[t % 4]
                P.tr(pb[:, 0:128], kf32[:, t * 128:(t + 1) * 128], identf[:])
                P.copy(ktok[:, t, :], pb[:, 0:128], eng="act" if t % 2 else "dve")
            P.dma(nk_o.rearrange("(t p) c -> p t c", p=128), ktok[:])
            P.dma(nv_o.rearrange("(t p) c -> p t c", p=128), vf32[:])
            P.barrier()
            P.emit(final=False)
            esP1.close()
            P.es = esL0
            if stage <= 1:
                P.emit(final=True)
                return nc, IN, OUT
            with P.scope():
                m01 = P.sb("m01", [128, 40], F32)
                P.dma(m01[:], mask01_d)
                pT = [P.sb("pT%d" % i, [128, 512], BF16) for i in range(4)]
                rden = P.sb("rden", [128, 512], F32)
                bdo = P.sb("bdo", [128, 128], BF16)
                P.memset(bdo[:], 0.0)
                P.memset(bdo[0:64, 0:64], 1.0)
                P.memset(bdo[64:128, 64:128], 1.0)
                kbd = [P.sb("kbd%d" % g, [128, 10, 2, 128], BF16) for g in range(2)]
                vbd = [P.sb("vbd%d" % g, [128, 10, 2, 128], BF16) for g in range(2)]
                for g in range(2):
                    top, bot = (kall[0], kall[1]) if g == 0 else (kall[1], kall[0])
                    topv = top[:].rearrange("p (t s c) -> p t s c", s=2, c=64)
                    botv = bot[:].rearrange("p (t s c) -> p t s c", s=2, c=64)
                    P.memset(kbd[g][:].rearrange("p a b c -> p (a b c)"), 0.0, eng="pool")
                    P.memset(vbd[g][:].rearrange("p a b c -> p (a b c)"), 0.0, eng="pool")
                    P.copy(kbd[g][0:64, :, :, 0:64], topv[0:64])
                    P.copy(kbd[g][64:128, :, 0, 64:128], botv[64:128, :, 1, :], eng="act")
                    P.copy(kbd[g][64:128, :, 1, 64:128], botv[64:128, :, 0, :], eng="act")
                    gc = slice(g * 64, (g + 1) * 64)
                    P.copy(vbd[g][0:64, :, 0, 0:64], vall[0:64, :, gc])
                    P.copy(vbd[g][64:128, :, 0, 64:128], vall[64:128, :, gc], eng="act")
                    P.copy(vbd[g][0:64, :, 1, 0:64], vsw[0:64, :, gc])
                    P.copy(vbd[g][64:128, :, 1, 64:128], vsw[64:128, :, gc], eng="act")
                ada_next = [0]
                for j in range(4):
                    g = j // 2
                    for half in range(2):
                        hs = slice(half * 512, (half + 1) * 512)
                        po, pd = PS[4], PS[5]
                        it = j * 2 + half
                        n_ada = 2 if it < 4 else 1
                        wb_ada = []
                        for _ in range(n_ada):
                            if ada_next[0] < 12:
                                wb_ada.append((ada_next[0], ada_load(1, ada_next[0])))
                                ada_next[0] += 1

                        def pv(st):
                            kt, s = st // 2, st % 2
                            P.mm(po[:], vbd[g][:, kt, s, :], pT[st % 4][:], start=(st == 0), stop=(st == 19))
                            P.mm(pd[:], bdo[:], pT[st % 4][:], start=(st == 0), stop=(st == 19))
                        for st in range(20):
                            kt, s = st // 2, st % 2
                            pS = PS[st % 3]
                            P.mm(pS[:], kbd[g][:, kt, s, :], qf[:, j, hs])
                            P.act(pT[st % 4][:], pS[:], AF.Exp, scale=0.125)
                            for qb in range(2):
                                mi = kt * 4 + half * 2 + qb
                                cs_ = slice(qb * 256, (qb + 1) * 256)
                                P.ts(pT[st % 4][:, cs_], pT[st % 4][:, cs_], m01[:, mi:mi + 1], None, ALU.mult)
                            if st > 1:
                                pv(st - 2)
                        pv(18)
                        pv(19)
                        for (tl_, wb_) in wb_ada:
                            ada_mm(1, tl_, wb_, PS[6])
                        P.gen("dve", "reciprocal", [rden[:]], [pd[:]])
                        P.tt(attnT[:, j, hs], po[:], rden[:], ALU.mult)
                P.tt(modv[:, 1, :], PS[6][:, 0:48], modb[:, 1, :], ALU.add)
            if stage <= 2:
                P.emit(final=True)
                return nc, IN, OUT
            with P.scope():
                cst = P.sb("cst_s", [128, 6, 128], F32)
                cstb = P.sb("cstb", [128, 6, 128], BF16)
                sel = P.sb("sel_s", [16, 16, 128], F32)
                convw = P.sb("convw_s", [128, 6, 6], F32)
                dtb = P.sb("dtb_s", [128, 16], F32)
                Aneg = P.sb("Aneg", [128, 16], F32)
                dsk = P.sb("dsk_s", [128, 4], F32)
                ssdg = P.sb("ssdg_s", [128, 4], F32)
                P.dma(cst[:], cst_d)
                P.dma(sel[:], sel_d)
                P.dma(convw[:], convw_d)
                P.dma(dtb[:], dtb_d)
                P.dma(Aneg[:], alog_d)
                P.dma(dsk[:], dsk_d)
                P.dma(ssdg[:], ssdg_d)
                P.copy(cstb[:], cst[:])
                P.act(Aneg[:], Aneg[:], AF.Exp)
                P.ts(Aneg[:], Aneg[:], -1.0, None, ALU.mult)
                P.memset(xpad[:, :, 0, 0:2], 0.0, eng="dve")
                P.memset(xpad[:, :, 3, 258:260], 0.0, eng="dve")
                P.ts(xpad[:, :, 1:4, 0:2], xpad[:, :, 0:3, 256:258], mcol[:], None, ALU.mult)
                P.ts(xpad[:, :, 0:3, 258:260], xpad[:, :, 1:4, 2:4], mcol[:], None, ALU.mult)
                xc = P.sb("xc", [128, 6, TT], BF16)
                acc = P.sb("cacc", [128, 4, 256], F32)
                for c in range(6):
                    P.ts(acc[:], xpad[:, c, :, 0:256], convw[:, c, 0:1], convw[:, c, 5:6], ALU.mult, ALU.add)
                    for i in range(1, 5):
                        P.stt(acc[:], xpad[:, c, :, i:i + 256], convw[:, c, i:i + 1], acc[:], ALU.mult, ALU.add)
                    P.act(xc[:, c, :], acc[:].rearrange("p s t -> p (s t)"), AF.Silu)
                xtok = P.sb("xtok", [128, 8, 640], BF16)
                for t in range(8):
                    for jx in range(5):
                        P.tr(PSB[:, jx * 128:(jx + 1) * 128], xc[:, jx, t * 128:(t + 1) * 128], identb[:])
                    P.copy(xtok[:, t, :], PSB[:, 0:640], eng="act" if t % 2 else "dve")
                dtt = P.sb("dtt", [128, 8, 16], F32)
                atok = P.sb("atok", [128, 8, 16], F32)
                P.tt(dtt[:], dtraw[:], dtb[:].rearrange("p (o e) -> p o e", o=1).to_broadcast([128, 8, 16]), ALU.add)
                P.act(dtt[:], dtt[:], AF.Exp)
                P.act(dtt[:], dtt[:], AF.Ln, bias=1.0)
                P.tt(atok[:], dtt[:], Aneg[:].rearrange("p (o e) -> p o e", o=1).to_broadcast([128, 8, 16]), ALU.mult)
                if stage <= 2.5:
                    P.emit(final=True)
                    return nc, IN, OUT
                onesf = P.sb("onesf", [128, 128], F32)
                P.memset(onesf[:], 1.0)
                ysum = P.sb("ysum", [128, 4, TT], F32)
                hin = P.sb("hin", [128, 4, 64], F32)
                hinb = P.sb("hinb", [128, 4, 64], BF16)
                h0s = P.sb("h0s", [64, 8, 64], F32)
                hout = P.sb("hout", [64, 8, 64], F32)
                negcs2 = [P.sb("negcs%d" % i, [128, 8], F32) for i in range(2)]
                csT2 = [None, None]
                csH = [P.sb("csH%d" % i, [16, 128], BF16) for i in range(2)]
                csL = [P.sb("csL%d" % i, [16, 128], BF16) for i in range(2)]
                selb = P.sb("selb", [16, 16, 128], BF16)
                P.copy(selb[:], sel[:])
                etot2 = [P.sb("etot%d" % i, [128, 16], F32) for i in range(2)]
                de2 = [P.sb("de%d" % i, [128, 8], F32) for i in range(2)]
                GT2 = [P.sb("GT%d" % i, [128, 2, 128], F32) for i in range(2)]
                xdt2 = [P.sb("xdt%d" % i, [128, 8, 64], BF16) for i in range(2)]
                Bd2 = [P.sb("Bd%d" % i, [128, 8, 64], BF16) for i in range(2)]
                segs = [P.sb("seg%d" % i, [128, 8, 128], F32) for i in range(2)]
                scs = [P.sb("sc%d" % i, [128, 8, 128], BF16) for i in range(2)]
                eas = [P.sb("ea%d" % i, [128, 4, 128], F32) for i in range(2)]
                Css = [P.sb("Cs%d" % i, [128, 4, 128], BF16) for i in range(2)]
                htmp = P.sb("htmp", [128, 4, 64], F32)
                pSm, pG, pD, pE, pY, pSt, pO = PS[0], PS[1], PS[2], PS[3], PS[4], PS[5], PS[6]
                if not SSD_LOOP:
                    for j in range(4):
                        P.memset(ysum[:, j, :], 0.0, eng="dve")
                    P.memset(hout[:], 0.0, eng="dve")
                    for d in range(2):
                        for sidx in range(4):
                            P.dma(nssd_o[d, sidx], hout[:])
                for d in (range(2) if SSD_LOOP else []):
                    P.dma(h0s[:], ssd_h0_d[d])
                    for h in range(8):
                        bg = (h // 4) * 64
                        P.mm(pSt[bg:bg + 64, (h % 4) * 64:(h % 4 + 1) * 64], h0s[:, h, :], identf[0:64, 0:64])
                    P.copy(hin[:], pSt[:, 0:256].rearrange("p (s q) -> p s q", s=4))
                    P.copy(hinb[:], hin[:], eng="act")
                    tri = cst[:, d, :]
                    msk = cstb[:, 2 + d, :]
                    def chunk_pro(ci):
                        c = ci if d == 0 else 7 - ci
                        cl = slice(c * 128, (c + 1) * 128)
                        pz = ci % 2
                        negcs, csT, etot, de, GT, xdt, Bd = negcs2[pz], csT2[pz], etot2[pz], de2[pz], GT2[pz], xdt2[pz], Bd2[pz]
                        a_c = atok[:, c, d * 8:(d + 1) * 8]
                        P.mm(pSm[:, 0:8], tri, a_c)
                        P.mm(pSm[0:16, 128:256], atok[:, c, :], tri)
                        P.mm(pSm[:, 256:272], onesf[:], atok[:, c, :])
                        P.ts(negcs[:], pSm[:, 0:8], -1.0, None, ALU.mult)
                        P.copy(csH[pz][:], pSm[0:16, 128:256], eng="act")
                        P.tt(csL[pz][:], pSm[0:16, 128:256], csH[pz][:], ALU.subtract)
                        P.act(etot[:], pSm[:, 256:272], AF.Exp)
                        P.tt(de[:], pSm[:, 256 + d * 8:256 + d * 8 + 8], negcs[:], ALU.add)
                        P.act(de[:], de[:], AF.Exp)
                        P.mm(pG[:, 0:128], xc[0:64, 4, cl], xc[0:64, 5, cl])
                        P.copy(GT[:, 0, :], pG[:, 0:128], eng="act")
                        P.mm(pG[:, 128:256], xc[64:128, 4, cl], xc[64:128, 5, cl])
                        P.copy(GT[:, 1, :], pG[:, 128:256], eng="act")
                        P.tt(xdt[:], xtok[:, c, 0:512].rearrange("p (h q) -> p h q", h=8),
                             dtt[:, c, d * 8:(d + 1) * 8].rearrange("p (h o) -> p h o", o=1).to_broadcast([128, 8, 64]),
                             ALU.mult)
                        P.tt(Bd[:].rearrange("p (g f) n -> p g f n", g=2),
                             xtok[:, c, 512:640].rearrange("p (g o n) -> p g o n", g=2, o=1).to_broadcast([128, 2, 4, 64]),
                             de[:].rearrange("p (g f o) -> p g f o", g=2, o=1).to_broadcast([128, 2, 4, 64]), ALU.mult)

                    def stages_a(ci):
                        c = ci if d == 0 else 7 - ci
                        cl = slice(c * 128, (c + 1) * 128)
                        pz = ci % 2
                        negcs, csT, etot, de, GT, xdt, Bd = negcs2[pz], csT2[pz], etot2[pz], de2[pz], GT2[pz], xdt2[pz], Bd2[pz]
                        segA, scA, eaA, CsA = segs[ci % 2], scs[ci % 2], eas[ci % 2], Css[ci % 2]
                        for h in range(8):
                            e = d * 8 + h
                            pDh = (pD if h < 4 else pO)[:, (h % 4) * 128:(h % 4 + 1) * 128]
                            P.mm(pDh, selb[:, e, :], csH[pz][:], start=True, stop=False)
                            P.mm(pDh, selb[:, e, :], csL[pz][:], start=False, stop=False)
                            P.mm(pDh, cstb[:, 4, :], msk, start=False, stop=True)
                        for h in range(8):
                            pDh = (pD if h < 4 else pO)[:, (h % 4) * 128:(h % 4 + 1) * 128]
                            P.act(segA[:, h, :], pDh, AF.Exp, bias=negcs[:, h:h + 1])
                        for g in range(2):
                            P.tt(scA[:, 4 * g:4 * g + 4, :], segA[:, 4 * g:4 * g + 4, :],
                                 GT[:, g:g + 1, :].to_broadcast([128, 4, 128]), ALU.mult)
                        for h in range(8):
                            bg, e = (h // 4) * 64, d * 8 + h
                            pEh = pE[bg:bg + 64, (h % 4) * 128:(h % 4 + 1) * 128]
                            P.mm(pEh, selb[:, e, 0:64], csH[pz][:], start=True, stop=False)
                            P.mm(pEh, selb[:, e, 0:64], csL[pz][:], start=False, stop=True)
                        for g in range(2):
                            bgs = slice(g * 64, g * 64 + 64)
                            P.act(eaA[bgs].rearrange("p a b -> p (a b)"), pE[bgs, 0:512], AF.Exp)
                            P.tt(CsA[bgs], eaA[bgs],
                                 xc[bgs, 5, cl].rearrange("p (o l) -> p o l", o=1).to_broadcast([64, 4, 128]), ALU.mult)

                    def stage_b(ci):
                        c = ci if d == 0 else 7 - ci
                        cl = slice(c * 128, (c + 1) * 128)
                        pz = ci % 2
                        negcs, csT, etot, de, GT, xdt, Bd = negcs2[pz], csT2[pz], etot2[pz], de2[pz], GT2[pz], xdt2[pz], Bd2[pz]
                        segA, scA, eaA, CsA = segs[ci % 2], scs[ci % 2], eas[ci % 2], Css[ci % 2]
                        for h in range(8):
                            bg = (h // 4) * 64
                            b0 = (h % 2) * 64
                            pYh = pY[b0:b0 + 64, (h // 2) * 128:(h // 2 + 1) * 128]
                            P.mm(pYh, xdt[:, h, :], scA[:, h, :], start=True, stop=False)
                            P.mm(pYh, hinb[bg:bg + 64, h % 4, :], CsA[bg:bg + 64, h % 4, :], start=False, stop=True)
                            P.mm(pSt[bg:bg + 64, (h % 4) * 64:(h % 4 + 1) * 64], Bd[:, h, :], xdt[:, h, :])
                        pYv = pY[:].rearrange("p (j l) -> p j l", j=4)
                        if d == 0:
                            P.copy(ysum[:, :, cl], pYv)
                        else:
                            P.tt(ysum[:, :, cl], pYv, ysum[:, :, cl], ALU.add)
                        for hf in range(2):
                            ps_ = slice(hf * 64, (hf + 1) * 64)
                            P.tt(htmp[ps_], hin[ps_],
                                 etot[ps_, d * 8 + hf * 4:d * 8 + hf * 4 + 4].rearrange("p (s o) -> p s o", o=1).to_broadcast([64, 4, 64]),
                                 ALU.mult)
                        P.tt(hin[:], htmp[:], pSt[:, 0:256].rearrange("p (s q) -> p s q", s=4), ALU.add)
                        seg_end = (c % 2 == 1) if d == 0 else (c % 2 == 0)
                        if seg_end:
                            sidx = c // 2
                            for h in range(8):
                                bg = (h // 4) * 64
                                P.tr((pO if h < 4 else pG)[0:64, (h % 4) * 64:(h % 4 + 1) * 64], hin[bg:bg + 64, h % 4, :], identf[bg:bg + 64, bg:bg + 64])
                            P.copy(hout[:, 0:4, :], pO[0:64, 0:256].rearrange("p (h n) -> p h n", h=4), eng="act")
                            P.copy(hout[:, 4:8, :], pG[0:64, 0:256].rearrange("p (h n) -> p h n", h=4), eng="act")
                            P.dma(nssd_o[d, sidx], hout[:])
                            P.ts(hin[:], hin[:], mcol[:], None, ALU.mult)
                        P.copy(hinb[:], hin[:], eng="act")

                    chunk_pro(0)
                    stages_a(0)
                    for ci in range(8):
                        if ci + 1 < 8:
                            chunk_pro(ci + 1)
                            stages_a(ci + 1)
                        stage_b(ci)
                sq4 = xtok[:, 0:4, 0:512]
                rs4 = segs[0][:].rearrange("p a b -> p (a b)")
                for j in range(4):
                    P.stt(ysum[:, j, :], xc[:, j, :], dsk[:, j:j + 1], ysum[:, j, :], ALU.mult, ALU.add)
                    P.tt(ysum[:, j, :], ysum[:, j, :], zs[:, j, :], ALU.mult)
                for half in range(2):
                    hs = slice(half * 512, (half + 1) * 512)
                    for j in range(4):
                        P.act(sq4[:, j, :], ysum[:, j, hs], AF.Square)
                    for j in range(4):
                        P.mm(PS[half][:], onesb[:], sq4[:, j, :], start=(j == 0), stop=(j == 3))
                    P.act(rs4[:, hs], PS[half][:], AF.Sqrt, bias=epsc[:], scale=1.0 / 512)
                P.gen("dve", "reciprocal", [rs4[:]], [rs4[:]])
                for j in range(4):
                    P.stt(ynT[:, j, :], ysum[:, j, :], ssdg[:, j:j + 1], rs4[:], ALU.mult, ALU.mult)
            if stage <= 3:
                P.emit(final=True)
                return nc, IN, OUT
            for tl in range(2):
                wb = wload(wout_d[tl].rearrange("p k c -> p (k c)"), 4096)
                wv = wb[:, 0:4096].rearrange("p (k c) -> p k c", k=8)
                for j in range(4):
                    c = tl * 4 + j
                    for half in range(2):
                        hs = slice(half * 512, (half + 1) * 512)
                        pb = PS[(2 * j + half) % 4]
                        for kk in range(8):
                            rhs = attnT[:, kk, hs] if kk < 4 else ynT[:, kk - 4, hs]
                            P.mm(pb[:], wv[:, kk, j * 128:(j + 1) * 128], rhs, start=(kk == 0), stop=(kk == 7))
                        P.stt(xT[:, c, hs], pb[:], modcol(0, 2, c), xT[:, c, hs], ALU.mult, ALU.add)
        if DBG:
            P.dma(O("dbg_a", [128, 8, TT]), xT[:])
        with P.scope():
            hT2 = P.sb("hT2", [128, 8, TT], BF16)
            rms_to_hT(hT2, 0, 1)
            ffn(0, hT2)
        if DBG:
            P.dma(O("dbg_b", [128, 8, TT]), xT[:])
        if stage >= 6:
            rwkv_layer(P, nc, IN, OUT, xT, PS, PSB, modcol, rms_to_hT, wload, identf, identb, onesb, epsc, mcol, I, O, wbuf)
        if DBG:
            P.dma(O("dbg_c", [128, 8, TT]), xT[:])
        with P.scope():
            hT3 = P.sb("hT3", [128, 8, TT], BF16)
            rms_to_hT(hT3, 1, 1)
            ffn(1, hT3)
        with P.scope():
            if stage < 6:
                zt = P.sb("zt", [128, 4096], F32)
                P.memset(zt[:], 0.0)
                P.dma(nrw_o, zt[:])
            yT = P.sb("yT", [128, 8, TT], F32)
            rms_to_hT(yT, 2, 0)
            ost = [P.sb("ostage%d" % i, [128, 1024], F32) for i in range(2)]
            for t in range(8):
                st = ost[t % 2]
                for half in range(2):
                    pb = PS[(2 * t + half) % 4]
                    for j in range(4):
                        P.tr(pb[:, j * 128:(j + 1) * 128], yT[:, half * 4 + j, t * 128:(t + 1) * 128], identf[:])
                    P.copy(st[:, half * 512:(half + 1) * 512], pb[:], eng="act" if half else "dve")
                P.dma(y_o[t * 128:(t + 1) * 128, :], st[:])
        P.emit(final=True)
    return nc, IN, OUT


def rwkv_layer(P, nc, IN, OUT, xT, PS, PSB, modcol, rms_to_hT, wload, identf, identb, onesb, epsc, mcol, I, O, wbuf):
    mu_d = I("rw_mu", [128, 2, 6, 8])
    wr_d = I("rw_wr", [2, 128, 8, 512])
    wk_d = I("rw_wk", [2, 128, 8, 512])
    wv_d = I("rw_wv", [2, 128, 8, 512])
    wo_d = I("rw_wo", [2, 128, 8, 512])
    l1_d = I("rw_l1", [128, 8, 384])
    l2_d = I("rw_l2", [128, 3, 1024])
    cols_d = I("rw_cols", [128, 8, 8])
    lnbc_d = I("rw_ln", [128, 2, 1024])
    s0_d = I("rw_s0", [2, 8, 128, 64])
    rmask_d = I("rw_mask", [128, 6, 128])
    nrw_o = O("nrw", [2, 4, 8, 128, 64])

    with P.scope():
        mu = P.sb("mu", [128, 2, 6, 8], F32)
        c0 = P.sb("c0", [128, 6, 8], F32)
        cols = P.sb("rwcols", [128, 8, 8], F32)
        l2 = P.sb("l2", [128, 3, 1024], BF16)
        mk = P.sb("rmk", [128, 8, 128], BF16)
        blkf = P.sb("blkf", [128, 128], F32)
        bd = P.sb("bd2", [128, 128], BF16)
        E2 = P.sb("E2", [128, 2], BF16)
        rst = P.sb("rst", [128, 1024], BF16)
        eps12 = P.sb("eps12", [128, 1], F32)
        gneps = P.sb("gneps", [128, 1], F32)
        P.dma(mu[:], mu_d)
        P.dma(cols[:], cols_d)
        P.dma(l2[:], l2_d, q="pool")
        P.memset(rst[:], 1.0)
        P.memset(rst[:].rearrange("p (c t) -> p c t", t=64)[:, :, 0:1], 0.0)
        P.memset(eps12[:], 1e-12)
        P.memset(gneps[:], 64e-5)
        P.tt(c0[:], mu[:, 0, :, :], mu[:, 1, :, :], ALU.add)
        P.ts(c0[:], c0[:], -1.0, 1.0, ALU.mult, ALU.add)
        P.ts(cols[:, :, 6:7], cols[:, :, 5:6], -1.0, 1.0, ALU.mult, ALU.add)

        r_all = P.sb("r_all", [128, 8, TT], BF16)
        k_all = P.sb("k_all", [128, 8, TT], BF16)
        v_tok = P.sb("v_tok", [128, 8, 1024], BF16)
        tw = P.sb("tw", [128, TT], BF16)
        ta = P.sb("ta", [128, TT], BF16)
        sg = P.sb("sg", [128, TT], BF16)
        oT = r_all
        with P.scope():
            hpad = P.sb("hpad", [128, 8, 4, 258], BF16)
            l1 = P.sb("l1", [128, 8, 384], BF16)
            P.dma(l1[:], l1_d, q="pool")
            rmf = P.sb("rmf", [128, 6, 128], F32)
            P.dma(rmf[:], rmask_d)
            P.ts(mk[:, 0, :], rmf[:, 0, :], -1.0, None, ALU.mult)
            P.ts(mk[:, 1, :], rmf[:, 1, :], -1.0, None, ALU.mult)
            P.copy(mk[:, 2, :], rmf[:, 0, :])
            P.copy(mk[:, 3, :], rmf[:, 1, :])
            P.copy(mk[:, 4, :], rmf[:, 2, :])
            P.copy(mk[:, 5, :], rmf[:, 3, :])
            P.ts(mk[:, 6, :], rmf[:, 4, :], -1.0, None, ALU.mult)
            P.copy(mk[:, 7, :], rmf[:, 4, :])
            P.copy(blkf[:], rmf[:, 4, :])
            P.copy(bd[:], rmf[:, 4, :])
            P.copy(E2[:, 0:1], rmf[:, 4, 0:1])
            P.copy(E2[:, 1:2], rmf[:, 4, 64:65])
            xms = [P.sb("xm%d" % i, [128, 8, TT], BF16) for i in range(2)]
            rms_to_hT(None, 1, 0, outf=lambda c: hpad[:, c, :, 1:257])
            P.memset(hpad[:, :, 0, 0:1], 0.0, eng="dve")
            P.memset(hpad[:, :, 3, 257:258], 0.0, eng="dve")
            P.ts(hpad[:, :, 1:4, 0:1], hpad[:, :, 0:3, 256:257], mcol[:], None, ALU.mult)
            P.ts(hpad[:, :, 0:3, 257:258], hpad[:, :, 1:4, 1:2], mcol[:], None, ALU.mult)

            def mix(i):
                xm = xms[i % 2]
                for c in range(8):
                    xv_ = xm[:, c, :].rearrange("p (s t) -> p s t", s=4)
                    P.act(xv_, hpad[:, c, :, 1:257], AF.Copy, scale=c0[:, i, c:c + 1])
                    P.stt(xv_, hpad[:, c, :, 0:256], mu[:, 0, i, c:c + 1], xv_, ALU.mult, ALU.add)
                    P.stt(xv_, hpad[:, c, :, 2:258], mu[:, 1, i, c:c + 1], xv_, ALU.mult, ALU.add)
                return xm

            def proj_fm(xm, w_d, dst):
                for tl in range(2):
                    wb = wload(w_d[tl].rearrange("p k c -> p (k c)"), 4096)
                    wv = wb[:, 0:4096].rearrange("p (k c) -> p k c", k=8)
                    for jj in range(4):
                        c = tl * 4 + jj
                        for half in range(2):
                            hs = slice(half * 512, (half + 1) * 512)
                            pb = PS[(2 * jj + half) % 4]
                            for kk in range(8):
                                P.mm(pb[:], wv[:, kk, jj * 128:(jj + 1) * 128], xm[:, kk, hs], start=(kk == 0), stop=(kk == 7))
                            P.copy(dst[:, c, hs], pb[:], eng="act" if half else "dve")

            def lora1(xm, cs_, func, dst):
                for half in range(2):
                    hs = slice(half * 512, (half + 1) * 512)
                    pb = PS[4 + half]
                    for kk in range(8):
                        P.mm(pb[:], l1[:, kk, cs_], xm[:, kk, hs], start=(kk == 0), stop=(kk == 7))
                    P.act(dst[:, hs], pb[:], func)

            def vproj(xm):
                for tl in range(2):
                    wb = wload(wv_d[tl].rearrange("p k c -> p (k c)"), 4096)
                    wv = wb[:, 0:4096].rearrange("p (k c) -> p k c", k=8)
                    for t in range(8):
                        pb = PS[t % 4]
                        for kk in range(8):
                            P.mm(pb[:], xm[:, kk, t * 128:(t + 1) * 128], wv[:, kk, :], start=(kk == 0), stop=(kk == 7))
                        P.copy(v_tok[:, t, tl * 512:(tl + 1) * 512], pb[:], eng="act" if t % 2 else "dve")

            consumers = [lambda xm: proj_fm(xm, wr_d, r_all),
                         lambda xm: lora1(xm, slice(0, 128), AF.Tanh, tw),
                         lambda xm: proj_fm(xm, wk_d, k_all),
                         vproj,
                         lambda xm: lora1(xm, slice(128, 256), AF.Copy, ta),
                         lambda xm: lora1(xm, slice(256, 384), AF.Sigmoid, sg)]
            xmv = {0: mix(0)}
            for i in range(6):
                if i + 1 < 6:
                    xmv[i + 1] = mix(i + 1)
                consumers[i](xmv[i])

        S_f = [[None] * 8, [None] * 8]
        with P.scope():
            Sfd = [P.sb("Sf%d" % i, [128, 128], F32) for i in range(2)]
            Sbd = [P.sb("Sb%d" % i, [128, 128], BF16) for i in range(2)]
            f32t = lambda n: P.sb(n, [128, TT], F32)
            b16t = lambda n: P.sb(n, [128, TT], BF16)
            lwv, csv, ex, tmpf = [f32t(n) for n in ("lwv", "csv", "ex", "tmpf")]
            kkr, av, kdir, bvec = [b16t(n) for n in ("kkr", "av", "kdir", "bvec")]
            bsum2 = [b16t("bsum%d" % i) for i in range(2)]
            rinv, kap = ex, kkr
            lnbc2 = [P.sb("lnbc%d" % i, [128, 2, 128], F32) for i in range(2)]
            sqb = P.sb("sqb", [128, 512], BF16)
            gj2 = [P.sb("gj%d" % i, [128, TT], BF16) for i in range(2)]
            rkb = P.sb("rkb", [128, TT], BF16)
            bon = P.sb("bon", [128, 8, 2], F32)
            Rt = P.sb("Rt", [128, TT], BF16)
            Kt = P.sb("Kt", [128, TT], BF16)
            Kh = P.sb("Kh", [128, TT], BF16)
            Bh = P.sb("Bh", [128, TT], BF16)
            WLt = P.sb("WLt", [128, 16], F32)
            tokm = P.sb("tokm", [128, 8, 384], BF16)
            ytok2 = [P.sb("ytok%d" % i, [128, 8, 128], F32) for i in range(2)]
            Gm = [P.sb("Gm%d" % i, [128, 4, 128], BF16) for i in range(8)]
            Rb = [P.sb("Rb%d" % i, [128, 128], BF16) for i in range(8)]
            X4s = [P.sb("X4_%d" % i, [128, 4, 128], BF16) for i in range(2)]
            SQ = [[P.sb("SQ%d_%d" % (h_, i), [128, 2, 2, 128], BF16) for i in range(2)] for h_ in range(2)]
            QT = [P.sb("QT%d" % i, [128, 128], BF16) for i in range(4)]
            ATb = [P.sb("ATb%d" % i, [128, 128], BF16) for i in range(2)]
            negI = P.sb("negI", [128, 128], BF16)
            P.ts(negI[:], identf[:], -1.0, None, ALU.mult)
            Cpp = [P.sb("Cpp%d" % i, [128, 128], F32) for i in range(2)]
            mcat = [P.sb("mcat%d" % i, [128, 4, 128], BF16) for i in range(2)]
            for dd, idx in enumerate(((0, 1, 1, 5), (1, 0, 0, 4))):
                for a_, mi in enumerate(idx):
                    P.copy(mcat[dd][:, a_, :], mk[:, mi, :], eng="pool")
            s0t = P.sb("s0t", [128, 128], F32)
            sout = P.sb("sout", [128, 64], F32)
            gnm = P.sb("gnm", [128, 16], F32)
            gnv = P.sb("gnv", [128, 16], F32)
            cen = csv[:].rearrange("p (t f) -> p t f", t=8)
            sq2 = ex[:].rearrange("p (t f) -> p t f", t=8)
            o1 = kdir[:].rearrange("p (t f) -> p t f", t=8)
            P.memset(s0t[:], 0.0)

            class KV:
                def __init__(self, ap, key):
                    self.ap, self.key = ap, key

                def __getitem__(self, idx):
                    return (self.ap[idx], self.key)

            WLt1 = P.sb("WLt1", [128, 16], F32)
            DS = [dict(Rt=Rt, Kt=Kt, Kh=Kh, Bh=Bh, tokm=tokm, WLt=WLt),
                  dict(Rt=KV(wbuf[0][:, 0:1024], "Rt1"), Kt=KV(wbuf[0][:, 1024:2048], "Kt1"),
                       Kh=KV(wbuf[0][:, 2048:3072], "Kh1"), Bh=KV(wbuf[0][:, 3072:4096], "Bh1"),
                       tokm=KV(wbuf[1][:, 0:3072].rearrange("p (g c) -> p g c", g=8), "tokm1"), WLt=WLt1)]
            LWS = 0.6065306597126334
            PSq, PS6 = PS[5], PS[6]
            nbatch = [0]

            def prologue_gen(j):
                pp = j % 2
                kj = k_all[:, j, :]
                col = lambda i: cols[:, j, i:i + 1]
                for half in range(2):
                    hs = slice(half * 512, (half + 1) * 512)
                    P.mm(PS[4][:], l2[:, 2, j * 128:(j + 1) * 128], sg[:, hs])
                    P.copy(gj2[pp][:, hs], PS[4][:], eng="act")
                    yield
                P.dma(lnbc2[pp][:], lnbc_d[:, :, j * 128:(j + 1) * 128])
                P.ts(tmpf[:], kj, col(4), None, ALU.mult)
                yield
                for half in range(2):
                    hs = slice(half * 512, (half + 1) * 512)
                    P.act(sqb[:], tmpf[:, hs], AF.Square)
                    P.mm(PS[4][:], bd[:], sqb[:])
                    P.act(rinv[:, hs], PS[4][:], AF.Sqrt, bias=eps12[:])
                    yield
                P.gen("dve", "reciprocal", [rinv[:]], [rinv[:]])
                yield
                P.tt(kap[:], tmpf[:], rinv[:], ALU.mult)
                P.memset(ytok2[pp][:], 0.0, eng="pool")
                yield

            def derive_gen(j, d, p):
                S_ = DS[p]
                Rt_, Kt_, Kh_, Bh_, tokm_, WLt_ = S_["Rt"], S_["Kt"], S_["Kh"], S_["Bh"], S_["tokm"], S_["WLt"]
                pp = j % 2
                rj = r_all[:, j, :]
                kj = k_all[:, j, :]
                col = lambda i: cols[:, j, i:i + 1]
                dsl = slice(d * 64, (d + 1) * 64)
                for half in range(2):
                    hs = slice(half * 512, (half + 1) * 512)
                    P.mm(PS[4][:], l2[dsl, 0, j * 128:(j + 1) * 128], tw[dsl, hs])
                    P.act(lwv[:, hs], PS[4][:], AF.Sigmoid, bias=col(0 + d))
                    yield
                    P.mm(PS[4][:], l2[dsl, 1, j * 128:(j + 1) * 128], ta[dsl, hs])
                    P.act(av[:, hs], PS[4][:], AF.Sigmoid, bias=col(2 + d))
                    yield
                H2 = (slice(0, 512), slice(512, 1024))
                for hs in H2:
                    P.act(tmpf[:, hs], av[:, hs], AF.Identity, bias=col(6), scale=col(5))
                    yield
                for hs in H2:
                    P.tt(kdir[:, hs], tmpf[:, hs], kj[:, hs], ALU.mult)
                    yield
                for hs in H2:
                    P.tt(bvec[:, hs], kap[:, hs], av[:, hs], ALU.mult)
                    yield
                for hs in H2:
                    if d == 0:
                        P.copy(bsum2[pp][:, hs], kdir[:, hs], eng="act")
                    else:
                        P.tt(bsum2[pp][:, hs], bsum2[pp][:, hs], kdir[:, hs], ALU.add)
                    yield
                for hs in H2:
                    P.gen("dve", "tensor_tensor_scan", [csv[:, hs]], [rst[:, hs], lwv[:, hs]], 0.0, ALU.mult, ALU.add)
                    yield
                if d == 1:
                    for hs in H2:
                        c3 = csv[:, hs].rearrange("p (c t) -> p c t", t=64)
                        t3 = tmpf[:, hs].rearrange("p (c t) -> p c t", t=64)
                        P.tt(t3, c3[:, :, 63:64].to_broadcast([128, 8, 64]), c3, ALU.subtract)
                        yield
                        P.tt(csv[:, hs], tmpf[:, hs], lwv[:, hs], ALU.add)
                        yield
                for hs in H2:
                    P.act(ex[:, hs], csv[:, hs], AF.Exp, scale=-LWS)
                    yield
                for hs in H2:
                    P.tt(Rt_[:, hs], ex[:, hs], rj[:, hs], ALU.mult)
                    yield
                e3 = ex[:].rearrange("p (c t) -> p c t", t=64)
                P.copy(WLt_[:], e3[:, :, 63] if d == 0 else e3[:, :, 0])
                for hs in H2:
                    P.tt(tmpf[:, hs], csv[:, hs], lwv[:, hs], ALU.subtract)
                    yield
                for hs in H2:
                    P.act(ex[:, hs], tmpf[:, hs], AF.Exp, scale=-LWS)
                    yield
                for hs in H2:
                    P.tt(Kt_[:, hs], ex[:, hs], kap[:, hs], ALU.mult)
                    yield
                for hs in H2:
                    P.act(ex[:, hs], csv[:, hs], AF.Exp, scale=LWS)
                    yield
                for hs in H2:
                    P.tt(Kh_[:, hs], ex[:, hs], kdir[:, hs], ALU.mult)
                    yield
                for hs in H2:
                    P.tt(Bh_[:, hs], ex[:, hs], bvec[:, hs], ALU.mult)
                    yield
                for gI in range(8):
                    gs = slice(gI * 128, (gI + 1) * 128)
                    for ii_, src in enumerate((Kt_, Kh_, Bh_)):
                        P.mm(PS[4][:, ii_ * 128:(ii_ + 1) * 128], src[:, gs], identb[:])
                    P.copy(tokm_[:, gI, :], PS[4][:, 0:384], eng="act" if gI % 2 else "dve")
                    yield
                P.dma(s0t[0:64, 0:64], s0_d[d, j, 0:64, :])
                P.dma(s0t[64:128, 64:128], s0_d[d, j, 64:128, :])
                P.tr(PS[4][:, 0:128], s0t[:], identf[:])
                P.copy(Sbd[d][:], PS[4][:, 0:128], eng="act")
                yield

            def emit_batch(j, d, p, groups, tickers):
                S_ = DS[p]
                Rt_, Kt_, Kh_, Bh_ = S_["Rt"], S_["Kt"], S_["Kh"], S_["Bh"]
                mR_d = mk[:, 5, :] if d == 0 else mk[:, 4, :]
                mcat_d = mcat[d]
                bi = nbatch[0]
                nbatch[0] += 1

                def tick():
                    for g_ in list(tickers):
                        try:
                            next(g_)
                        except StopIteration:
                            tickers.remove(g_)
                X4 = X4s[bi % 2]
                PXh = [PS[0], PS[2]]
                SCh = [PS[1], PS[3]]
                info = {}
                for hf in range(2):
                    for cl in range(2):
                        c = hf * 2 + cl
                        gI, hh, s = groups[hf], cl, (bi % 2) * 4 + c
                        gs = slice(gI * 128, (gI + 1) * 128)
                        bs = slice(hh * 64, hh * 64 + 64)
                        info[c] = (Rt_[bs, gs], Kt_[bs, gs], Kh_[bs, gs], Bh_[bs, gs], bs,
                                   v_tok[:, gI, j * 128 + hh * 64:j * 128 + hh * 64 + 64], s)
                for cl in range(2):
                    for hf in range(2):
                        c = hf * 2 + cl
                        rt_, kt_, kh_, bh_, bs, vh, s = info[c]
                        pc = SCh[hf]
                        P.mm(pc[:, 0:128], kt_, bh_)
                        P.mm(pc[:, 128:256], bh_, kt_)
                        P.mm(pc[:, 256:384], kh_, kt_)
                        P.mm(pc[:, 384:512], kh_, rt_)
                        P.tt(Gm[s][:].rearrange("p a b -> p (a b)"), pc[:, 0:512],
                             mcat_d[:].rearrange("p a b -> p (a b)"), ALU.mult)
                        tick()
                for cl in range(2):
                    for hf in range(2):
                        c = hf * 2 + cl
                        rt_, kt_, kh_, bh_, bs, vh, s = info[c]
                        pc, PX = SCh[hf], PXh[hf]
                        P.mm(pc[:, 0:128], bh_, rt_)
                        P.mm(PX[:, cl * 128:cl * 128 + 64], Gm[s][:, 2, :], vh, start=(cl == 0), stop=True)
                        P.mm(PX[:, cl * 128 + 64:(cl + 1) * 128], kt_, identb[bs, bs], start=False, stop=True)
                        P.tt(Rb[s][:], pc[:, 0:128], mR_d, ALU.mult)
                        tick()
                for hf in range(2):
                    P.copy(X4[:, 2 * hf:2 * hf + 2, :].rearrange("p a b -> p (a b)"), PXh[hf][:, 0:256], eng="act")
                cur = {}
                for c in range(4):
                    s = info[c][6]
                    cur[c] = (Gm[s][:, 0, :], Gm[s][:, 1, :])
                for lev in range(6):
                    for hf in range(2):
                        pc, PX = SCh[hf], PXh[hf]
                        for cl in range(2):
                            c = hf * 2 + cl
                            P.mm(PX[:, cl * 128:(cl + 1) * 128], cur[c][1], X4[:, c, :], start=False, stop=True)
                        if lev < 5:
                            for cl in range(2):
                                c = hf * 2 + cl
                                P.mm(pc[:, cl * 128:(cl + 1) * 128], cur[c][1], cur[c][0])
                                P.mm(pc[:, 256 + cl * 128:256 + (cl + 1) * 128], cur[c][0], cur[c][1])
                    for _ in range(NHEAT):
                        P.tr(PSB[:, 0:128], identb[:], identb[:])
                    for hf in range(2):
                        P.copy(X4[:, 2 * hf:2 * hf + 2, :].rearrange("p a b -> p (a b)"), PXh[hf][:, 0:256], eng="act")
                        if lev < 5:
                            sq = SQ[hf][lev % 2]
                            P.copy(sq[:].rearrange("p a b c -> p (a b c)"), SCh[hf][:, 0:512],
                                   eng="dve" if hf == 0 else "act")
                            for cl in range(2):
                                cur[hf * 2 + cl] = (sq[:, 0, cl, :], sq[:, 1, cl, :])
                    tick()
                for cl in range(2):
                    for hf in range(2):
                        c = hf * 2 + cl
                        rt_, kt_, kh_, bh_, bs, vh, s = info[c]
                        pq = SCh[hf][bs, cl * 128:(cl + 1) * 128]
                        P.mm(pq, X4[:, c, 64:128], Rb[s][:])
                        P.stt(QT[(bi % 2) * 2 + hf][bs, :], pq, -1.0, rt_, ALU.mult, ALU.add)
                        tick()
                return bi

            def seq(j, d, p, bi, groups):
                S_ = DS[p]
                tokm_, WLt_ = S_["tokm"], S_["WLt"]
                ytok_ = ytok2[j % 2]
                Sb_, Sf_ = Sbd[d], Sfd[d]
                X4 = X4s[bi % 2]
                for gl, gI in enumerate(groups):
                    sl = [(bi % 2) * 4 + gl * 2 + hh for hh in range(2)]
                    cc = [gl * 2 + hh for hh in range(2)]
                    qt = QT[(bi % 2) * 2 + gl]
                    for hh in range(2):
                        vh = v_tok[:, gI, j * 128 + hh * 64:j * 128 + hh * 64 + 64]
                        P.mm(PSq[:, hh * 64:hh * 64 + 64], Gm[sl[hh]][:, 3, :], vh, start=True, stop=False)
                        P.mm(PSq[:, hh * 64:hh * 64 + 64], Rb[sl[hh]][:], X4[:, cc[hh], 0:64], start=False, stop=True)
                    yield
                    P.tt(ytok_[:, gI, :], PSq[:, 0:128], ytok_[:, gI, :], ALU.add)
                    for qi in range(2):
                        q = qi if d == 0 else 1 - qi
                        qs = slice(q * 64, (q + 1) * 64)
                        ch = gI * 2 + q
                        pb_ = ch % 2
                        for hh in range(2):
                            P.mm(PSq[hh * 64:hh * 64 + 64, 256:384], X4[qs, cc[hh], 64:128], tokm_[qs, gI, 256:384],
                                 start=True, stop=False)
                        P.mm(PSq[:, 256:384], negI[:], identb[:], start=False, stop=True)
                        for hh in range(2):
                            oc = slice(384 + hh * 64, 448 + hh * 64)
                            P.mm(PSq[:, oc], tokm_[qs, gI, 128:256],
                                 v_tok[qs, gI, j * 128 + hh * 64:j * 128 + hh * 64 + 64], start=True, stop=False)
                            P.mm(PSq[:, oc], tokm_[qs, gI, 256:384], X4[qs, cc[hh], 0:64], start=False, stop=True)
                        yield
                        P.tt(ATb[pb_][:], PSq[:, 256:384], mk[:, 6, :], ALU.mult)
                        P.stt(Cpp[pb_][:], PSq[:, 384:512], WLt_[:, ch:ch + 1], blkf[:], ALU.mult, ALU.mult)
                        yield
                        P.mm(PS6[:, 0:128], ATb[pb_][:], Sb_[:])
                        P.mm(PSq[qs, 128:256], qt[:, qs], Sb_[:])
                        yield
                        seg_end = (ch % 4 == 3) if d == 0 else (ch % 4 == 0)
                        if seg_end:
                            P.stt(Sf_[:], PS6[:, 0:128], WLt_[:, ch:ch + 1], Cpp[pb_][:], ALU.mult, ALU.add)
                        P.stt(Sb_[:], PS6[:, 0:128], WLt_[:, ch:ch + 1], Cpp[pb_][:], ALU.mult, ALU.add)
                        P.tt(ytok_[qs, gI, :], PSq[qs, 128:256], ytok_[qs, gI, :], ALU.add)
                        if seg_end:
                            P.tr(PS6[:, 128:256], Sf_[:], identf[:])
                            P.copy(sout[0:64, :], PS6[0:64, 128:192])
                            P.copy(sout[64:128, :], PS6[64:128, 192:256])
                            P.dma(nrw_o[d, ch // 4, j], sout[:])
                            P.ts(Sb_[:], Sb_[:], mcol[:], None, ALU.mult)
                        yield

            def epilogue_gen(j):
                pp = j % 2
                rj = r_all[:, j, :]
                col = lambda i: cols[:, j, i:i + 1]
                ytok, gj, lnbc, bsum = ytok2[pp], gj2[pp], lnbc2[pp], bsum2[pp]
                P.stt(tmpf[:], bsum[:], col(7), rj, ALU.mult, ALU.mult)
                yield
                P.copy(rkb[:], tmpf[:], eng="act")
                yield
                for t in range(8):
                    P.mm(PS[4][:, 2 * t:2 * t + 2], rkb[:, t * 128:(t + 1) * 128], E2[:])
                P.copy(bon[:], PS[4][:, 0:16].rearrange("p (t h) -> p t h", h=2))
                yield
                y3 = ytok[:].rearrange("p t (h v) -> p (t h) v", h=2)
                c3 = cen[:].rearrange("p t (h v) -> p (t h) v", h=2)
                s3 = sq2[:].rearrange("p t (h v) -> p (t h) v", h=2)
                P.gen("dve", "tensor_reduce", [gnm[:]], [y3], AX.X, ALU.add)
                yield
                P.ts(gnm[:], gnm[:], 1.0 / 64, None, ALU.mult)
                yield
                P.tt(c3, y3, gnm[:].rearrange("p (a o) -> p a o", o=1).to_broadcast([128, 16, 64]), ALU.subtract)
                yield
                P.tt(s3, c3, c3, ALU.mult)
                yield
                P.gen("dve", "tensor_reduce", [gnv[:]], [s3], AX.X, ALU.add)
                yield
                P.act(gnv[:], gnv[:], AF.Sqrt, bias=gneps[:], scale=1.0 / 64)
                yield
                P.gen("dve", "reciprocal", [gnv[:]], [gnv[:]])
                yield
                P.tt(c3, c3, gnv[:].rearrange("p (a o) -> p a o", o=1).to_broadcast([128, 16, 64]), ALU.mult)
                yield
                js = slice(j * 128, (j + 1) * 128)
                P.tt(cen[:], cen[:], lnbc[:, 0, :].rearrange("p (o f) -> p o f", o=1).to_broadcast([128, 8, 128]), ALU.mult)
                yield
                P.tt(cen[:], cen[:], lnbc[:, 1, :].rearrange("p (o f) -> p o f", o=1).to_broadcast([128, 8, 128]), ALU.add)
                yield
                P.tt(sq2[:].rearrange("p t (h v) -> p t h v", h=2), v_tok[:, :, js].rearrange("p t (h v) -> p t h v", h=2),
                     bon[:].rearrange("p t (h o) -> p t h o", o=1).to_broadcast([128, 8, 2, 64]), ALU.mult)
                yield
                P.tt(o1[:], cen[:], sq2[:], ALU.add)
                yield
                for hb in range(2):
                    for t4 in range(4):
                        P.mm(PS[4][:, t4 * 128:(t4 + 1) * 128], o1[:, hb * 4 + t4, :], identb[:])
                    P.tt(oT[:, j, hb * 512:(hb + 1) * 512], PS[4][:, 0:512], gj[:, hb * 512:(hb + 1) * 512], ALU.mult)
                    yield

            def chain_gens(*gs):
                for g_ in gs:
                    if g_ is not None:
                        yield from g_

            def drain(gs):
                for g_ in list(gs):
                    for _ in g_:
                        pass
                    gs.remove(g_)

            drain([chain_gens(prologue_gen(0), derive_gen(0, 0, 0))])
            tickers = []
            pending_epi = [None]
            for t in range(16):
                j, d, p = t // 2, t % 2, t % 2
                if t < 15:
                    j2, d2 = (t + 1) // 2, (t + 1) % 2
                    epi_, pending_epi[0] = pending_epi[0], None
                    step_chain = chain_gens(epi_, prologue_gen(j2) if d2 == 0 else None,
                                            derive_gen(j2, d2, (t + 1) % 2))
                    tickers.append(step_chain)
                else:
                    step_chain = None
                order = list(range(8)) if d == 0 else list(range(7, -1, -1))
                for b_ in range(4):
                    groups = order[2 * b_:2 * b_ + 2]
                    bi = emit_batch(j, d, p, groups, tickers)
                    tickers.append(seq(j, d, p, bi, groups))
                if step_chain is not None and step_chain in tickers:
                    for _ in step_chain:
                        pass
                    tickers.remove(step_chain)
                if d == 1:
                    drain(tickers)
                    pending_epi[0] = epilogue_gen(j)
            drain(tickers)
            drain([pending_epi[0]])
        for tl in range(2):
            wb = wload(wo_d[tl].rearrange("p k c -> p (k c)"), 4096)
            wv = wb[:, 0:4096].rearrange("p (k c) -> p k c", k=8)
            for jj in range(4):
                c = tl * 4 + jj
                for half in range(2):
                    hs = slice(half * 512, (half + 1) * 512)
                    pb = PS[(2 * jj + half) % 4]
                    for kk in range(8):
                        P.mm(pb[:], wv[:, kk, jj * 128:(jj + 1) * 128], oT[:, kk, hs], start=(kk == 0), stop=(kk == 7))
                    P.stt(xT[:, c, hs], pb[:], modcol(1, 2, c), xT[:, c, hs], ALU.mult, ALU.add)


def rope_tables():
    rows = TT // 64
    row = np.repeat(np.arange(rows, dtype=np.float32), 64)
    col = np.tile(np.arange(64, dtype=np.float32), rows)
    n_freq = 16
    inv_freq = (10000.0 ** (-np.arange(n_freq, dtype=np.float32) / n_freq)).astype(np.float32)
    ang = np.concatenate([row[:, None] * inv_freq, col[:, None] * inv_freq], axis=-1)
    return np.cos(ang).astype(np.float32), np.sin(ang).astype(np.float32)


def host_prepare(inp):
    f = lambda a: np.ascontiguousarray(a, dtype=np.float32)
    shared = {}
    mw = inp["mod_w"].reshape(2, 8, 128, 12, 512)
    shared["modw"] = f(mw.transpose(0, 3, 2, 1, 4))
    shared["modb"] = f(inp["mod_b"].reshape(2, 48, 128).transpose(0, 2, 1))
    ngs = np.stack([inp["norm_mix_g"][0], inp["norm_ffn_g"][0], inp["norm_mix_g"][1], inp["norm_ffn_g"][1],
                    inp["final_norm_g"]], 0)
    shared["normg"] = f(ngs.reshape(5, 8, 128).transpose(2, 0, 1))
    W = inp["ab_w_in"][0]
    perm64 = np.concatenate([np.arange(32, 64), np.arange(0, 32)])
    qcols = [np.arange(j * 128, (j + 1) * 128) for j in range(4)]
    qrcols = [np.concatenate([h * 64 + perm64 for h in (2 * j, 2 * j + 1)]) for j in range(4)]
    kA = 512 + np.arange(128)
    kB = 512 + np.concatenate([np.arange(64, 128), np.arange(0, 64)])
    kAr = 512 + np.concatenate([perm64, 64 + perm64])
    kBr = 512 + np.concatenate([64 + perm64, perm64])
    zc = 768 + np.arange(512)
    xbc = 1280 + np.arange(768)
    vc = 640 + np.arange(128)
    dtc = 2048 + np.arange(16)
    pad = np.zeros(112, dtype=np.int64)
    cols = np.concatenate([qcols[0], qrcols[0], qcols[1], qrcols[1], qcols[2], qrcols[2], qcols[3], qrcols[3],
                           kA, kAr, kB, kBr, zc, xbc, vc, dtc, pad])
    assert cols.shape[0] == 6 * 512
    Wc = W[:, cols]
    shared["win"] = f(Wc.reshape(8, 128, 6, 512).transpose(2, 1, 0, 3))
    shared["wout"] = f(inp["ab_w_out"][0].reshape(8, 128, 2, 512).transpose(2, 1, 0, 3))
    wg = inp["ffn_w_gate"].reshape(2, 8, 128, 11, 256)
    wu = inp["ffn_w_up"].reshape(2, 8, 128, 11, 256)
    wgu = np.concatenate([wg, wu], axis=-1)
    shared["wgu"] = f(wgu.transpose(0, 3, 2, 1, 4))
    wd = inp["ffn_w_down"].reshape(2, 22, 128, 4, 256)
    shared["wdn"] = f(wd.transpose(0, 3, 2, 1, 4))
    qgv, kgv = inp["attn_q_g"][0], inp["attn_k_g"][0]
    qgc = np.tile(qgv, 2)
    kgc = np.tile(kgv, 2)
    shared["qg"] = f(np.stack([qgc, np.tile(qgv[perm64], 2), kgc, np.tile(kgv[perm64], 2)], 1))
    cw = inp["ssd_conv_w"][0]
    cb = inp["ssd_conv_b"][0]
    cwb = np.concatenate([cw, cb[None]], 0)
    shared["convw"] = f(cwb.reshape(6, 6, 128).transpose(2, 1, 0))
    shared["dtb"] = f(np.broadcast_to(inp["ssd_dt_bias"][0].reshape(1, 16), (128, 16)))
    shared["alog"] = f(np.broadcast_to(inp["ssd_a_log"][0].reshape(1, 16), (128, 16)))
    dsk = inp["ssd_d"][0]
    shared["dsk"] = f(np.stack([np.concatenate([np.full(64, dsk[2 * j]), np.full(64, dsk[2 * j + 1])])
                                for j in range(4)], 1))
    shared["ssdg"] = f(inp["ssd_norm_g"][0].reshape(4, 128).T)
    ii = np.arange(128)
    tri_f = (ii[:, None] <= ii[None, :]).astype(np.float32)
    tri_b = (ii[:, None] >= ii[None, :]).astype(np.float32)
    mask_f = (ii[:, None] > ii[None, :]).astype(np.float32)
    mask_b = (ii[:, None] < ii[None, :]).astype(np.float32)
    shared["cst"] = f(np.stack([tri_f, tri_b, mask_f, mask_b, np.eye(128) * NEG * 3, np.ones((128, 128))], 1))
    sel = np.zeros((16, 16, 128), np.float32)
    for h in range(16):
        sel[h, h, :] = 1.0
    shared["sel"] = sel
    mu = inp["rwkv_mu"][0]
    shared["rw_mu"] = f(mu.reshape(2, 6, 8, 128).transpose(3, 0, 1, 2))
    lay = lambda w: f(w.reshape(8, 128, 2, 512).transpose(2, 1, 0, 3))
    shared["rw_wr"] = lay(inp["rwkv_w_r"][0])
    shared["rw_wk"] = lay(inp["rwkv_w_k"][0])
    shared["rw_wv"] = lay(inp["rwkv_w_v"][0])
    shared["rw_wo"] = lay(inp["rwkv_w_o"][0])
    w1, a1, g1 = inp["rwkv_w1"][0], inp["rwkv_a1"][0], inp["rwkv_g1"][0]
    l1 = np.concatenate([w1[0], w1[1], a1[0], a1[1], g1], axis=1)
    shared["rw_l1"] = f(l1.reshape(8, 128, 384).transpose(1, 0, 2))
    w2, a2, g2 = inp["rwkv_w2"][0], inp["rwkv_a2"][0], inp["rwkv_g2"][0]
    shared["rw_l2"] = f(np.stack([w2.reshape(128, 1024), a2.reshape(128, 1024), g2], 1))
    cl = np.stack([inp["rwkv_w0"][0, 0], inp["rwkv_w0"][0, 1], inp["rwkv_a0"][0, 0], inp["rwkv_a0"][0, 1],
                   inp["rwkv_k_k"][0], inp["rwkv_k_a"][0], np.zeros(1024, np.float32),
                   inp["rwkv_r_k"][0].reshape(1024)], 0)
    shared["rw_cols"] = f(cl.reshape(8, 8, 128).transpose(2, 1, 0))
    ln = np.stack([inp["rwkv_ln_g"][0], inp["rwkv_ln_b"][0]], 0)
    shared["rw_ln"] = f(np.broadcast_to(ln[None], (128, 2, 1024)))
    blk = (ii[:, None] // 64) == (ii[None, :] // 64)
    M1f = ((ii[:, None] > ii[None, :]) & blk).astype(np.float32)
    M2f = ((ii[:, None] >= ii[None, :]) & blk).astype(np.float32)
    shared["rw_mask"] = f(np.stack([M1f, M1f.T, M2f, M2f.T, blk.astype(np.float32), np.zeros((128, 128))], 1))
    cosT, sinT = rope_tables()
    per_core = []
    for core in range(NCORES):
        d = {}
        if core < 2:
            b = core
            d["x_tok"] = f(inp["x_sample"][b])
            d["cvec"] = f(inp["c"][b].reshape(8, 128).T)
            d["mcol"] = np.ones((128, 1), np.float32)
            d["mask01"] = np.ones((128, 40), np.float32)
            C = np.zeros((128, TT), np.float32)
            S = np.zeros((128, TT), np.float32)
            for p in range(128):
                dd = p % 64
                fq = dd % 32
                C[p] = cosT[:, fq]
                S[p] = -sinT[:, fq] if dd < 32 else sinT[:, fq]
            d["ropeC"], d["ropeS"] = C, S
            ck = inp["cache_attn_k"][b, 0].reshape(256, 128)
            d["ctxkA"] = f(ck.T)
            d["ctxkB"] = f(np.concatenate([ck[:, 64:], ck[:, :64]], 1).T)
            d["ctxv"] = f(inp["cache_attn_v"][b, 0].reshape(256, 128))
            h0 = np.stack([inp["state_ssd_fwd"][b, 0], inp["state_ssd_bwd"][b, 0]], 0)
            d["ssd_h0"] = f(h0.transpose(0, 2, 1, 3))
            s0 = np.stack([inp["state_rwkv_fwd"][b, 0], inp["state_rwkv_bwd"][b, 0]], 0)
            d["rw_s0"] = f(s0.reshape(2, 8, 128, 64))
        else:
            pc = min(core - 2, 3)
            seqs = inp["x_prompt"][pc * 4:(pc + 1) * 4]
            d["x_tok"] = f(seqs.reshape(TT, 1024))
            d["cvec"] = f(inp["c_ctx"].reshape(8, 128).T)
            d["mcol"] = np.zeros((128, 1), np.float32)
            m01 = np.zeros((128, 10, 4), np.float32)
            for kt in range(2, 10):
                m01[:, kt, (kt - 2) // 2] = 1.0
            d["mask01"] = m01.reshape(128, 40)
            d["ropeC"] = np.ones((128, TT), np.float32)
            d["ropeS"] = np.zeros((128, TT), np.float32)
            d["ctxkA"] = np.zeros((128, 256), np.float32)
            d["ctxkB"] = np.zeros((128, 256), np.float32)
            d["ctxv"] = np.zeros((256, 128), np.float32)
            d["ssd_h0"] = np.zeros((2, 64, 8, 64), np.float32)
            d["rw_s0"] = np.zeros((2, 8, 128, 64), np.float32)
        d.update(shared)
        per_core.append(d)
    return per_core


_CACHE = {}


def run(inputs, stage=STAGE):
    inp = {k: np.asarray(v) for k, v in inputs.items()}
    per_core = host_prepare(inp)
    if stage not in _CACHE:
        _CACHE[stage] = build(stage)
    nc, IN, OUT = _CACHE[stage]
    in_maps = [{k: d[k] for k in IN} for d in per_core]
    res = run_bass_kernel_spmd(nc, in_maps, core_ids=list(range(NCORES)))
    return res.results


def kernel(**inputs):
    r = run(inputs)
    return assemble(r)


def assemble(r):
    y_sample = np.stack([r[0]["y"], r[1]["y"]], 0)
    y_prompt = np.concatenate([r[2 + i]["y"].reshape(4, 256, 1024) for i in range(4)], 0)
    nk = np.concatenate([r[2 + i]["nk"].reshape(4, 256, 2, 64) for i in range(4)], 0)[:, None]
    nv = np.concatenate([r[2 + i]["nv"].reshape(4, 256, 2, 64) for i in range(4)], 0)[:, None]
    ssd = []
    for d in range(2):
        a = np.concatenate([r[2 + i]["nssd"][d] for i in range(4)], 0)
        ssd.append(np.ascontiguousarray(a.transpose(0, 2, 1, 3))[:, None])
    rw = []
    for d in range(2):
        a = np.concatenate([r[2 + i]["nrw"].reshape(2, 4, 16, 64, 64)[d] for i in range(4)], 0)
        rw.append(np.ascontiguousarray(a)[:, None])
    f = lambda a: np.ascontiguousarray(a, dtype=np.float32)
    return (f(y_prompt), f(y_sample), f(nk), f(nv), f(ssd[0]), f(ssd[1]), f(rw[0]), f(rw[1]))
```
